# Optimizing a Trainium2 kernel written in Bass

```python
import math
import jax, jax.numpy as jnp
from jax import lax
import numpy as np

D_MODEL = 1024
BATCH = 32
SEQ = 256
DEPTH = 4
DEC_BATCH = 8
DEC_SEQ = 2048
PAST_LEN = 256

GRID_W = 64
EPS = 1e-6
N_EVEN = (DEPTH + 1) // 2
N_ODD = DEPTH // 2
MIX_W = D_MODEL
CHUNK = 64
H_A = 4
DK_A = 64
DV_A = 128
W_A = H_A * DV_A
W_B = MIX_W - W_A
SC_K = 3
H_C = 4
DK_C = 128
DV_C = 128
W_C = H_C * DV_C
QKV_K = 3
W_D = MIX_W - W_C
CF_K = 31
EV_IN = 2 * H_A * DK_A + 2 * W_A + 3 * W_B
OD_IN = 2 * H_C * DK_C + 2 * W_C + 2 * 2 * H_C + 2 * W_D
ROPE_BASE = 10000.0
N_KEYS = 128
N_EXP = N_KEYS * N_KEYS
H_P = 8
D_KEY = 256
TOPK = 16
PEER_BLOCK = 128

kernel_name = 'hybrid_retention_gdn_peer_diffusion_step'


def rmsnorm(x, g):
    xf = x.astype(jnp.float32)
    y = xf * lax.rsqrt(jnp.mean(xf * xf, axis=-1, keepdims=True) + EPS)
    return (y * g.astype(jnp.float32)).astype(x.dtype)


def layernorm(x, g, b):
    xf = x.astype(jnp.float32)
    mu = jnp.mean(xf, axis=-1, keepdims=True)
    xc = xf - mu
    y = xc * lax.rsqrt(jnp.mean(xc * xc, axis=-1, keepdims=True) + EPS)
    return (y * g.astype(jnp.float32) + b.astype(jnp.float32)).astype(x.dtype)


def dwconv(x, w):
    k, ch = w.shape
    return lax.conv_general_dilated(x, w[:, None, :].astype(x.dtype), window_strides=(1,),
                                    padding=[(k // 2, k // 2)],
                                    dimension_numbers=('NWC', 'WIO', 'NWC'),
                                    feature_group_count=ch)


def modulation(cvec, w, b):
    m = jax.nn.silu(cvec) @ w + b
    return jnp.split(m[:, None, :], 6, axis=-1)


def axial_rope(x):
    L, dk = x.shape[1], x.shape[-1]
    rows = L // GRID_W
    row = jnp.repeat(jnp.arange(rows, dtype=jnp.float32), GRID_W)
    col = jnp.tile(jnp.arange(GRID_W, dtype=jnp.float32), rows)
    nf = dk // 4
    freqs = ROPE_BASE ** (-jnp.arange(nf, dtype=jnp.float32) / nf)
    ang = jnp.concatenate([row[:, None] * freqs, col[:, None] * freqs], axis=-1)
    cos = jnp.cos(ang)[None, :, None, :]
    sin = jnp.sin(ang)[None, :, None, :]
    x1, x2 = x[..., : dk // 2], x[..., dk // 2:]
    return jnp.concatenate([x1 * cos - x2 * sin, x2 * cos + x1 * sin], axis=-1)


def retention_scan(q, k, v, log_gamma, s0):
    B, L, H, dk = q.shape
    nc = L // CHUNK
    qc = q.reshape(B, nc, CHUNK, H, dk)
    kc = k.reshape(B, nc, CHUNK, H, dk)
    vc = v.reshape(B, nc, CHUNK, H, -1)
    pos = jnp.arange(CHUNK, dtype=jnp.float32)
    lg = log_gamma[:, None]
    diff = pos[:, None] - pos[None, :]
    dmat = jnp.where(diff >= 0, jnp.exp(lg[:, :, None] * jnp.maximum(diff, 0.0)), 0.0)
    intra = jnp.einsum('bnihd,bnjhd->bnhij', qc, kc) * dmat
    o_intra = jnp.einsum('bnhij,bnjhe->bnihe', intra, vc)
    w_end = jnp.exp(lg * (CHUNK - 1 - pos))
    kv = jnp.einsum('bnjhd,hj,bnjhe->nbhde', kc, w_end, vc)
    g_chunk = jnp.exp(log_gamma * CHUNK)[None, :, None, None]

    def step(S, kv_n):
        return S * g_chunk + kv_n, S

    s_final, s_prev = lax.scan(step, s0, kv)
    w_start = jnp.exp(lg * (pos + 1.0))
    o_cross = jnp.einsum('bnihd,hi,nbhde->bnihe', qc, w_start, s_prev)
    return (o_intra + o_cross).reshape(B, L, H, -1), s_final


def gdn_scan(q, k, v, g, beta, s0):
    B, L, H, dk = q.shape
    dv = v.shape[-1]
    nc = L // CHUNK

    def chunks(t):
        return t.reshape(B, nc, CHUNK, H, -1).transpose(1, 0, 3, 2, 4)

    qc = chunks(q * (dk ** -0.5))
    kc = chunks(k)
    vc = chunks(v)
    bc = beta.reshape(B, nc, CHUNK, H).transpose(1, 0, 3, 2)
    gc = jnp.cumsum(g.reshape(B, nc, CHUNK, H).transpose(1, 0, 3, 2), axis=-1)
    idx = jnp.arange(CHUNK)
    incl = idx[:, None] >= idx[None, :]
    strict = idx[:, None] > idx[None, :]
    diff = gc[..., :, None] - gc[..., None, :]
    decay = jnp.where(incl, jnp.exp(jnp.where(incl, diff, 0.0)), 0.0)
    kb = kc * bc[..., None]
    vb = vc * bc[..., None]
    a_low = jnp.where(strict, jnp.einsum('nbhid,nbhjd->nbhij', kb, kc) * decay, 0.0)
    eye = jnp.eye(CHUNK, dtype=jnp.float32)
    t_inv = lax.linalg.triangular_solve(a_low + eye, jnp.broadcast_to(eye, a_low.shape),
                                        left_side=True, lower=True, unit_diagonal=True)
    u = jnp.einsum('nbhij,nbhje->nbhie', t_inv, vb)
    w = jnp.einsum('nbhij,nbhjd->nbhid', t_inv, kb * jnp.exp(gc)[..., None])
    attn = jnp.where(incl, jnp.einsum('nbhid,nbhjd->nbhij', qc, kc) * decay, 0.0)

    def step(S, xs):
        qn, kn, un, wn, gn, an = xs
        v_new = un - jnp.einsum('bhcd,bhde->bhce', wn, S)
        o = (jnp.einsum('bhcd,bhde->bhce', qn * jnp.exp(gn)[..., None], S)
             + jnp.einsum('bhij,bhje->bhie', an, v_new))
        gl = gn[..., -1]
        S = (S * jnp.exp(gl)[..., None, None]
             + jnp.einsum('bhcd,bhce->bhde', kn * jnp.exp(gl[..., None] - gn)[..., None], v_new))
        return S, o

    s_final, o = lax.scan(step, s0, (qc, kc, u, w, gc, attn))
    return o.transpose(1, 0, 3, 2, 4).reshape(B, L, H, dv), s_final


def even_mixer(h, w_in, w_out, gamma_logit, ret_norm_g, sc_w, s0f, s0b, latent):
    B, L, _ = h.shape
    p = h @ w_in
    q, k, v, g, bg, cg, hb = jnp.split(p, [256, 512, 1024, 1536, 2048, 2560], axis=-1)
    f32 = jnp.float32
    qh = q.reshape(B, L, H_A, DK_A).astype(f32)
    kh = k.reshape(B, L, H_A, DK_A).astype(f32) * (DK_A ** -0.5)
    vh = v.reshape(B, L, H_A, DV_A).astype(f32)
    if latent:
        qh = axial_rope(qh)
        kh = axial_rope(kh)
    log_gamma = jax.nn.log_sigmoid(gamma_logit.astype(f32))
    of, sf = retention_scan(qh, kh, vh, log_gamma[0], s0f)
    ob, sb = retention_scan(qh[:, ::-1], kh[:, ::-1], vh[:, ::-1], log_gamma[1], s0b)
    o = of + ob[:, ::-1]
    o = o * lax.rsqrt(jnp.mean(o * o, axis=-1, keepdims=True) + EPS) * ret_norm_g.astype(f32)
    ya = (jax.nn.silu(g.astype(f32)) * o.reshape(B, L, W_A)).astype(h.dtype)
    yb = bg * dwconv(cg * hb, sc_w)
    return jnp.concatenate([ya, yb], axis=-1) @ w_out, sf, sb


def odd_mixer(h, w_in, w_out, conv_w, a_log, dt_bias, gdn_norm_g, cf_w, cf_b, ln_g, ln_b, s0f, s0b):
    B, L, _ = h.shape
    f32 = jnp.float32
    p = h @ w_in
    qkv, z, a, bb, glu = jnp.split(p, [3 * W_C, 4 * W_C, 4 * W_C + 2 * H_C, 4 * W_C + 4 * H_C], axis=-1)
    qkv = jax.nn.silu(dwconv(qkv, conv_w)).astype(f32)
    q, k, v = jnp.split(qkv, 3, axis=-1)
    q = q.reshape(B, L, H_C, DK_C)
    k = k.reshape(B, L, H_C, DK_C)
    v = v.reshape(B, L, H_C, DV_C)
    q = q * lax.rsqrt(jnp.sum(q * q, axis=-1, keepdims=True) + EPS)
    k = k * lax.rsqrt(jnp.sum(k * k, axis=-1, keepdims=True) + EPS)
    a = a.reshape(B, L, 2, H_C).astype(f32)
    bb = bb.reshape(B, L, 2, H_C).astype(f32)
    g = -jnp.exp(a_log.astype(f32)) * jax.nn.softplus(a + dt_bias.astype(f32))
    beta = jax.nn.sigmoid(bb)
    of, sf = gdn_scan(q, k, v, g[:, :, 0], beta[:, :, 0], s0f)
    ob, sb = gdn_scan(q[:, ::-1], k[:, ::-1], v[:, ::-1], g[:, ::-1, 1], beta[:, ::-1, 1], s0b)
    o = of + ob[:, ::-1]
    o = o * lax.rsqrt(jnp.mean(o * o, axis=-1, keepdims=True) + EPS) * gdn_norm_g.astype(f32)
    zh = jax.nn.silu(z.astype(f32)).reshape(B, L, H_C, DV_C)
    yc = (o * zh).reshape(B, L, W_C).astype(h.dtype)
    ca, cgate = jnp.split(glu, 2, axis=-1)
    hc = ca * jax.nn.sigmoid(cgate)
    hc = dwconv(hc, cf_w) + cf_b
    yd = jax.nn.silu(layernorm(hc, ln_g, ln_b))
    return jnp.concatenate([yc, yd], axis=-1) @ w_out, sf, sb


def peer_ffn(h, wq, keys, u_tab, v_tab):
    B, L, D = h.shape
    T = B * L
    x = h.reshape(T, D)
    q = (x @ wq).reshape(T, H_P, 2, D_KEY // 2)
    s = jnp.einsum('thpd,hpnd->thpn', q, keys)
    s1, i1 = lax.top_k(s[:, :, 0], TOPK)
    s2, i2 = lax.top_k(s[:, :, 1], TOPK)
    cand_s = (s1[..., :, None] + s2[..., None, :]).reshape(T, H_P, TOPK * TOPK)
    cand_i = (i1[..., :, None] * N_KEYS + i2[..., None, :]).reshape(T, H_P, TOPK * TOPK)
    top_s, sel = lax.top_k(cand_s, TOPK)
    eidx = jnp.take_along_axis(cand_i, sel, axis=-1)
    gates = jax.nn.softmax(top_s.astype(jnp.float32), axis=-1).astype(h.dtype)
    nb = T // PEER_BLOCK
    hk = H_P * TOPK

    def block(args):
        xt, et, gt = args
        act = jax.nn.gelu(jnp.einsum('td,tkd->tk', xt, u_tab[et]))
        return jnp.einsum('tk,tkd->td', gt * act, v_tab[et])

    out = lax.map(block, (x.reshape(nb, PEER_BLOCK, D), eidx.reshape(nb, PEER_BLOCK, hk),
                          gates.reshape(nb, PEER_BLOCK, hk)))
    return out.reshape(B, L, D)


def setup_inputs(seed: int = 0) -> dict:
    key = jax.random.key(seed)
    ks = jax.random.split(key, 32)
    f32 = jnp.float32
    d = D_MODEL

    def nrm(k, shape, scale):
        return jax.random.normal(k, shape, f32) * scale

    base = jnp.log(2.0 ** (5.0 + jnp.arange(H_A, dtype=f32)) - 1.0)
    dt = jnp.exp(jax.random.uniform(ks[20], (N_ODD, 2, H_C), f32, math.log(1e-3), math.log(1e-1)))
    return {
        'x_prompt': nrm(ks[0], (BATCH, SEQ, d), 1.0),
        'x_sample': nrm(ks[1], (DEC_BATCH, DEC_SEQ, d), 1.0),
        'state_ret': nrm(ks[2], (DEC_BATCH, N_EVEN, 2, H_A, DK_A, DV_A), 0.1),
        'state_gdn': nrm(ks[3], (DEC_BATCH, N_ODD, 2, H_C, DK_C, DV_C), 0.1),
        'c': nrm(ks[4], (DEC_BATCH, d), 1.0),
        'c_ctx': nrm(ks[5], (d,), 1.0),
        'ada_w': nrm(ks[6], (DEPTH, d, 6 * d), 0.5 * d ** -0.5),
        'ada_b': nrm(ks[7], (DEPTH, 6 * d), 0.02),
        'norm_mix_g': 1.0 + nrm(ks[8], (DEPTH, d), 0.02),
        'norm_ffn_g': 1.0 + nrm(ks[9], (DEPTH, d), 0.02),
        'final_norm_g': 1.0 + nrm(ks[10], (d,), 0.02),
        'ev_w_in': nrm(ks[11], (N_EVEN, d, EV_IN), d ** -0.5),
        'ev_w_out': nrm(ks[12], (N_EVEN, MIX_W, d), MIX_W ** -0.5),
        'ret_gamma_logit': base + nrm(ks[13], (N_EVEN, 2, H_A), 0.01),
        'ret_norm_g': 1.0 + nrm(ks[14], (N_EVEN, H_A, DV_A), 0.02),
        'sc_conv_w': nrm(ks[15], (N_EVEN, SC_K, W_B), SC_K ** -0.5),
        'od_w_in': nrm(ks[16], (N_ODD, d, OD_IN), d ** -0.5),
        'od_w_out': nrm(ks[17], (N_ODD, MIX_W, d), MIX_W ** -0.5),
        'gdn_conv_w': nrm(ks[18], (N_ODD, QKV_K, 3 * W_C), QKV_K ** -0.5),
        'gdn_a_log': jnp.log(jax.random.uniform(ks[19], (N_ODD, 2, H_C), f32, 1.0, 16.0)),
        'gdn_dt_bias': dt + jnp.log(-jnp.expm1(-dt)),
        'gdn_norm_g': 1.0 + nrm(ks[21], (N_ODD, DV_C), 0.02),
        'cf_dw_w': nrm(ks[22], (N_ODD, CF_K, W_D), CF_K ** -0.5),
        'cf_dw_b': nrm(ks[23], (N_ODD, W_D), 0.02),
        'cf_ln_g': 1.0 + nrm(ks[24], (N_ODD, W_D), 0.02),
        'cf_ln_b': nrm(ks[25], (N_ODD, W_D), 0.02),
        'peer_wq': nrm(ks[26], (DEPTH, d, H_P * D_KEY), d ** -0.5),
        'peer_keys': nrm(ks[27], (DEPTH, H_P, 2, N_KEYS, D_KEY // 2), (D_KEY // 2) ** -0.5),
        'peer_u': nrm(ks[28], (DEPTH, N_EXP, d), d ** -0.5),
        'peer_v': nrm(ks[29], (DEPTH, N_EXP, d), 0.1),
    }


def reference(x_prompt, x_sample, state_ret, state_gdn, c, c_ctx, ada_w, ada_b, norm_mix_g, norm_ffn_g,
              final_norm_g, ev_w_in, ev_w_out, ret_gamma_logit, ret_norm_g, sc_conv_w, od_w_in, od_w_out,
              gdn_conv_w, gdn_a_log, gdn_dt_bias, gdn_norm_g, cf_dw_w, cf_dw_b, cf_ln_g, cf_ln_b,
              peer_wq, peer_keys, peer_u, peer_v):
    f32 = jnp.float32
    xp, xs = x_prompt, x_sample
    bp = xp.shape[0]
    ret_new = []
    gdn_new = []
    for l in range(DEPTH):
        sh1p, sc1p, ga1p, sh2p, sc2p, ga2p = modulation(c_ctx[None, :], ada_w[l], ada_b[l])
        sh1s, sc1s, ga1s, sh2s, sc2s, ga2s = modulation(c, ada_w[l], ada_b[l])
        hp = rmsnorm(xp, norm_mix_g[l]) * (1.0 + sc1p) + sh1p
        hs = rmsnorm(xs, norm_mix_g[l]) * (1.0 + sc1s) + sh1s
        i = l // 2
        if l % 2 == 0:
            z0 = jnp.zeros((bp, H_A, DK_A, DV_A), f32)
            yp, sf, sb = even_mixer(hp, ev_w_in[i], ev_w_out[i], ret_gamma_logit[i], ret_norm_g[i],
                                    sc_conv_w[i], z0, z0, False)
            ys, _, _ = even_mixer(hs, ev_w_in[i], ev_w_out[i], ret_gamma_logit[i], ret_norm_g[i],
                                  sc_conv_w[i], state_ret[:, i, 0].astype(f32), state_ret[:, i, 1].astype(f32), True)
            ret_new.append(jnp.stack([sf, sb], axis=1))
        else:
            z0 = jnp.zeros((bp, H_C, DK_C, DV_C), f32)
            yp, sf, sb = odd_mixer(hp, od_w_in[i], od_w_out[i], gdn_conv_w[i], gdn_a_log[i], gdn_dt_bias[i],
                                   gdn_norm_g[i], cf_dw_w[i], cf_dw_b[i], cf_ln_g[i], cf_ln_b[i], z0, z0)
            ys, _, _ = odd_mixer(hs, od_w_in[i], od_w_out[i], gdn_conv_w[i], gdn_a_log[i], gdn_dt_bias[i],
                                 gdn_norm_g[i], cf_dw_w[i], cf_dw_b[i], cf_ln_g[i], cf_ln_b[i],
                                 state_gdn[:, i, 0].astype(f32), state_gdn[:, i, 1].astype(f32))
            gdn_new.append(jnp.stack([sf, sb], axis=1))
        xp = xp + ga1p * yp
        xs = xs + ga1s * ys
        hp = rmsnorm(xp, norm_ffn_g[l]) * (1.0 + sc2p) + sh2p
        hs = rmsnorm(xs, norm_ffn_g[l]) * (1.0 + sc2s) + sh2s
        xp = xp + ga2p * peer_ffn(hp, peer_wq[l], peer_keys[l], peer_u[l], peer_v[l])
        xs = xs + ga2s * peer_ffn(hs, peer_wq[l], peer_keys[l], peer_u[l], peer_v[l])
    y_prompt = rmsnorm(xp, final_norm_g)
    y_sample = rmsnorm(xs, final_norm_g)
    new_state_ret = jnp.stack(ret_new, axis=1).astype(x_prompt.dtype)
    new_state_gdn = jnp.stack(gdn_new, axis=1).astype(x_prompt.dtype)
    return (y_prompt, y_sample, new_state_ret, new_state_gdn)
```

```python
import os
import numpy as np
from contextlib import ExitStack, contextmanager
import concourse.bass as bass
import concourse.mybir as mybir
from concourse.bass_utils import run_bass_kernel_spmd

F32 = mybir.dt.float32
BF16 = mybir.dt.bfloat16
U32 = mybir.dt.uint32
I32 = mybir.dt.int32
ALU = mybir.AluOpType
AF = mybir.ActivationFunctionType
AX = mybir.AxisListType

NCORES = 8
D = 1024
NT = 3072
NPR = 1024
LP = 256
LS = 2048
EPS = 1e-6
SEQS = [(0, 256, False), (256, 256, False), (512, 256, False), (768, 256, False), (1024, 2048, True)]
STOP = os.environ.get("KSTOP", "")
SKIPG = os.environ.get("KSKIPG", "") == "1"


class Res:
    __slots__ = ("w", "rs")

    def __init__(self):
        self.w = None
        self.rs = []


class T:
    def __init__(self, t, res=None):
        self.t = t
        self.res = res if res is not None else Res()

    def __getitem__(self, k):
        return self.t[k]


class Builder:
    EPOCH = 16000
    NDMA = 24

    def __init__(self, nc, es):
        self.nc = nc
        self.es = es
        self.E = {"pe": nc.tensor, "act": nc.scalar, "dve": nc.vector, "pool": nc.gpsimd, "sp": nc.sync}
        self.cur = {}
        self.cnt = {}
        self.nsem = 0
        for e in self.E:
            self._newsem(e)
        self.seen = {e: {} for e in self.E}
        self.dsem = [es.enter_context(nc.semaphore("dq%d" % i)) for i in range(self.NDMA)]
        self.duse = [0] * self.NDMA
        self.dk = 0
        self.uid = 0

    def _newsem(self, e):
        self.nsem += 1
        self.cur[e] = self.es.enter_context(self.nc.semaphore("s_%s_%d" % (e, self.nsem)))
        self.cnt[e] = 0

    def name(self, p):
        self.uid += 1
        return "%s_%d" % (p, self.uid)

    def sb(self, es, shape, dt=F32, name="t"):
        return T(es.enter_context(self.nc.sbuf_tensor(self.name(name), list(shape), dt)))

    def ps(self, es, shape, dt=F32, name="p"):
        return T(es.enter_context(self.nc.psum_tensor(self.name(name), list(shape), dt)))

    def dram(self, name, shape, dt, kind):
        return T(self.nc.dram_tensor(name, list(shape), dt, kind=kind).ap())

    def _deps(self, eng, r, w):
        deps = {}

        def add(ev):
            if ev is None:
                return
            s, v, src = ev
            if eng == "pe" and src == "pe":
                return
            k = id(s)
            if k not in deps or deps[k][1] < v:
                deps[k] = (s, v)

        for x in r:
            add(x.res.w)
        for x in w:
            add(x.res.w)
            for ev in x.res.rs:
                add(ev)
        E = self.E[eng]
        sn = self.seen[eng]
        for k, (s, v) in deps.items():
            if sn.get(k, 0) >= v:
                continue
            E.wait_ge(s, v)
            sn[k] = v

    def _mark(self, ev, r, w):
        for x in r:
            x.res.rs.append(ev)
        for x in w:
            x.res.w = ev
            x.res.rs = []

    def I(self, eng, f, r=(), w=()):
        self._deps(eng, r, w)
        inst = f()
        if self.cnt[eng] >= self.EPOCH:
            self._newsem(eng)
        self.cnt[eng] += 1
        ev = (self.cur[eng], self.cnt[eng], eng)
        inst.then_inc(ev[0], 1)
        self._mark(ev, r, w)
        return ev

    def DMA(self, eng, f, r=(), w=()):
        self._deps(eng, r, w)
        k = self.dk % self.NDMA
        self.dk += 1
        s = self.dsem[k]
        E = self.E[eng]
        if self.duse[k] > 0:
            key = id(s)
            if self.seen[eng].get(key, 0) < 16 * self.duse[k]:
                E.wait_ge(s, 16 * self.duse[k])
                self.seen[eng][key] = 16 * self.duse[k]
        self.duse[k] += 1
        inst = f()
        ev = (s, 16 * self.duse[k], "dma")
        inst.then_inc(s, 16)
        self._mark(ev, r, w)
        return ev

    def load(self, dst, dst_ap, src, src_ap, eng="sp"):
        return self.DMA(eng, lambda: self.E[eng].dma_start(out=dst_ap, in_=src_ap), r=[src], w=[dst])

    def tt(self, eng, out, o_ap, a, a_ap, b, b_ap, op):
        return self.I(eng, lambda: self.E[eng].tensor_tensor(out=o_ap, in0=a_ap, in1=b_ap, op=op), r=[a, b], w=[out])

    def ts(self, eng, out, o_ap, a, a_ap, s1, op0, s2=None, op1=None, extra=()):
        if op1 is None:
            f = lambda: self.E[eng].tensor_scalar(out=o_ap, in0=a_ap, scalar1=s1, scalar2=None, op0=op0)
        else:
            f = lambda: self.E[eng].tensor_scalar(out=o_ap, in0=a_ap, scalar1=s1, scalar2=s2, op0=op0, op1=op1)
        return self.I(eng, f, r=[a] + list(extra), w=[out])

    def stt(self, out, o_ap, a, a_ap, sc, b, b_ap, op0, op1, extra=(), accum=None, accum_t=None):
        if accum is None:
            f = lambda: self.nc.vector.scalar_tensor_tensor(out=o_ap, in0=a_ap, scalar=sc, in1=b_ap, op0=op0, op1=op1)
            w = [out]
        else:
            f = lambda: self.nc.vector.scalar_tensor_tensor(out=o_ap, in0=a_ap, scalar=sc, in1=b_ap, op0=op0, op1=op1, accum_out=accum)
            w = [out, accum_t]
        return self.I("dve", f, r=[a, b] + list(extra), w=w)

    def act(self, out, o_ap, a, a_ap, func, scale=1.0, bias=0.0, extra=()):
        return self.I("act", lambda: self.nc.scalar.activation(out=o_ap, in_=a_ap, func=func, bias=bias, scale=scale),
                      r=[a] + list(extra), w=[out])

    def cp(self, eng, out, o_ap, a, a_ap):
        if eng == "act":
            return self.I("act", lambda: self.nc.scalar.copy(out=o_ap, in_=a_ap), r=[a], w=[out])
        return self.I(eng, lambda: self.E[eng].tensor_copy(out=o_ap, in_=a_ap), r=[a], w=[out])

    def mm(self, out, o_ap, l, l_ap, rr, r_ap, start=True, stop=True):
        return self.I("pe", lambda: self.nc.tensor.matmul(o_ap, lhsT=l_ap, rhs=r_ap, start=start, stop=stop), r=[l, rr], w=[out])

    def tr(self, out, o_ap, a, a_ap, ident):
        return self.I("pe", lambda: self.nc.tensor.transpose(o_ap, a_ap, ident[:]), r=[a, ident], w=[out])

    def memset(self, eng, out, o_ap, v):
        return self.I(eng, lambda: self.E[eng].memset(o_ap, v), r=[], w=[out])

    def load_cast(self, stg, dst, dst_ap, src, src_ap, width, k):
        st = stg[k % len(stg)]
        self.load(st, st[:, 0:width], src, src_ap)
        self.cp("pool" if k % 2 == 0 else "act", dst, dst_ap, st, st[:, 0:width])

    def barrier(self):
        evs = [(self.cur[e], self.cnt[e]) for e in self.E if self.cnt[e] > 0]
        evs += [(self.dsem[k], 16 * self.duse[k]) for k in range(self.NDMA) if self.duse[k] > 0]
        for eng in self.E:
            E = self.E[eng]
            sn = self.seen[eng]
            for (s, v) in evs:
                if sn.get(id(s), 0) >= v:
                    continue
                E.wait_ge(s, v)
                sn[id(s)] = v

    @contextmanager
    def phase(self):
        with ExitStack() as ph:
            yield ph
            self.barrier()

    def final_wait(self, outs):
        for o in outs:
            self._deps("sp", [o], [])


def host_consts():
    c = {}
    c["ident"] = np.eye(128, dtype=np.float32)
    c["ones"] = np.ones((128, 128), np.float32)
    p = np.arange(128)
    hm = np.zeros((128, 4), np.float32)
    for h in range(4):
        hm[h * 32:(h + 1) * 32, h] = 0.125
    c["hm"] = hm
    t = np.arange(LS, dtype=np.float32)
    row = np.floor(t / 64.0)
    col = t - row * 64.0
    nf = 16
    freqs = (10000.0 ** (-np.arange(nf, dtype=np.float32) / nf)).astype(np.float32)
    ang = np.concatenate([row[:, None] * freqs, col[:, None] * freqs], axis=-1).astype(np.float32)
    c["cos"] = np.tile(np.cos(ang).T.astype(np.float32), (4, 1))
    c["sin"] = np.tile(np.sin(ang).T.astype(np.float32), (4, 1))
    m = np.arange(3968)
    c["dtab"] = (m[None, :] - 1920 - p[:, None]).astype(np.float32)
    c["posf"] = np.tile((t + 1.0)[None, :], (128, 1)).astype(np.float32)
    c["posb"] = np.tile((LS - t)[None, :], (128, 1)).astype(np.float32)
    posst = np.zeros((128, 2, 2), np.float32)
    for jt in range(2):
        posst[:, jt, 0] = 255 - (jt * 128 + p)
        posst[:, jt, 1] = jt * 128 + p
    c["posst"] = posst
    c["Mf"] = (p[:, None] <= p[None, :]).astype(np.float32)
    c["Mb"] = (p[:, None] >= p[None, :]).astype(np.float32)
    c["Sf"] = (p[:, None] < p[None, :]).astype(np.float32)
    c["Sb"] = (p[:, None] > p[None, :]).astype(np.float32)
    Elo = np.zeros((128, 7, 128), np.float32)
    for lv in range(7):
        sz = 1 << lv
        blk_i = p[:, None] // (2 * sz)
        blk_j = p[None, :] // (2 * sz)
        Elo[:, lv, :] = ((blk_i == blk_j) & ((p[:, None] % (2 * sz)) >= sz) & ((p[None, :] % (2 * sz)) < sz)).astype(np.float32)
    c["Elo"] = Elo
    c["Eup"] = np.ascontiguousarray(Elo.transpose(2, 1, 0))
    c["iota128"] = np.tile(np.arange(128, dtype=np.float32)[None, :], (128, 1))
    c["iota16"] = np.tile(np.arange(16, dtype=np.float32)[None, :], (128, 1))
    return c


CONST_SHAPES = {"ident": [128, 128], "ones": [128, 128], "hm": [128, 4], "cos": [128, LS], "sin": [128, LS],
                "dtab": [128, 3968], "posf": [128, LS], "posb": [128, LS], "posst": [128, 2, 2],
                "Mf": [128, 128], "Mb": [128, 128], "Sf": [128, 128], "Sb": [128, 128], "iota16": [128, 16],
                "Elo": [128, 7, 128], "Eup": [128, 7, 128], "iota128": [128, 128]}

IN_SHAPES = {
    "xT0": [8, 128, NT], "cT": [128, 8, 2], "ada_w": [4, 1024, 6144], "ada_bT": [128, 4, 48],
    "gmixT": [128, 4, 8], "gffnT": [128, 4, 8], "gfinT": [128, 8],
    "ev_w_in": [2, 1024, 3072], "ev_w_out": [2, 1024, 1024], "lgam": [2, 8], "retgT": [128, 2, 4],
    "scwT": [128, 2, 4, 3], "od_w_in": [2, 1024, 3088], "od_w_out": [2, 1024, 1024], "gcwT": [128, 2, 12, 3],
    "alog": [2, 8], "dtb": [2, 8], "gdngT": [128, 2], "cfwT": [128, 2, 4, 31], "cfbT": [128, 2, 4],
    "lngT": [128, 2, 4], "lnbT": [128, 2, 4], "peer_wq": [4, 1024, 2048], "keysT": [4, 16, 128, 128],
    "peer_u": [4, 128, 128, 1024], "peer_v": [4, 128, 128, 1024],
    "sret": [2, 2, 4, 64, 128], "sgdn": [2, 2, 4, 128, 128],
}


def build():
    nc = bass.Bass("TRN2", target_bir_lowering=False)
    es0 = ExitStack()
    b = Builder(nc, es0)
    dr = {}
    for k, s in IN_SHAPES.items():
        dr[k] = b.dram(k, s, F32, "ExternalInput")
    for k, s in CONST_SHAPES.items():
        dr[k] = b.dram("c_" + k, s, F32, "ExternalInput")
    yT = b.dram("yT", [8, 128, NT], F32, "ExternalOutput")
    nsr = b.dram("nsr", [4, 2, 2, 4, 64, 128], F32, "ExternalOutput")
    nsg = b.dram("nsg", [4, 2, 2, 4, 128, 128], F32, "ExternalOutput")
    xT = b.dram("xT", [8, 128, NT], F32, "Internal")
    pT = b.dram("pT", [24, 128, NT], F32, "ExternalOutput" if STOP else "Internal")
    vtok = b.dram("vtok", [NT, 512], BF16, "Internal")
    ktok = b.dram("ktok", [NPR, 256], BF16, "Internal")
    qkr = b.dram("qkr", [4, 128, NT], BF16, "Internal")
    yTd = b.dram("yTd", [8, 128, NT], BF16, "ExternalOutput" if STOP else "Internal")
    dmod = b.dram("dmod", [128, 4 * 48 * 2], F32, "ExternalOutput") if STOP else None
    gbtok = b.dram("gbtok", [NT, 16], F32, "Internal")
    qkvT = b.dram("qkvT", [12, 128, NT], F32, "Internal")
    ktok32 = b.dram("ktok32", [NT, 512], F32, "Internal")
    vtok32 = b.dram("vtok32", [NT, 512], F32, "Internal")
    oTd = b.dram("oTd", [2, 4, 128, NT], F32, "Internal")
    uTb = b.dram("uTb", [128, 128, 1024], BF16, "Internal")
    vbd = b.dram("vbd", [128, 128, 1024], BF16, "Internal")
    ekg = b.dram("ekg", [3, NT, 128], F32, "Internal")
    h2bd = b.dram("h2bd", [8, 128, NT], BF16, "Internal")
    scr = (uTb, vbd, ekg, h2bd)
    outs_written = []

    es = es0
    ident = b.sb(es, [128, 128], F32, "ident")
    ones = b.sb(es, [128, 128], F32, "ones")
    epsc = b.sb(es, [128, 1], F32, "epsc")
    onec = b.sb(es, [128, 1], F32, "onec")
    b.load(ident, ident[:], dr["ident"], dr["ident"].t)
    b.load(ones, ones[:], dr["ones"], dr["ones"].t)
    b.memset("dve", epsc, epsc[:], EPS)
    b.memset("dve", onec, onec[:], 1.0)
    modT = b.sb(es, [128, 4, 48, 2], F32, "modT")
    A1 = b.sb(es, [128, 4, 8, 2], F32, "A1")
    A2 = b.sb(es, [128, 4, 8, 2], F32, "A2")
    gfin = b.sb(es, [128, 8], F32, "gfin")
    b.load(gfin, gfin[:], dr["gfinT"], dr["gfinT"].t)

    with b.phase() as ph:
        cT = b.sb(ph, [128, 8, 2], F32, "cT")
        scT = b.sb(ph, [128, 8, 2], F32, "scT")
        abT = b.sb(ph, [128, 4, 48], F32, "abT")
        gmx = b.sb(ph, [128, 4, 8], F32, "gmx")
        gff = b.sb(ph, [128, 4, 8], F32, "gff")
        tmpm = b.sb(ph, [128, 4, 8, 2], F32, "tmpm")
        wts = [b.sb(ph, [128, 8, 768], F32, "adaw") for _ in range(4)]
        pm = b.ps(ph, [128, 512], F32, "pm")
        b.load(cT, cT[:], dr["cT"], dr["cT"].t)
        b.load(abT, abT[:], dr["ada_bT"], dr["ada_bT"].t)
        b.load(gmx, gmx[:], dr["gmixT"], dr["gmixT"].t)
        b.load(gff, gff[:], dr["gffnT"], dr["gffnT"].t)
        b.act(scT, scT[:], cT, cT[:], AF.Silu)
        it = 0
        for l in range(4):
            for grp in range(8):
                wt = wts[it % 4]
                it += 1
                src = dr["ada_w"].t[l, :, grp * 768:(grp + 1) * 768].rearrange("(kc p) n -> p kc n", p=128)
                b.load(wt, wt[:], dr["ada_w"], src)
                for cc in range(6):
                    for kc in range(8):
                        b.mm(pm, pm[:, cc * 2:cc * 2 + 2], wt, wt[:, kc, cc * 128:(cc + 1) * 128], scT, scT[:, kc, :],
                             start=(kc == 0), stop=(kc == 7))
                for cc in range(6):
                    ch = grp * 6 + cc
                    b.ts("dve", modT, modT[:, l, ch, :], pm, pm[:, cc * 2:cc * 2 + 2], abT[:, l, ch:ch + 1], ALU.add, extra=[abT])
        b.ts("dve", tmpm, tmpm[:], modT, modT[:, :, 8:16, :], 1.0, ALU.add)
        b.tt("dve", A1, A1[:], tmpm, tmpm[:], gmx, gmx[:].unsqueeze(3).to_broadcast([128, 4, 8, 2]), ALU.mult)
        b.ts("dve", tmpm, tmpm[:], modT, modT[:, :, 32:40, :], 1.0, ALU.add)
        b.tt("dve", A2, A2[:], tmpm, tmpm[:], gff, gff[:].unsqueeze(3).to_broadcast([128, 4, 8, 2]), ALU.mult)

    def norm_mod(ph, xg, W, Acol, Bcol, hb=None, hf=None, tmp=None, sq=None, ps_=None, rstd=None):
        b.act(sq, sq[:, :, 0:W], xg, xg[:, :, 0:W], AF.Square)
        for c in range(8):
            b.mm(ps_, ps_[:, 0:W], ones, ones[:], sq, sq[:, c, 0:W], start=(c == 0), stop=(c == 7))
        b.act(rstd, rstd[:, 0:W], ps_, ps_[:, 0:W], AF.Ln, scale=1.0 / D, bias=epsc[:, 0:1], extra=[epsc])
        b.act(rstd, rstd[:, 0:W], rstd, rstd[:, 0:W], AF.Exp, scale=-0.5)
        for c in range(8):
            b.tt("dve", tmp, tmp[:, 0:W], xg, xg[:, c, 0:W], rstd, rstd[:, 0:W], ALU.mult)
            tgt = hf if hf is not None else hb
            if Bcol is not None:
                b.ts("dve", tgt, tgt[:, c, 0:W], tmp, tmp[:, 0:W], Acol(c), ALU.mult, Bcol(c), ALU.add, extra=[A1, A2, modT, gfin])
            else:
                b.ts("dve", tgt, tgt[:, c, 0:W], tmp, tmp[:, 0:W], Acol(c), ALU.mult, extra=[A1, A2, modT, gfin])
            if hf is not None and hb is not None:
                b.cp("pool", hb, hb[:, c, 0:W], hf, hf[:, c, 0:W])

    def xview(dt_, lo, W):
        return dt_.t[:, :, lo:lo + W].rearrange("c p t -> p c t")

    with b.phase() as ph:
        xb = [b.sb(ph, [128, 8, 512], F32, "xcp") for _ in range(2)]
        for tg in range(6):
            x_ = xb[tg % 2]
            b.load(x_, x_[:], dr["xT0"], xview(dr["xT0"], tg * 512, 512))
            b.load(xT, xview(xT, tg * 512, 512), x_, x_[:])

    for l in range(4):
        i = l // 2
        even = (l % 2 == 0)
        if STOP == "M":
            break
        with b.phase() as ph:
            ncol = 3072 if even else 3088
            win = b.sb(ph, [128, 8, ncol], BF16, "win")
            wsrc = dr["ev_w_in" if even else "od_w_in"]
            cstg = [b.sb(ph, [128, 1024], F32, "cstg") for _ in range(3)]
            ck = 0
            for kc in range(8):
                for c0 in range(0, ncol, 1024):
                    c1 = min(ncol, c0 + 1024)
                    b.load_cast(cstg, win, win[:, kc, c0:c1], wsrc, wsrc.t[i, kc * 128:(kc + 1) * 128, c0:c1], c1 - c0, ck)
                    ck += 1
            xg = b.sb(ph, [128, 8, 512], F32, "xg")
            hT = b.sb(ph, [128, 8, 512], BF16, "hT")
            sq = b.sb(ph, [128, 8, 512], F32, "sq")
            tmp = b.sb(ph, [128, 512], F32, "tmp")
            rstd = b.sb(ph, [128, 512], F32, "rstd")
            psn = b.ps(ph, [128, 512], F32, "psn")
            pmm = [b.ps(ph, [128, 512], F32, "pmm") for _ in range(3)]
            stg = [b.sb(ph, [128, 512], F32, "stg") for _ in range(3)]
            stgb = [b.sb(ph, [128, 512], BF16, "stgb") for _ in range(2)]
            if even:
                fm_cols = [c * 128 for c in (0, 1, 2, 3)] + [1024 + c * 128 for c in range(16)]
                fm_dst = [0, 1, 2, 3] + list(range(8, 24))
            else:
                fm_cols = [c * 128 for c in range(16)] + [2064 + c * 128 for c in range(8)]
                fm_dst = list(range(24))
                nega = b.sb(ph, [128, 8], F32, "nega")
                dtbb = b.sb(ph, [128, 8], F32, "dtbb")
                gbs = b.sb(ph, [128, 16], F32, "gbs")
                b.load(nega, nega[:], dr["alog"], dr["alog"].t[i:i + 1, :].partition_broadcast(128))
                b.load(dtbb, dtbb[:], dr["dtb"], dr["dtb"].t[i:i + 1, :].partition_broadcast(128))
                b.act(nega, nega[:], nega, nega[:], AF.Exp)
                b.ts("dve", nega, nega[:], nega, nega[:], -1.0, ALU.mult)
            k = 0
            for tg in range(6):
                v = 0 if tg < 2 else 1
                b.load(xg, xg[:], xT, xview(xT, tg * 512, 512))
                norm_mod(ph, xg, 512, lambda c: A1[:, l, c, v:v + 1], lambda c: modT[:, l, c, v:v + 1], hb=hT, tmp=tmp, sq=sq, ps_=psn, rstd=rstd)
                for cc, dc in zip(fm_cols, fm_dst):
                    p_ = pmm[k % 3]
                    s_ = stg[k % 3]
                    k += 1
                    for kc in range(8):
                        b.mm(p_, p_[:], win, win[:, kc, cc:cc + 128], hT, hT[:, kc, :], start=(kc == 0), stop=(kc == 7))
                    b.cp("act", s_, s_[:], p_, p_[:])
                    b.load(pT, pT.t[dc, :, tg * 512:(tg + 1) * 512], s_, s_[:])
                for tt_ in range(4):
                    tok0 = tg * 512 + tt_ * 128
                    if even:
                        p_ = pmm[k % 3]
                        sb_ = stgb[k % 2]
                        k += 1
                        for kc in range(8):
                            b.mm(p_, p_[:], hT, hT[:, kc, tt_ * 128:(tt_ + 1) * 128], win, win[:, kc, 512:1024], start=(kc == 0), stop=(kc == 7))
                        b.cp("act", sb_, sb_[:], p_, p_[:])
                        b.load(vtok, vtok.t[tok0:tok0 + 128, :], sb_, sb_[:])
                        if tg < 2:
                            p_ = pmm[k % 3]
                            sb_ = stgb[k % 2]
                            k += 1
                            for kc in range(8):
                                b.mm(p_, p_[:, 0:256], hT, hT[:, kc, tt_ * 128:(tt_ + 1) * 128], win, win[:, kc, 256:512], start=(kc == 0), stop=(kc == 7))
                            b.cp("act", sb_, sb_[:, 0:256], p_, p_[:, 0:256])
                            b.load(ktok, ktok.t[tok0:tok0 + 128, :], sb_, sb_[:, 0:256])
                    else:
                        p_ = pmm[k % 3]
                        k += 1
                        for kc in range(8):
                            b.mm(p_, p_[:, 0:16], hT, hT[:, kc, tt_ * 128:(tt_ + 1) * 128], win, win[:, kc, 2048:2064], start=(kc == 0), stop=(kc == 7))
                        b.tt("dve", gbs, gbs[:, 0:8], p_, p_[:, 0:8], dtbb, dtbb[:], ALU.add)
                        b.act(gbs, gbs[:, 0:8], gbs, gbs[:, 0:8], AF.Exp)
                        b.act(gbs, gbs[:, 0:8], gbs, gbs[:, 0:8], AF.Ln, bias=onec[:, 0:1], extra=[onec])
                        b.tt("dve", gbs, gbs[:, 0:8], gbs, gbs[:, 0:8], nega, nega[:], ALU.mult)
                        b.act(gbs, gbs[:, 8:16], p_, p_[:, 8:16], AF.Sigmoid)
                        b.load(gbtok, gbtok.t[tok0:tok0 + 128, :], gbs, gbs[:])
        if STOP == "P2":
            break
        if even:
            even_mixer(b, nc, dr, i, pT, vtok, ktok, qkr, yTd, nsr, ident, ones, epsc, outs_written)
        else:
            odd_mixer(b, nc, dr, i, pT, gbtok, qkvT, ktok32, vtok32, oTd, yTd, nsg, ident, ones, epsc, onec, outs_written)
        peer_phase(b, nc, dr, l, i, even, xT, yTd, modT, A2, ident, ones, epsc, norm_mod, xview, do_peer=(STOP != "%da" % l), scr=scr)
        if STOP in ("%da" % l, "%db" % l):
            break

    with b.phase() as ph:
        xg = b.sb(ph, [128, 8, 512], F32, "xg")
        hf = b.sb(ph, [128, 8, 512], F32, "hf")
        if STOP:
            dbg = b.dram("dbg", [8, 128, NT], F32, "ExternalOutput")
            for tg in range(6):
                b.load(xg, xg[:], xT, xview(xT, tg * 512, 512))
                b.load(dbg, xview(dbg, tg * 512, 512), xg, xg[:])
            b.load(dmod, dmod.t, modT, modT[:].rearrange("p a b c -> p (a b c)"))
            b.final_wait([dbg, dmod, pT, yTd])
        sq = b.sb(ph, [128, 8, 512], F32, "sq")
        tmp = b.sb(ph, [128, 512], F32, "tmp")
        rstd = b.sb(ph, [128, 512], F32, "rstd")
        psn = b.ps(ph, [128, 512], F32, "psn")
        for tg in range(6):
            b.load(xg, xg[:], xT, xview(xT, tg * 512, 512))
            norm_mod(ph, xg, 512, lambda c: gfin[:, c:c + 1], None, hf=hf, tmp=tmp, sq=sq, ps_=psn, rstd=rstd)
            b.load(yT, xview(yT, tg * 512, 512), hf, hf[:])
    b.final_wait([yT, nsr, nsg])
    es0.close()
    return nc


def even_mixer(b, nc, dr, i, pT, vtok, ktok, qkr, yTd, nsr, ident, ones, epsc, outs_written):
    with b.phase() as ph:
        x1 = b.sb(ph, [128, 512], F32, "x1")
        x2 = b.sb(ph, [128, 512], F32, "x2")
        cs = b.sb(ph, [128, LS], F32, "cos")
        sn = b.sb(ph, [128, LS], F32, "sin")
        t1 = b.sb(ph, [128, 512], F32, "t1")
        t2 = b.sb(ph, [128, 512], F32, "t2")
        o1 = b.sb(ph, [128, 512], BF16, "o1")
        o2 = b.sb(ph, [128, 512], BF16, "o2")
        b.load(cs, cs[:], dr["cos"], dr["cos"].t)
        b.load(sn, sn[:], dr["sin"], dr["sin"].t)
        for tg in range(6):
            lat = tg >= 2
            pos0 = (tg - 2) * 512
            for qk in range(2):
                b.load(x1, x1[:], pT, pT.t[qk * 2, :, tg * 512:(tg + 1) * 512])
                b.load(x2, x2[:], pT, pT.t[qk * 2 + 1, :, tg * 512:(tg + 1) * 512])
                if lat:
                    c_ = cs[:, pos0:pos0 + 512]
                    s_ = sn[:, pos0:pos0 + 512]
                    b.tt("dve", t1, t1[:], x1, x1[:], cs, c_, ALU.mult)
                    b.tt("pool", t2, t2[:], x2, x2[:], sn, s_, ALU.mult)
                    b.tt("dve", o1, o1[:], t1, t1[:], t2, t2[:], ALU.subtract)
                    b.tt("dve", t1, t1[:], x2, x2[:], cs, c_, ALU.mult)
                    b.tt("pool", t2, t2[:], x1, x1[:], sn, s_, ALU.mult)
                    b.tt("dve", o2, o2[:], t1, t1[:], t2, t2[:], ALU.add)
                else:
                    b.cp("dve", o1, o1[:], x1, x1[:])
                    b.cp("pool", o2, o2[:], x2, x2[:])
                b.load(qkr, qkr.t[qk * 2, :, tg * 512:(tg + 1) * 512], o1, o1[:])
                b.load(qkr, qkr.t[qk * 2 + 1, :, tg * 512:(tg + 1) * 512], o2, o2[:])
    with b.phase() as ph:
        lgt = b.sb(ph, [128, 8], F32, "lgt")
        nlg = b.sb(ph, [128, 8], F32, "nlg")
        hm = b.sb(ph, [128, 4], F32, "hm")
        retg = b.sb(ph, [128, 2, 4], F32, "retg")
        dtab = b.sb(ph, [128, 3968], F32, "dtab")
        ta = b.sb(ph, [128, 3968], F32, "ta")
        tb = b.sb(ph, [128, 3968], F32, "tb")
        Th = b.sb(ph, [128, 3968], F32, "Th")
        posf = b.sb(ph, [128, LS], F32, "posf")
        posb = b.sb(ph, [128, LS], F32, "posb")
        wfr = b.sb(ph, [128, LS], F32, "wfr")
        wbr = b.sb(ph, [128, LS], F32, "wbr")
        posst = b.sb(ph, [128, 2, 2], F32, "posst")
        wst = b.sb(ph, [128, 2, 2], F32, "wst")
        Q = b.sb(ph, [128, 2, LS], BF16, "Q")
        K = b.sb(ph, [128, 2, LS], BF16, "K")
        Kh = b.sb(ph, [128, 2, LS], BF16, "Kh")
        V = b.sb(ph, [128, 16, 128], BF16, "V")
        Vs = b.sb(ph, [128, 2, 2, 128], BF16, "Vs")
        kt = b.sb(ph, [128, 2, 256], BF16, "kt")
        S0 = b.sb(ph, [128, 2, 2, 128], BF16, "S0")
        S0f = b.sb(ph, [128, 2, 2, 128], F32, "S0f")
        Sm = [b.sb(ph, [128, 512], BF16, "Sm") for _ in range(2)]
        pst = [b.ps(ph, [128, 512], F32, "pst") for _ in range(2)]
        po = b.ps(ph, [128, 512], F32, "po")
        pc = [b.ps(ph, [128, 512], F32, "pc") for _ in range(2)]
        pn = b.ps(ph, [128, 512], F32, "pn")
        pss = b.ps(ph, [128, 512], F32, "pss")
        o = b.sb(ph, [128, 512], F32, "o")
        osq = b.sb(ph, [128, 512], F32, "osq")
        rs = b.sb(ph, [128, 512], F32, "rs")
        tq = b.sb(ph, [128, 512], F32, "tq")
        gch = b.sb(ph, [128, 512], F32, "gch")
        ya = b.sb(ph, [128, 512], BF16, "ya")
        sts = b.sb(ph, [128, 128], F32, "sts")
        b.load(lgt, lgt[:], dr["lgam"], dr["lgam"].t[i:i + 1, :].partition_broadcast(128))
        b.load(hm, hm[:], dr["hm"], dr["hm"].t)
        b.load(retg, retg[:], dr["retgT"], dr["retgT"].t)
        b.load(dtab, dtab[:], dr["dtab"], dr["dtab"].t)
        b.load(posf, posf[:], dr["posf"], dr["posf"].t)
        b.load(posb, posb[:], dr["posb"], dr["posb"].t)
        b.load(posst, posst[:], dr["posst"], dr["posst"].t)
        b.act(nlg, nlg[:], lgt, lgt[:], AF.Exp, scale=-1.0)
        b.ts("dve", nlg, nlg[:], nlg, nlg[:], 1.0, ALU.add)
        b.act(nlg, nlg[:], nlg, nlg[:], AF.Ln)
        b.ts("dve", lgt, lgt[:], nlg, nlg[:], -1.0, ALU.mult)
        for h in range(4):
            lf = lgt[:, h:h + 1]
            lb = lgt[:, 4 + h:5 + h]
            nlb = nlg[:, 4 + h:5 + h]
            b.ts("dve", ta, ta[:], dtab, dtab[:], 0.0, ALU.max)
            b.act(ta, ta[:], ta, ta[:], AF.Exp, scale=lf, extra=[lgt])
            b.ts("dve", tb, tb[:], dtab, dtab[:], 0.0, ALU.is_ge)
            b.tt("dve", Th, Th[:], ta, ta[:], tb, tb[:], ALU.mult)
            b.ts("dve", ta, ta[:], dtab, dtab[:], 0.0, ALU.min)
            b.act(ta, ta[:], ta, ta[:], AF.Exp, scale=nlb, extra=[nlg])
            b.ts("dve", tb, tb[:], dtab, dtab[:], 0.0, ALU.is_le)
            b.tt("dve", ta, ta[:], ta, ta[:], tb, tb[:], ALU.mult)
            b.tt("dve", Th, Th[:], Th, Th[:], ta, ta[:], ALU.add)
            b.act(wfr, wfr[:], posf, posf[:], AF.Exp, scale=lf, extra=[lgt])
            b.act(wbr, wbr[:], posb, posb[:], AF.Exp, scale=lb, extra=[lgt])
            b.act(wst, wst[:, :, 0], posst, posst[:, :, 0], AF.Exp, scale=lf, extra=[lgt])
            b.act(wst, wst[:, :, 1], posst, posst[:, :, 1], AF.Exp, scale=lb, extra=[lgt])
            b.ts("dve", wst, wst[:], wst, wst[:], 0.125, ALU.mult)
            b.memset("pool", S0f, S0f[:], 0.0)
            for d in range(2):
                for s in range(2):
                    b.load(S0f, S0f[h * 32:(h + 1) * 32, d, s, :], dr["sret"], dr["sret"].t[i, d, h, s * 32:(s + 1) * 32, :])
            b.cp("pool", S0, S0[:], S0f, S0f[:])
            for si, (off, L, lat) in enumerate(SEQS):
                nj = L // 128
                IG = min(512, L)
                b.load(Q, Q[:, :, 0:L], qkr, qkr.t[0:2, :, off:off + L].rearrange("c p t -> p c t"))
                b.load(K, K[:, :, 0:L], qkr, qkr.t[2:4, :, off:off + L].rearrange("c p t -> p c t"))
                b.load(V, V[:, 0:nj, :], vtok, vtok.t[off:off + L, h * 128:(h + 1) * 128].rearrange("(j p) e -> p j e", p=128))
                b.ts("dve", Kh, Kh[:, :, 0:L], K, K[:, :, 0:L], hm[:, h:h + 1], ALU.mult, extra=[hm])
                it = 0
                for ig in range(L // IG):
                    i0 = ig * IG
                    for jt in range(nj):
                        p_ = pst[it % 2]
                        s_ = Sm[it % 2]
                        it += 1
                        b.mm(p_, p_[:, 0:IG], Kh, Kh[:, 0, jt * 128:(jt + 1) * 128], Q, Q[:, 0, i0:i0 + IG], start=True, stop=False)
                        b.mm(p_, p_[:, 0:IG], Kh, Kh[:, 1, jt * 128:(jt + 1) * 128], Q, Q[:, 1, i0:i0 + IG], start=False, stop=True)
                        m0 = i0 - jt * 128 + 1920
                        b.tt("dve", s_, s_[:, 0:IG], p_, p_[:, 0:IG], Th, Th[:, m0:m0 + IG], ALU.mult)
                        b.mm(po, po[:, 0:IG], V, V[:, jt, :], s_, s_[:, 0:IG], start=(jt == 0), stop=(jt == nj - 1))
                    b.cp("act", o, o[:, 0:IG], po, po[:, 0:IG])
                    if lat:
                        for d, wr in ((0, wfr), (1, wbr)):
                            b.mm(pc[d], pc[d][:, 0:IG], S0, S0[:, d, 0, :], Q, Q[:, 0, i0:i0 + IG], start=True, stop=False)
                            b.mm(pc[d], pc[d][:, 0:IG], S0, S0[:, d, 1, :], Q, Q[:, 1, i0:i0 + IG], start=False, stop=True)
                            b.tt("dve", tq, tq[:, 0:IG], pc[d], pc[d][:, 0:IG], wr, wr[:, i0:i0 + IG], ALU.mult)
                            b.tt("dve", o, o[:, 0:IG], o, o[:, 0:IG], tq, tq[:, 0:IG], ALU.add)
                    b.act(osq, osq[:, 0:IG], o, o[:, 0:IG], AF.Square)
                    b.mm(pn, pn[:, 0:IG], ones, ones[:], osq, osq[:, 0:IG])
                    b.act(rs, rs[:, 0:IG], pn, pn[:, 0:IG], AF.Ln, scale=1.0 / 128.0, bias=epsc[:, 0:1], extra=[epsc])
                    b.act(rs, rs[:, 0:IG], rs, rs[:, 0:IG], AF.Exp, scale=-0.5)
                    b.load(gch, gch[:, 0:IG], pT, pT.t[8 + h, :, off + i0:off + i0 + IG])
                    b.act(gch, gch[:, 0:IG], gch, gch[:, 0:IG], AF.Silu)
                    b.tt("dve", o, o[:, 0:IG], o, o[:, 0:IG], rs, rs[:, 0:IG], ALU.mult)
                    b.stt(ya, ya[:, 0:IG], o, o[:, 0:IG], retg[:, i, h:h + 1], gch, gch[:, 0:IG], ALU.mult, ALU.mult, extra=[retg])
                    b.load(yTd, yTd.t[h, :, off + i0:off + i0 + IG], ya, ya[:, 0:IG])
                if not lat:
                    b.load(kt, kt[:], ktok, ktok.t[off:off + L, :].rearrange("(j p) c -> p j c", p=128))
                    for d in range(2):
                        for jt in range(2):
                            b.ts("dve", Vs, Vs[:, d, jt, :], V, V[:, jt, :], wst[:, jt, d:d + 1], ALU.mult, extra=[wst])
                    for d in range(2):
                        for s in range(2):
                            for jt in range(2):
                                b.mm(pss, pss[:, 0:128], kt, kt[:, jt, s * 128:(s + 1) * 128], Vs, Vs[:, d, jt, :], start=(jt == 0), stop=(jt == 1))
                            b.cp("act", sts, sts[:], pss, pss[:, 0:128])
                            b.load(nsr, nsr.t[si, i, d, h, s * 32:(s + 1) * 32, :], sts, sts[h * 32:(h + 1) * 32, :])
    with b.phase() as ph:
        scw = b.sb(ph, [128, 2, 4, 3], F32, "scw")
        b.load(scw, scw[:], dr["scwT"], dr["scwT"].t)
        bg = b.sb(ph, [128, LS], F32, "bg")
        cg = b.sb(ph, [128, LS], F32, "cg")
        hb_ = b.sb(ph, [128, LS], F32, "hb")
        up = b.sb(ph, [128, LS + 2], F32, "up")
        acc = b.sb(ph, [128, LS], F32, "acc")
        yb = b.sb(ph, [128, LS], BF16, "yb")
        for (off, L, lat) in SEQS:
            for cb in range(4):
                b.load(bg, bg[:, 0:L], pT, pT.t[12 + cb, :, off:off + L])
                b.load(cg, cg[:, 0:L], pT, pT.t[16 + cb, :, off:off + L])
                b.load(hb_, hb_[:, 0:L], pT, pT.t[20 + cb, :, off:off + L])
                b.memset("pool", up, up[:, 0:1], 0.0)
                b.memset("pool", up, up[:, L + 1:L + 2], 0.0)
                b.tt("pool", up, up[:, 1:L + 1], cg, cg[:, 0:L], hb_, hb_[:, 0:L], ALU.mult)
                b.ts("dve", acc, acc[:, 0:L], up, up[:, 0:L], scw[:, i, cb, 0:1], ALU.mult, extra=[scw])
                b.stt(acc, acc[:, 0:L], up, up[:, 1:L + 1], scw[:, i, cb, 1:2], acc, acc[:, 0:L], ALU.mult, ALU.add, extra=[scw])
                b.stt(acc, acc[:, 0:L], up, up[:, 2:L + 2], scw[:, i, cb, 2:3], acc, acc[:, 0:L], ALU.mult, ALU.add, extra=[scw])
                b.tt("dve", yb, yb[:, 0:L], acc, acc[:, 0:L], bg, bg[:, 0:L], ALU.mult)
                b.load(yTd, yTd.t[4 + cb, :, off:off + L], yb, yb[:, 0:L])


def odd_mixer(b, nc, dr, i, pT, gbtok, qkvT, ktok32, vtok32, oTd, yTd, nsg, ident, ones, epsc, onec, outs_written):
    with b.phase() as ph:
        gcw = b.sb(ph, [128, 2, 12, 3], F32, "gcw")
        b.load(gcw, gcw[:], dr["gcwT"], dr["gcwT"].t)
        xp = b.sb(ph, [128, LS + 2], F32, "xp")
        acc = b.sb(ph, [128, LS], F32, "acc")
        sq = b.sb(ph, [128, 512], F32, "sq")
        rs = b.sb(ph, [128, 512], F32, "rs")
        pn = b.ps(ph, [128, 512], F32, "pn")
        ptr = b.ps(ph, [128, 512], F32, "ptr")
        tk = b.sb(ph, [128, 512], F32, "tk")
        for (off, L, lat) in SEQS:
            for ch in range(12):
                b.memset("pool", xp, xp[:, 0:1], 0.0)
                b.memset("pool", xp, xp[:, L + 1:L + 2], 0.0)
                b.load(xp, xp[:, 1:L + 1], pT, pT.t[ch, :, off:off + L])
                b.ts("dve", acc, acc[:, 0:L], xp, xp[:, 0:L], gcw[:, i, ch, 0:1], ALU.mult, extra=[gcw])
                b.stt(acc, acc[:, 0:L], xp, xp[:, 1:L + 1], gcw[:, i, ch, 1:2], acc, acc[:, 0:L], ALU.mult, ALU.add, extra=[gcw])
                b.stt(acc, acc[:, 0:L], xp, xp[:, 2:L + 2], gcw[:, i, ch, 2:3], acc, acc[:, 0:L], ALU.mult, ALU.add, extra=[gcw])
                b.act(acc, acc[:, 0:L], acc, acc[:, 0:L], AF.Silu)
                W = min(512, L)
                for pc_ in range(L // W):
                    sl = slice(pc_ * W, (pc_ + 1) * W)
                    if ch < 8:
                        b.act(sq, sq[:, 0:W], acc, acc[:, sl], AF.Square)
                        b.mm(pn, pn[:, 0:W], ones, ones[:], sq, sq[:, 0:W])
                        b.act(rs, rs[:, 0:W], pn, pn[:, 0:W], AF.Ln, bias=epsc[:, 0:1], extra=[epsc])
                        b.act(rs, rs[:, 0:W], rs, rs[:, 0:W], AF.Exp, scale=-0.5)
                        if ch < 4:
                            b.stt(acc, acc[:, sl], acc, acc[:, sl], 128.0 ** -0.5, rs, rs[:, 0:W], ALU.mult, ALU.mult)
                        else:
                            b.tt("dve", acc, acc[:, sl], acc, acc[:, sl], rs, rs[:, 0:W], ALU.mult)
                    if ch >= 4:
                        dst = ktok32 if ch < 8 else vtok32
                        hh = ch % 4
                        nt_ = W // 128
                        for t_ in range(nt_):
                            b.tr(ptr, ptr[:, t_ * 128:(t_ + 1) * 128], acc, acc[:, pc_ * W + t_ * 128: pc_ * W + (t_ + 1) * 128], ident)
                        b.cp("act", tk, tk[:, 0:W], ptr, ptr[:, 0:W])
                        tok0 = off + pc_ * W
                        b.load(dst, dst.t[tok0:tok0 + W, hh * 128:(hh + 1) * 128].rearrange("(t p) e -> p t e", p=128),
                               tk, tk[:, 0:W].rearrange("p (t e) -> p t e", e=128))
                b.load(qkvT, qkvT.t[ch, :, off:off + L], acc, acc[:, 0:L])
    with b.phase() as ph:
        Mf = b.sb(ph, [128, 128], F32, "Mf")
        Mb = b.sb(ph, [128, 128], F32, "Mb")
        Sf = b.sb(ph, [128, 128], F32, "Sf")
        Sb = b.sb(ph, [128, 128], F32, "Sb")
        for t_, k_ in ((Mf, "Mf"), (Mb, "Mb"), (Sf, "Sf"), (Sb, "Sb")):
            b.load(t_, t_[:], dr[k_], dr[k_].t)

        def t4(nm):
            return b.sb(ph, [128, 4, 128], F32, nm)
        qTt, kTt, ktk, vtk = t4("qTt"), t4("kTt"), t4("ktk"), t4("vtk")
        gb = b.sb(ph, [128, 16], F32, "gb")
        gc = b.sb(ph, [128, 4], F32, "gc")
        gl = b.sb(ph, [128, 4], F32, "gl")
        egl = b.sb(ph, [128, 4], F32, "egl")
        eg = b.sb(ph, [128, 4], F32, "eg")
        egd = b.sb(ph, [128, 4], F32, "egd")
        bge = b.sb(ph, [128, 4], F32, "bge")
        gB = [b.sb(ph, [128, 128], F32, "gB") for _ in range(2)]
        nd, dec, expbc, dincl, dstr = t4("nd"), t4("dec"), t4("expbc"), t4("dincl"), t4("dstr")
        A, AT, attn, attnT = t4("A"), t4("AT"), t4("attn"), t4("attnT")
        Pa, PTa, Pb, PTb, TT, Tm = t4("Pa"), t4("PTa"), t4("Pb"), t4("PTb"), t4("TT"), t4("Tm")
        Elo = b.sb(ph, [128, 7, 128], F32, "Elo")
        Eup = b.sb(ph, [128, 7, 128], F32, "Eup")
        b.load(Elo, Elo[:], dr["Elo"], dr["Elo"].t)
        b.load(Eup, Eup[:], dr["Eup"], dr["Eup"].t)
        kbg, vb, kd, u, wT, qgT, vnew, S, ost = t4("kbg"), t4("vb"), t4("kd"), t4("u"), t4("wT"), t4("qgT"), t4("vnew"), t4("S"), t4("ost")
        pS = b.ps(ph, [128, 512], F32, "pS")
        P_ = [b.ps(ph, [128, 512], F32, "pg") for _ in range(7)]

        def hs(t_, h):
            return t_[:, h, :]

        def ph_(p, h):
            return p[:, h * 128:(h + 1) * 128]

        def p3(p):
            return p[:].rearrange("p (h f) -> p h f", h=4)

        for si, (off, L, lat) in enumerate(SEQS):
            nt_ = L // 128
            for d in range(2):
                Mtri = Mf if d == 0 else Mb
                incl = Mb if d == 0 else Mf
                strict = Sb if d == 0 else Sf
                if lat:
                    b.load(S, S[:], dr["sgdn"], dr["sgdn"].t[i, d].rearrange("h k v -> k h v"))
                else:
                    b.memset("pool", S, S[:], 0.0)
                order = range(nt_) if d == 0 else range(nt_ - 1, -1, -1)
                for ti in order:
                    tok0 = off + ti * 128
                    b.load(qTt, qTt[:], qkvT, qkvT.t[0:4, :, tok0:tok0 + 128].rearrange("h p t -> p h t"))
                    b.load(kTt, kTt[:], qkvT, qkvT.t[4:8, :, tok0:tok0 + 128].rearrange("h p t -> p h t"))
                    b.load(ktk, ktk[:], ktok32, ktok32.t[tok0:tok0 + 128, :].rearrange("p (h e) -> p h e", h=4))
                    b.load(vtk, vtk[:], vtok32, vtok32.t[tok0:tok0 + 128, :].rearrange("p (h e) -> p h e", h=4))
                    b.load(gb, gb[:], gbtok, gbtok.t[tok0:tok0 + 128, :])
                    gcol = gb[:, d * 4:d * 4 + 4]
                    bcol = gb[:, 8 + d * 4:8 + d * 4 + 4]
                    b.mm(pS, pS[:, 0:4], Mtri, Mtri[:], gb, gcol)
                    b.mm(pS, pS[:, 4:8], ones, ones[:], gb, gcol)
                    b.cp("dve", gc, gc[:], pS, pS[:, 0:4])
                    b.cp("dve", gl, gl[:], pS, pS[:, 4:8])
                    b.act(egl, egl[:], gl, gl[:], AF.Exp)
                    b.act(eg, eg[:], gc, gc[:], AF.Exp)
                    b.tt("dve", egd, egd[:], gl, gl[:], gc, gc[:], ALU.subtract)
                    b.act(egd, egd[:], egd, egd[:], AF.Exp)
                    b.tt("dve", bge, bge[:], eg, eg[:], gb, bcol, ALU.mult)
                    for h in range(4):
                        g_ = gB[h % 2]
                        b.ts("pool", g_, g_[:], ones, ones[:], gb[:, d * 4 + h:d * 4 + h + 1], ALU.mult, extra=[gb])
                        b.mm(P_[0], ph_(P_[0], h), g_, g_[:], Mtri, Mtri[:])
                    for h in range(4):
                        b.ts("dve", nd, hs(nd, h), P_[0], ph_(P_[0], h), gc[:, h:h + 1], ALU.subtract, 0.0, ALU.max, extra=[gc])
                    b.act(dec, dec[:], nd, nd[:], AF.Exp, scale=-1.0)
                    b.act(expbc, expbc[:], P_[0], p3(P_[0]), AF.Exp)
                    b.tt("pool", dincl, dincl[:], dec, dec[:], incl, incl[:].unsqueeze(1).to_broadcast([128, 4, 128]), ALU.mult)
                    b.tt("pool", dstr, dstr[:], dec, dec[:], strict, strict[:].unsqueeze(1).to_broadcast([128, 4, 128]), ALU.mult)
                    for h in range(4):
                        b.mm(P_[1], ph_(P_[1], h), kTt, hs(kTt, h), kTt, hs(kTt, h))
                        b.mm(P_[2], ph_(P_[2], h), qTt, hs(qTt, h), kTt, hs(kTt, h))
                    for h in range(4):
                        b.stt(A, hs(A, h), P_[1], ph_(P_[1], h), gb[:, 8 + d * 4 + h:8 + d * 4 + h + 1], dstr, hs(dstr, h), ALU.mult, ALU.mult, extra=[gb])
                    b.tt("dve", attn, attn[:], P_[2], p3(P_[2]), dincl, dincl[:], ALU.mult)
                    for h in range(4):
                        b.tr(P_[3], ph_(P_[3], h), A, hs(A, h), ident)
                        b.tr(P_[4], ph_(P_[4], h), attn, hs(attn, h), ident)
                    b.cp("act", AT, AT[:], P_[3], p3(P_[3]))
                    b.cp("act", attnT, attnT[:], P_[4], p3(P_[4]))
                    b.cp("pool", Tm, Tm[:], ident, ident[:].unsqueeze(1).to_broadcast([128, 4, 128]))
                    b.cp("pool", TT, TT[:], ident, ident[:].unsqueeze(1).to_broadcast([128, 4, 128]))
                    EA_t = Elo if d == 0 else Eup
                    EAT_t = Eup if d == 0 else Elo
                    for lv in range(7):
                        b.tt("pool", Pa, Pa[:], A, A[:], EA_t, EA_t[:, lv, :].unsqueeze(1).to_broadcast([128, 4, 128]), ALU.mult)
                        b.tt("pool", PTa, PTa[:], AT, AT[:], EAT_t, EAT_t[:, lv, :].unsqueeze(1).to_broadcast([128, 4, 128]), ALU.mult)
                        for h in range(4):
                            b.mm(P_[1], ph_(P_[1], h), PTa, hs(PTa, h), Tm, hs(Tm, h))
                            b.mm(P_[2], ph_(P_[2], h), Pa, hs(Pa, h), TT, hs(TT, h))
                        b.cp("act", Pb, Pb[:], P_[1], p3(P_[1]))
                        b.cp("dve", PTb, PTb[:], P_[2], p3(P_[2]))
                        for h in range(4):
                            b.mm(P_[0], ph_(P_[0], h), TT, hs(TT, h), Pb, hs(Pb, h))
                            b.mm(P_[3], ph_(P_[3], h), Tm, hs(Tm, h), PTb, hs(PTb, h))
                        b.tt("dve", Tm, Tm[:], Tm, Tm[:], P_[0], p3(P_[0]), ALU.subtract)
                        b.tt("dve", TT, TT[:], TT, TT[:], P_[3], p3(P_[3]), ALU.subtract)
                    b.tt("pool", kbg, kbg[:], ktk, ktk[:], bge, bge[:].unsqueeze(2).to_broadcast([128, 4, 128]), ALU.mult)
                    b.tt("pool", vb, vb[:], vtk, vtk[:], gb, bcol.unsqueeze(2).to_broadcast([128, 4, 128]), ALU.mult)
                    b.tt("pool", kd, kd[:], ktk, ktk[:], egd, egd[:].unsqueeze(2).to_broadcast([128, 4, 128]), ALU.mult)
                    b.tt("dve", qgT, qgT[:], qTt, qTt[:], expbc, expbc[:], ALU.mult)
                    for h in range(4):
                        b.mm(P_[3], ph_(P_[3], h), TT, hs(TT, h), vb, hs(vb, h))
                        b.mm(P_[4], ph_(P_[4], h), kbg, hs(kbg, h), TT, hs(TT, h))
                    b.cp("act", u, u[:], P_[3], p3(P_[3]))
                    b.cp("act", wT, wT[:], P_[4], p3(P_[4]))
                    for h in range(4):
                        b.mm(P_[5], ph_(P_[5], h), wT, hs(wT, h), S, hs(S, h))
                    b.tt("dve", vnew, vnew[:], u, u[:], P_[5], p3(P_[5]), ALU.subtract)
                    for h in range(4):
                        b.mm(P_[6], ph_(P_[6], h), S, hs(S, h), qgT, hs(qgT, h), start=True, stop=False)
                        b.mm(P_[6], ph_(P_[6], h), vnew, hs(vnew, h), attnT, hs(attnT, h), start=False, stop=True)
                    for h in range(4):
                        b.mm(P_[5], ph_(P_[5], h), kd, hs(kd, h), vnew, hs(vnew, h))
                    b.cp("act", ost, ost[:], P_[6], p3(P_[6]))
                    b.load(oTd, oTd.t[d, :, :, tok0:tok0 + 128].rearrange("h p t -> p h t"), ost, ost[:])
                    for h in range(4):
                        b.stt(S, hs(S, h), S, hs(S, h), egl[:, h:h + 1], P_[5], ph_(P_[5], h), ALU.mult, ALU.add, extra=[egl])
                if not lat:
                    b.load(nsg, nsg.t[si, i, d].rearrange("h k v -> k h v"), S, S[:])
    with b.phase() as ph:
        gng = b.sb(ph, [128, 2], F32, "gng")
        cfw = b.sb(ph, [128, 2, 4, 31], F32, "cfw")
        cfb = b.sb(ph, [128, 2, 4], F32, "cfb")
        lng = b.sb(ph, [128, 2, 4], F32, "lng")
        lnb = b.sb(ph, [128, 2, 4], F32, "lnb")
        for t_, k_ in ((gng, "gdngT"), (cfw, "cfwT"), (cfb, "cfbT"), (lng, "lngT"), (lnb, "lnbT")):
            b.load(t_, t_[:], dr[k_], dr[k_].t)
        of = b.sb(ph, [128, 512], F32, "of")
        ob = b.sb(ph, [128, 512], F32, "ob")
        zz = b.sb(ph, [128, 512], F32, "zz")
        sq = b.sb(ph, [128, 512], F32, "sq")
        rs = b.sb(ph, [128, 512], F32, "rs")
        yc = b.sb(ph, [128, 512], BF16, "yc")
        pn = b.ps(ph, [128, 512], F32, "pn")
        pv = b.ps(ph, [128, 512], F32, "pv")
        for tg in range(6):
            cs_ = slice(tg * 512, (tg + 1) * 512)
            for h in range(4):
                b.load(of, of[:], oTd, oTd.t[0, h, :, cs_])
                b.load(ob, ob[:], oTd, oTd.t[1, h, :, cs_])
                b.load(zz, zz[:], pT, pT.t[12 + h, :, cs_])
                b.tt("dve", of, of[:], of, of[:], ob, ob[:], ALU.add)
                b.act(sq, sq[:], of, of[:], AF.Square)
                b.mm(pn, pn[:], ones, ones[:], sq, sq[:])
                b.act(rs, rs[:], pn, pn[:], AF.Ln, scale=1.0 / 128.0, bias=epsc[:, 0:1], extra=[epsc])
                b.act(rs, rs[:], rs, rs[:], AF.Exp, scale=-0.5)
                b.act(zz, zz[:], zz, zz[:], AF.Silu)
                b.tt("dve", of, of[:], of, of[:], rs, rs[:], ALU.mult)
                b.stt(yc, yc[:], of, of[:], gng[:, i:i + 1], zz, zz[:], ALU.mult, ALU.mult, extra=[gng])
                b.load(yTd, yTd.t[h, :, cs_], yc, yc[:])
        ca = b.sb(ph, [128, LS], F32, "ca")
        cg = b.sb(ph, [128, LS], F32, "cg")
        hp = b.sb(ph, [128, LS + 30], F32, "hp")
        cv = b.sb(ph, [128, 4, LS], F32, "cv")
        mean = b.sb(ph, [128, 512], F32, "mean")
        xc = b.sb(ph, [128, 4, 512], F32, "xc")
        sq4 = b.sb(ph, [128, 4, 512], F32, "sq4")
        for (off, L, lat) in SEQS:
            for cb in range(4):
                b.load(ca, ca[:, 0:L], pT, pT.t[16 + cb, :, off:off + L])
                b.load(cg, cg[:, 0:L], pT, pT.t[20 + cb, :, off:off + L])
                b.memset("pool", hp, hp[:, 0:15], 0.0)
                b.memset("pool", hp, hp[:, L + 15:L + 30], 0.0)
                b.act(cg, cg[:, 0:L], cg, cg[:, 0:L], AF.Sigmoid)
                b.tt("pool", hp, hp[:, 15:L + 15], ca, ca[:, 0:L], cg, cg[:, 0:L], ALU.mult)
                b.ts("dve", cv, cv[:, cb, 0:L], hp, hp[:, 0:L], cfw[:, i, cb, 0:1], ALU.mult, cfb[:, i, cb:cb + 1], ALU.add, extra=[cfw, cfb])
                for k in range(1, 31):
                    b.stt(cv, cv[:, cb, 0:L], hp, hp[:, k:k + L], cfw[:, i, cb, k:k + 1], cv, cv[:, cb, 0:L], ALU.mult, ALU.add, extra=[cfw])
            W = min(512, L)
            for pc_ in range(L // W):
                sl = slice(pc_ * W, (pc_ + 1) * W)
                for cb in range(4):
                    b.mm(pn, pn[:, 0:W], ones, ones[:], cv, cv[:, cb, sl], start=(cb == 0), stop=(cb == 3))
                b.act(mean, mean[:, 0:W], pn, pn[:, 0:W], AF.Copy, scale=1.0 / 512.0)
                for cb in range(4):
                    b.tt("dve", xc, xc[:, cb, 0:W], cv, cv[:, cb, sl], mean, mean[:, 0:W], ALU.subtract)
                b.act(sq4, sq4[:, :, 0:W], xc, xc[:, :, 0:W], AF.Square)
                for cb in range(4):
                    b.mm(pv, pv[:, 0:W], ones, ones[:], sq4, sq4[:, cb, 0:W], start=(cb == 0), stop=(cb == 3))
                b.act(rs, rs[:, 0:W], pv, pv[:, 0:W], AF.Ln, scale=1.0 / 512.0, bias=epsc[:, 0:1], extra=[epsc])
                b.act(rs, rs[:, 0:W], rs, rs[:, 0:W], AF.Exp, scale=-0.5)
                for cb in range(4):
                    b.tt("dve", xc, xc[:, cb, 0:W], xc, xc[:, cb, 0:W], rs, rs[:, 0:W], ALU.mult)
                    b.act(yc, yc[:, 0:W], xc, xc[:, cb, 0:W], AF.Silu, scale=lng[:, i, cb:cb + 1], bias=lnb[:, i, cb:cb + 1], extra=[lng, lnb])
                    b.load(yTd, yTd.t[4 + cb, :, off + pc_ * W:off + (pc_ + 1) * W], yc, yc[:, 0:W])


def peer_phase(b, nc, dr, l, i, even, xT, yTd, modT, A2, ident, ones, epsc, norm_mod, xview, do_peer=True, scr=None):
    uTb, vbd, ekg, h2bd = scr
    if do_peer:
        peer_cast(b, nc, dr, l, uTb, vbd)
    with b.phase() as ph:
        wout = b.sb(ph, [128, 8, 1024], BF16, "wout")
        wq = b.sb(ph, [128, 8, 2048], BF16, "wq")
        keys = b.sb(ph, [128, 16, 128], BF16, "keys")
        iota16 = b.sb(ph, [128, 16], F32, "iota16")
        wsrc = dr["ev_w_out" if even else "od_w_out"]
        cstg = [b.sb(ph, [128, 1024], F32, "cstg") for _ in range(2)]
        ck = 0
        for kc in range(8):
            b.load_cast(cstg, wout, wout[:, kc, :], wsrc, wsrc.t[i, kc * 128:(kc + 1) * 128, :], 1024, ck)
            ck += 1
            for c0 in (0, 1024):
                b.load_cast(cstg, wq, wq[:, kc, c0:c0 + 1024], dr["peer_wq"], dr["peer_wq"].t[l, kc * 128:(kc + 1) * 128, c0:c0 + 1024], 1024, ck)
                ck += 1
        for c8 in range(2):
            st = cstg[ck % 2]
            ck += 1
            b.load(st, st[:].rearrange("p (c n) -> p c n", c=8), dr["keysT"], dr["keysT"].t[l, c8 * 8:(c8 + 1) * 8].rearrange("c d n -> d c n"))
            b.cp("pool", keys, keys[:, c8 * 8:(c8 + 1) * 8, :], st, st[:].rearrange("p (c n) -> p c n", c=8))
        b.load(iota16, iota16[:], dr["iota16"], dr["iota16"].t)
        thr16 = b.sb(ph, [128, 16], F32, "thr16")
        b.ts("dve", thr16, thr16[:], iota16, iota16[:], 16.0, ALU.mult)
        xn = b.sb(ph, [128, 8, 512], F32, "xn")
        h2f = b.sb(ph, [128, 8, 512], F32, "h2f")
        h2b = b.sb(ph, [128, 8, 512], BF16, "h2b")
        yg, xg, sq = h2b, h2f, h2f
        tmp = b.sb(ph, [128, 512], F32, "tmp")
        rstd = b.sb(ph, [128, 512], F32, "rstd")
        psn = b.ps(ph, [128, 512], F32, "psn")
        pmm = [b.ps(ph, [128, 512], F32, "pmm") for _ in range(2)]
        psc = [b.ps(ph, [128, 512], F32, "psc") for _ in range(4)]
        qT = b.sb(ph, [128, 16, 512], BF16, "qT")
        h2t = b.sb(ph, [128, 1024], F32, "h2t")
        ssb = b.sb(ph, [128, 16, 128], F32, "ssb")
        wk = b.sb(ph, [128, 256], F32, "wk")
        mx = b.sb(ph, [128, 16, 16], F32, "mx")
        mi = b.sb(ph, [128, 16, 16], U32, "mi")
        mif = b.sb(ph, [128, 16, 16], F32, "mif")
        i1s = b.sb(ph, [128, 8, 16], F32, "i1s")
        cs_ = b.sb(ph, [128, 8, 256], F32, "cs")
        tsv = b.sb(ph, [128, 8, 16], F32, "tsv")
        sel = b.sb(ph, [128, 8, 16], U32, "sel")
        self_ = b.sb(ph, [128, 8, 16], F32, "self")
        aq = b.sb(ph, [128, 8, 16], F32, "aq")
        bq = b.sb(ph, [128, 8, 16], F32, "bq")
        oh = b.sb(ph, [128, 128, 16], F32, "oh")
        i1sel = b.sb(ph, [128, 128], F32, "i1sel")
        i2sel = b.sb(ph, [128, 128], F32, "i2sel")
        eidx = b.sb(ph, [128, 128], I32, "eidx")
        gt = b.sb(ph, [128, 8, 16], F32, "gt")
        zs = b.sb(ph, [128, 8], F32, "zs")
        pre = b.sb(ph, [128, 128], F32, "pre")
        wgt = b.sb(ph, [128, 128], F32, "wgt")
        junk = b.sb(ph, [128, 1024], F32, "junk")
        NB = 4
        gbuf = [b.sb(ph, [128, 1024], F32, "gbuf") for _ in range(NB)]
        acc = b.sb(ph, [128, 1024], F32, "acc")
        gk = 0
        mxv = mx[:].rearrange("p (h t) k -> p h t k", t=2)
        mifv = mif[:].rearrange("p (h t) k -> p h t k", t=2)
        for tg in range(6):
            v = 0 if tg < 2 else 1
            b.load(yg, yg[:], yTd, xview(yTd, tg * 512, 512))
            b.load(xg, xg[:], xT, xview(xT, tg * 512, 512))
            for dc in range(8):
                p_ = pmm[dc % 2]
                for kc in range(8):
                    b.mm(p_, p_[:], wout, wout[:, kc, dc * 128:(dc + 1) * 128], yg, yg[:, kc, :], start=(kc == 0), stop=(kc == 7))
                b.stt(xn, xn[:, dc, :], p_, p_[:], modT[:, l, 16 + dc, v:v + 1], xg, xg[:, dc, :], ALU.mult, ALU.add, extra=[modT])
            if not do_peer:
                b.load(xT, xview(xT, tg * 512, 512), xn, xn[:])
                continue
            norm_mod(ph, xn, 512, lambda c: A2[:, l, c, v:v + 1], lambda c: modT[:, l, 24 + c, v:v + 1], hb=h2b, hf=h2f, tmp=tmp, sq=sq, ps_=psn, rstd=rstd)
            for hp_ in range(16):
                p_ = pmm[hp_ % 2]
                for kc in range(8):
                    b.mm(p_, p_[:], wq, wq[:, kc, hp_ * 128:(hp_ + 1) * 128], h2b, h2b[:, kc, :], start=(kc == 0), stop=(kc == 7))
                b.cp("act", qT, qT[:, hp_, :], p_, p_[:])
            for tt_ in range(4):
                ts_ = slice(tt_ * 128, (tt_ + 1) * 128)
                for hp_ in range(16):
                    pq = psc[hp_ // 4]
                    b.mm(pq, pq[:, (hp_ % 4) * 128:(hp_ % 4 + 1) * 128], qT, qT[:, hp_, ts_], keys, keys[:, hp_, :])
                for q4 in range(4):
                    b.cp("act", ssb, ssb[:, q4 * 4:(q4 + 1) * 4, :], psc[q4], psc[q4][:].rearrange("p (a n) -> p a n", a=4))
                for hp_ in range(16):
                    b.I("dve", lambda hp_=hp_: nc.vector.max(out=mx[:, hp_, 0:8], in_=ssb[:, hp_, :]), r=[ssb], w=[mx])
                    b.I("dve", lambda hp_=hp_: nc.vector.max_index(out=mi[:, hp_, 0:8], in_max=mx[:, hp_, 0:8], in_values=ssb[:, hp_, :]), r=[ssb, mx], w=[mi])
                    b.I("dve", lambda hp_=hp_: nc.vector.match_replace(out=wk[:, 0:128], in_to_replace=mx[:, hp_, 0:8], in_values=ssb[:, hp_, :], imm_value=-1e30), r=[ssb, mx], w=[wk])
                    b.I("dve", lambda hp_=hp_: nc.vector.max(out=mx[:, hp_, 8:16], in_=wk[:, 0:128]), r=[wk], w=[mx])
                    b.I("dve", lambda hp_=hp_: nc.vector.max_index(out=mi[:, hp_, 8:16], in_max=mx[:, hp_, 8:16], in_values=wk[:, 0:128]), r=[wk, mx], w=[mi])
                b.cp("dve", mif, mif[:], mi, mi[:])
                csv = cs_[:].rearrange("p h (a c) -> p h a c", a=16)
                b.tt("dve", cs_, csv, mx, mxv[:, :, 0, :].unsqueeze(3).to_broadcast([128, 8, 16, 16]),
                     mx, mxv[:, :, 1, :].unsqueeze(2).to_broadcast([128, 8, 16, 16]), ALU.add)
                for h in range(8):
                    b.I("dve", lambda h=h: nc.vector.max(out=tsv[:, h, 0:8], in_=cs_[:, h, :]), r=[cs_], w=[tsv])
                    b.I("dve", lambda h=h: nc.vector.max_index(out=sel[:, h, 0:8], in_max=tsv[:, h, 0:8], in_values=cs_[:, h, :]), r=[cs_, tsv], w=[sel])
                    b.I("dve", lambda h=h: nc.vector.match_replace(out=wk[:], in_to_replace=tsv[:, h, 0:8], in_values=cs_[:, h, :], imm_value=-1e30), r=[cs_, tsv], w=[wk])
                    b.I("dve", lambda h=h: nc.vector.max(out=tsv[:, h, 8:16], in_=wk[:]), r=[wk], w=[tsv])
                    b.I("dve", lambda h=h: nc.vector.max_index(out=sel[:, h, 8:16], in_max=tsv[:, h, 8:16], in_values=wk[:]), r=[wk, tsv], w=[sel])
                b.cp("dve", self_, self_[:], sel, sel[:])
                ohv0 = oh[:].rearrange("p (h k) a -> p h k a", h=8)
                b.tt("dve", oh, ohv0, self_, self_[:].unsqueeze(3).to_broadcast([128, 8, 16, 16]),
                     thr16, thr16[:].unsqueeze(1).unsqueeze(1).to_broadcast([128, 8, 16, 16]), ALU.is_ge)
                b.I("dve", lambda: nc.vector.tensor_reduce(out=aq[:].rearrange("p h k -> p (h k)"), in_=oh[:], axis=AX.X, op=ALU.add), r=[oh], w=[aq])
                b.ts("dve", aq, aq[:], aq, aq[:], -1.0, ALU.add)
                b.stt(bq, bq[:], aq, aq[:], -16.0, self_, self_[:], ALU.mult, ALU.add)
                for (qq, half, dst) in ((aq, 0, i1sel), (bq, 1, i2sel)):
                    ohv = oh[:].rearrange("p (h k) a -> p h k a", h=8)
                    b.tt("dve", oh, ohv, qq, qq[:].unsqueeze(3).to_broadcast([128, 8, 16, 16]),
                         iota16, iota16[:].unsqueeze(1).unsqueeze(1).to_broadcast([128, 8, 16, 16]), ALU.is_equal)
                    b.tt("dve", oh, ohv, oh, ohv, mif, mifv[:, :, half, :].unsqueeze(2).to_broadcast([128, 8, 16, 16]), ALU.mult)
                    b.I("dve", lambda dst=dst: nc.vector.tensor_reduce(out=dst[:], in_=oh[:], axis=AX.X, op=ALU.add), r=[oh], w=[dst])
                b.tt("dve", gt, gt[:], tsv, tsv[:], tsv, tsv[:, :, 0:1].to_broadcast([128, 8, 16]), ALU.subtract)
                b.act(gt, gt[:], gt, gt[:], AF.Exp)
                b.I("dve", lambda: nc.vector.tensor_reduce(out=zs[:], in_=gt[:], axis=AX.X, op=ALU.add), r=[gt], w=[zs])
                b.I("dve", lambda: nc.vector.reciprocal(out=zs[:], in_=zs[:]), r=[zs], w=[zs])
                b.tt("dve", gt, gt[:], gt, gt[:], zs, zs[:].unsqueeze(2).to_broadcast([128, 8, 16]), ALU.mult)
                tok0 = tg * 512 + tt_ * 128
                b.load(ekg, ekg.t[0, tok0:tok0 + 128, :], i1sel, i1sel[:])
                b.load(ekg, ekg.t[1, tok0:tok0 + 128, :], i2sel, i2sel[:])
                b.load(ekg, ekg.t[2, tok0:tok0 + 128, :], gt, gt[:].rearrange("p h k -> p (h k)"))
            b.load(h2bd, xview(h2bd, tg * 512, 512), h2b, h2b[:])
            b.load(xT, xview(xT, tg * 512, 512), xn, xn[:])
    if do_peer:
        peer_dense(b, nc, dr, l, xT, modT, ident, uTb, vbd, ekg, h2bd, xview)


def peer_cast(b, nc, dr, l, uTb, vbd):
    with b.phase() as ph:
        st = [b.sb(ph, [128, 4, 1024], F32, "cst") for _ in range(3)]
        sb_ = [b.sb(ph, [128, 4, 1024], BF16, "csb") for _ in range(3)]
        k = 0
        engs = ("act", "pool", "dve")
        for (src, dst) in ((dr["peer_u"], uTb), (dr["peer_v"], vbd)):
            for g in range(32):
                f_ = st[k % 3]
                o_ = sb_[k % 3]
                b.load(f_, f_[:], src, src.t[l, g * 4:(g + 1) * 4].rearrange("c p n -> p c n"))
                b.cp(engs[k % 3], o_, o_[:], f_, f_[:])
                b.load(dst, dst.t[g * 4:(g + 1) * 4].rearrange("c p n -> p c n"), o_, o_[:])
                k += 1


def peer_dense(b, nc, dr, l, xT, modT, ident, uTb, vbd, ekg, h2bd, xview):
    TG = 256
    with b.phase() as ph:
        iota = b.sb(ph, [128, 128], F32, "iota128")
        b.load(iota, iota[:], dr["iota128"], dr["iota128"].t)
        GT = b.sb(ph, [128, 128, TG], BF16, "GT")
        h2b = b.sb(ph, [128, 8, TG], BF16, "h2b")
        xn = b.sb(ph, [128, 8, TG], F32, "xn")
        ek = b.sb(ph, [128, 3, 128], F32, "ek")
        ET = b.sb(ph, [128, 3, 128], F32, "ET")
        ohA = [b.sb(ph, [128, 16, 128], BF16, "ohA") for _ in range(2)]
        ohB = [b.sb(ph, [128, 16, 128], BF16, "ohB") for _ in range(2)]
        ohT = [b.sb(ph, [128, 16, 128], BF16, "ohT") for _ in range(2)]
        NW = 6
        Uc = [b.sb(ph, [128, 1024], BF16, "Uc") for _ in range(NW)]
        Vc = [b.sb(ph, [128, 1024], BF16, "Vc") for _ in range(NW)]
        gel = [b.sb(ph, [128, TG], BF16, "gel") for _ in range(4)]
        AT = [b.sb(ph, [128, TG], BF16, "AT") for _ in range(4)]
        fo = b.sb(ph, [128, 1024], F32, "fo")
        pg = [b.ps(ph, [128, 512], F32, "pg") for _ in range(2)]
        pss = [b.ps(ph, [128, 512], F32, "pss") for _ in range(2)]
        po = [b.ps(ph, [128, 512], F32, "po") for _ in range(4)]
        for grp in range(NT // TG):
            g0 = grp * TG
            v = 0 if g0 < NPR else 1
            b.load(h2b, h2b[:], h2bd, xview(h2bd, g0, TG))
            b.load(xn, xn[:], xT, xview(xT, g0, TG))
            kq = 0
            for tl in range(TG // 128):
                tok0 = g0 + tl * 128
                b.load(ek, ek[:], ekg, ekg.t[:, tok0:tok0 + 128, :].rearrange("c t s -> t c s"))
                for c3 in range(3):
                    b.tr(pg[0], pg[0][:, c3 * 128:(c3 + 1) * 128], ek, ek[:, c3, :], ident)
                b.cp("act", ET, ET[:], pg[0], pg[0][:, 0:384].rearrange("p (c t) -> p c t", c=3))
                for sub in range(8):
                    t0 = sub * 16
                    A_ = ohA[sub % 2]
                    B_ = ohB[sub % 2]
                    T_ = ohT[sub % 2]
                    io3 = iota[:].unsqueeze(1).to_broadcast([128, 16, 128])
                    b.tt("dve", B_, B_[:], iota, io3, ET, ET[:, 0, t0:t0 + 16].unsqueeze(2).to_broadcast([128, 16, 128]), ALU.is_equal)
                    b.tt("dve", T_, T_[:], iota, io3, ET, ET[:, 1, t0:t0 + 16].unsqueeze(2).to_broadcast([128, 16, 128]), ALU.is_equal)
                    b.tt("pool", A_, A_[:], T_, T_[:], ET, ET[:, 2, t0:t0 + 16].unsqueeze(2).to_broadcast([128, 16, 128]), ALU.mult)
                    for q4 in range(4):
                        p_ = pg[kq % 2]
                        kq += 1
                        pv = p_[:].rearrange("p (i t) -> p i t", t=4)
                        for tq in range(4):
                            tt_ = q4 * 4 + tq
                            b.mm(p_, pv[:, :, tq], A_, A_[:, tt_, :], B_, B_[:, tt_, :])
                        tg0 = tl * 128 + t0 + q4 * 4
                        b.cp("act", GT, GT[:, :, tg0:tg0 + 4], p_, pv)
            for i1 in range(128):
                u_ = Uc[i1 % NW]
                v_ = Vc[i1 % NW]
                b.load(u_, u_[:], uTb, uTb.t[i1])
                b.load(v_, v_[:], vbd, vbd.t[i1])
                ps_ = (pss + pg)[i1 % 4]
                for kc in range(8):
                    b.mm(ps_, ps_[:, 0:TG], u_, u_[:, kc * 128:(kc + 1) * 128], h2b, h2b[:, kc, :], start=(kc == 0), stop=(kc == 7))
                g_ = gel[i1 % 4]
                a_ = AT[i1 % 4]
                b.act(g_, g_[:], ps_, ps_[:, 0:TG], AF.Gelu_apprx_tanh)
                b.tt("pool" if i1 % 2 == 0 else "dve", a_, a_[:], g_, g_[:], GT, GT[:, i1, :], ALU.mult)
                for tl in range(TG // 128):
                    for half in range(2):
                        p_ = po[tl * 2 + half]
                        b.mm(p_, p_[:], a_, a_[:, tl * 128:(tl + 1) * 128], v_, v_[:, half * 512:(half + 1) * 512], start=(i1 == 0), stop=(i1 == 127))
            for tl in range(TG // 128):
                ts_ = slice(tl * 128, (tl + 1) * 128)
                for half in range(2):
                    b.cp("act", fo, fo[:, half * 512:(half + 1) * 512], po[tl * 2 + half], po[tl * 2 + half][:])
                for half in range(2):
                    p_ = pg[half]
                    for c4 in range(4):
                        c = half * 4 + c4
                        b.tr(p_, p_[:, c4 * 128:(c4 + 1) * 128], fo, fo[:, c * 128:(c + 1) * 128], ident)
                    for c4 in range(4):
                        c = half * 4 + c4
                        b.stt(xn, xn[:, c, ts_], p_, p_[:, c4 * 128:(c4 + 1) * 128], modT[:, l, 40 + c, v:v + 1], xn, xn[:, c, ts_], ALU.mult, ALU.add, extra=[modT])
            b.load(xT, xview(xT, g0, TG), xn, xn[:])


_CACHE = {}


def _perm_qk():
    idx = np.zeros(256, np.int64)
    for s in range(2):
        for h in range(4):
            for j in range(32):
                idx[s * 128 + h * 32 + j] = h * 64 + s * 32 + j
    return idx


def kernel(x_prompt, x_sample, state_ret, state_gdn, c, c_ctx, ada_w, ada_b, norm_mix_g, norm_ffn_g,
           final_norm_g, ev_w_in, ev_w_out, ret_gamma_logit, ret_norm_g, sc_conv_w, od_w_in, od_w_out,
           gdn_conv_w, gdn_a_log, gdn_dt_bias, gdn_norm_g, cf_dw_w, cf_dw_b, cf_ln_g, cf_ln_b,
           peer_wq, peer_keys, peer_u, peer_v):
    f = lambda a: np.ascontiguousarray(np.asarray(a, dtype=np.float32))
    x_prompt, x_sample = f(x_prompt), f(x_sample)
    shared = {}
    shared["ada_w"] = f(ada_w)
    shared["ada_bT"] = f(np.asarray(ada_b).reshape(4, 48, 128).transpose(2, 0, 1))
    shared["gmixT"] = f(np.asarray(norm_mix_g).reshape(4, 8, 128).transpose(2, 0, 1))
    shared["gffnT"] = f(np.asarray(norm_ffn_g).reshape(4, 8, 128).transpose(2, 0, 1))
    shared["gfinT"] = f(np.asarray(final_norm_g).reshape(8, 128).T)
    pi = _perm_qk()
    ew = np.array(ev_w_in, dtype=np.float32)
    ew[:, :, 0:256] = np.asarray(ev_w_in)[:, :, 0:256][:, :, pi]
    ew[:, :, 256:512] = np.asarray(ev_w_in)[:, :, 256:512][:, :, pi]
    shared["ev_w_in"] = f(ew)
    shared["ev_w_out"] = f(ev_w_out)
    shared["lgam"] = f(np.asarray(ret_gamma_logit).reshape(2, 8))
    shared["retgT"] = f(np.asarray(ret_norm_g).transpose(2, 0, 1))
    shared["scwT"] = f(np.asarray(sc_conv_w).reshape(2, 3, 4, 128).transpose(3, 0, 2, 1))
    shared["od_w_in"] = f(od_w_in)
    shared["od_w_out"] = f(od_w_out)
    shared["gcwT"] = f(np.asarray(gdn_conv_w).reshape(2, 3, 12, 128).transpose(3, 0, 2, 1))
    shared["alog"] = f(np.asarray(gdn_a_log).reshape(2, 8))
    shared["dtb"] = f(np.asarray(gdn_dt_bias).reshape(2, 8))
    shared["gdngT"] = f(np.asarray(gdn_norm_g).T)
    shared["cfwT"] = f(np.asarray(cf_dw_w).reshape(2, 31, 4, 128).transpose(3, 0, 2, 1))
    shared["cfbT"] = f(np.asarray(cf_dw_b).reshape(2, 4, 128).transpose(2, 0, 1))
    shared["lngT"] = f(np.asarray(cf_ln_g).reshape(2, 4, 128).transpose(2, 0, 1))
    shared["lnbT"] = f(np.asarray(cf_ln_b).reshape(2, 4, 128).transpose(2, 0, 1))
    shared["peer_wq"] = f(peer_wq)
    shared["keysT"] = f(np.asarray(peer_keys).reshape(4, 16, 128, 128).transpose(0, 1, 3, 2))
    shared["peer_u"] = f(np.asarray(peer_u).reshape(4, 128, 128, 8, 128).transpose(0, 1, 4, 3, 2)).reshape(4, 128, 128, 1024)
    shared["peer_v"] = f(peer_v).reshape(4, 128, 128, 1024)
    for k, v_ in host_consts().items():
        shared["c_" + k] = f(v_)
    in_maps = []
    cc = np.asarray(c, dtype=np.float32)
    cx = np.asarray(c_ctx, dtype=np.float32)
    for r in range(NCORES):
        m = dict(shared)
        xs = np.concatenate([x_prompt[4 * r:4 * r + 4].reshape(NPR, D), x_sample[r]], axis=0)
        m["xT0"] = f(xs.T.reshape(8, 128, NT))
        cv = np.stack([cx, cc[r]], axis=0)
        m["cT"] = f(cv.reshape(2, 8, 128).transpose(2, 1, 0))
        m["sret"] = f(np.asarray(state_ret)[r])
        m["sgdn"] = f(np.asarray(state_gdn)[r])
        in_maps.append(m)
    if "nc" not in _CACHE:
        _CACHE["nc"] = build()
    res = run_bass_kernel_spmd(_CACHE["nc"], in_maps, core_ids=list(range(NCORES)))
    y_prompt = np.zeros((32, 256, D), np.float32)
    y_sample = np.zeros((8, 2048, D), np.float32)
    nret = np.zeros((32, 2, 2, 4, 64, 128), np.float32)
    ngdn = np.zeros((32, 2, 2, 4, 128, 128), np.float32)
    if STOP:
        _CACHE["pT"] = [np.asarray(res.results[r]["pT"]) for r in range(NCORES)]
        _CACHE["yTd"] = [np.asarray(res.results[r]["yTd"]) for r in range(NCORES)]
        _CACHE["dmod"] = [np.asarray(res.results[r]["dmod"]) for r in range(NCORES)]
        _CACHE["dbg"] = [np.asarray(res.results[r]["dbg"]).reshape(D, NT).T for r in range(NCORES)]
    for r in range(NCORES):
        o = res.results[r]
        yt = np.asarray(o["yT"]).reshape(D, NT).T
        y_prompt[4 * r:4 * r + 4] = yt[:NPR].reshape(4, 256, D)
        y_sample[r] = yt[NPR:]
        nret[4 * r:4 * r + 4] = np.asarray(o["nsr"])
        ngdn[4 * r:4 * r + 4] = np.asarray(o["nsg"])
    return (y_prompt, y_sample, nret, ngdn)
```

```python
import os
import numpy as np
from contextlib import ExitStack, contextmanager
import concourse.bass as bass
import concourse.mybir as mybir
from concourse.bass_utils import run_bass_kernel_spmd

F32 = mybir.dt.float32
BF16 = mybir.dt.bfloat16
U32 = mybir.dt.uint32
I32 = mybir.dt.int32
ALU = mybir.AluOpType
AF = mybir.ActivationFunctionType
AX = mybir.AxisListType

NCORES = 8
D = 1024
NT = 3072
NPR = 1024
LP = 256
LS = 2048
EPS = 1e-6
SEQS = [(0, 256, False), (256, 256, False), (512, 256, False), (768, 256, False), (1024, 2048, True)]
STOP = os.environ.get("KSTOP", "")
SKIPG = os.environ.get("KSKIPG", "") == "1"


class Res:
    __slots__ = ("w", "rs")

    def __init__(self):
        self.w = None
        self.rs = []


class T:
    def __init__(self, t, res=None):
        self.t = t
        self.res = res if res is not None else Res()

    def __getitem__(self, k):
        return self.t[k]


class Builder:
    EPOCH = 16000
    NDMA = 24

    def __init__(self, nc, es):
        self.nc = nc
        self.es = es
        self.E = {"pe": nc.tensor, "act": nc.scalar, "dve": nc.vector, "pool": nc.gpsimd, "sp": nc.sync}
        self.cur = {}
        self.cnt = {}
        self.nsem = 0
        for e in self.E:
            self._newsem(e)
        self.seen = {e: {} for e in self.E}
        self.dsem = [es.enter_context(nc.semaphore("dq%d" % i)) for i in range(self.NDMA)]
        self.duse = [0] * self.NDMA
        self.dk = 0
        self.uid = 0

    def _newsem(self, e):
        self.nsem += 1
        self.cur[e] = self.es.enter_context(self.nc.semaphore("s_%s_%d" % (e, self.nsem)))
        self.cnt[e] = 0

    def name(self, p):
        self.uid += 1
        return "%s_%d" % (p, self.uid)

    def sb(self, es, shape, dt=F32, name="t"):
        return T(es.enter_context(self.nc.sbuf_tensor(self.name(name), list(shape), dt)))

    def ps(self, es, shape, dt=F32, name="p"):
        return T(es.enter_context(self.nc.psum_tensor(self.name(name), list(shape), dt)))

    def dram(self, name, shape, dt, kind):
        return T(self.nc.dram_tensor(name, list(shape), dt, kind=kind).ap())

    def _deps(self, eng, r, w):
        deps = {}

        def add(ev):
            if ev is None:
                return
            s, v, src = ev
            if eng == "pe" and src == "pe":
                return
            k = id(s)
            if k not in deps or deps[k][1] < v:
                deps[k] = (s, v)

        for x in r:
            add(x.res.w)
        for x in w:
            add(x.res.w)
            for ev in x.res.rs:
                add(ev)
        E = self.E[eng]
        sn = self.seen[eng]
        for k, (s, v) in deps.items():
            if sn.get(k, 0) >= v:
                continue
            E.wait_ge(s, v)
            sn[k] = v

    def _mark(self, ev, r, w):
        for x in r:
            x.res.rs.append(ev)
        for x in w:
            x.res.w = ev
            x.res.rs = []

    def I(self, eng, f, r=(), w=()):
        self._deps(eng, r, w)
        inst = f()
        if self.cnt[eng] >= self.EPOCH:
            self._newsem(eng)
        self.cnt[eng] += 1
        ev = (self.cur[eng], self.cnt[eng], eng)
        inst.then_inc(ev[0], 1)
        self._mark(ev, r, w)
        return ev

    def DMA(self, eng, f, r=(), w=()):
        self._deps(eng, r, w)
        k = self.dk % self.NDMA
        self.dk += 1
        s = self.dsem[k]
        E = self.E[eng]
        if self.duse[k] > 0:
            key = id(s)
            if self.seen[eng].get(key, 0) < 16 * self.duse[k]:
                E.wait_ge(s, 16 * self.duse[k])
                self.seen[eng][key] = 16 * self.duse[k]
        self.duse[k] += 1
        inst = f()
        ev = (s, 16 * self.duse[k], "dma")
        inst.then_inc(s, 16)
        self._mark(ev, r, w)
        return ev

    def load(self, dst, dst_ap, src, src_ap, eng="sp"):
        return self.DMA(eng, lambda: self.E[eng].dma_start(out=dst_ap, in_=src_ap), r=[src], w=[dst])

    def tt(self, eng, out, o_ap, a, a_ap, b, b_ap, op):
        return self.I(eng, lambda: self.E[eng].tensor_tensor(out=o_ap, in0=a_ap, in1=b_ap, op=op), r=[a, b], w=[out])

    def ts(self, eng, out, o_ap, a, a_ap, s1, op0, s2=None, op1=None, extra=()):
        if op1 is None:
            f = lambda: self.E[eng].tensor_scalar(out=o_ap, in0=a_ap, scalar1=s1, scalar2=None, op0=op0)
        else:
            f = lambda: self.E[eng].tensor_scalar(out=o_ap, in0=a_ap, scalar1=s1, scalar2=s2, op0=op0, op1=op1)
        return self.I(eng, f, r=[a] + list(extra), w=[out])

    def stt(self, out, o_ap, a, a_ap, sc, b, b_ap, op0, op1, extra=(), accum=None, accum_t=None):
        if accum is None:
            f = lambda: self.nc.vector.scalar_tensor_tensor(out=o_ap, in0=a_ap, scalar=sc, in1=b_ap, op0=op0, op1=op1)
            w = [out]
        else:
            f = lambda: self.nc.vector.scalar_tensor_tensor(out=o_ap, in0=a_ap, scalar=sc, in1=b_ap, op0=op0, op1=op1, accum_out=accum)
            w = [out, accum_t]
        return self.I("dve", f, r=[a, b] + list(extra), w=w)

    def act(self, out, o_ap, a, a_ap, func, scale=1.0, bias=0.0, extra=()):
        return self.I("act", lambda: self.nc.scalar.activation(out=o_ap, in_=a_ap, func=func, bias=bias, scale=scale),
                      r=[a] + list(extra), w=[out])

    def cp(self, eng, out, o_ap, a, a_ap):
        if eng == "act":
            return self.I("act", lambda: self.nc.scalar.copy(out=o_ap, in_=a_ap), r=[a], w=[out])
        return self.I(eng, lambda: self.E[eng].tensor_copy(out=o_ap, in_=a_ap), r=[a], w=[out])

    def mm(self, out, o_ap, l, l_ap, rr, r_ap, start=True, stop=True):
        return self.I("pe", lambda: self.nc.tensor.matmul(o_ap, lhsT=l_ap, rhs=r_ap, start=start, stop=stop), r=[l, rr], w=[out])

    def tr(self, out, o_ap, a, a_ap, ident):
        return self.I("pe", lambda: self.nc.tensor.transpose(o_ap, a_ap, ident[:]), r=[a, ident], w=[out])

    def memset(self, eng, out, o_ap, v):
        return self.I(eng, lambda: self.E[eng].memset(o_ap, v), r=[], w=[out])

    def load_cast(self, stg, dst, dst_ap, src, src_ap, width, k):
        st = stg[k % len(stg)]
        self.load(st, st[:, 0:width], src, src_ap)
        self.cp("pool" if k % 2 == 0 else "act", dst, dst_ap, st, st[:, 0:width])

    def barrier(self):
        evs = [(self.cur[e], self.cnt[e]) for e in self.E if self.cnt[e] > 0]
        evs += [(self.dsem[k], 16 * self.duse[k]) for k in range(self.NDMA) if self.duse[k] > 0]
        for eng in self.E:
            E = self.E[eng]
            sn = self.seen[eng]
            for (s, v) in evs:
                if sn.get(id(s), 0) >= v:
                    continue
                E.wait_ge(s, v)
                sn[id(s)] = v

    @contextmanager
    def phase(self):
        with ExitStack() as ph:
            yield ph
            self.barrier()

    def final_wait(self, outs):
        for o in outs:
            self._deps("sp", [o], [])


def host_consts():
    c = {}
    c["ident"] = np.eye(128, dtype=np.float32)
    c["ones"] = np.ones((128, 128), np.float32)
    p = np.arange(128)
    hm = np.zeros((128, 4), np.float32)
    for h in range(4):
        hm[h * 32:(h + 1) * 32, h] = 0.125
    c["hm"] = hm
    t = np.arange(LS, dtype=np.float32)
    row = np.floor(t / 64.0)
    col = t - row * 64.0
    nf = 16
    freqs = (10000.0 ** (-np.arange(nf, dtype=np.float32) / nf)).astype(np.float32)
    ang = np.concatenate([row[:, None] * freqs, col[:, None] * freqs], axis=-1).astype(np.float32)
    c["cos"] = np.tile(np.cos(ang).T.astype(np.float32), (4, 1))
    c["sin"] = np.tile(np.sin(ang).T.astype(np.float32), (4, 1))
    m = np.arange(3968)
    c["dtab"] = (m[None, :] - 1920 - p[:, None]).astype(np.float32)
    c["posf"] = np.tile((t + 1.0)[None, :], (128, 1)).astype(np.float32)
    c["posb"] = np.tile((LS - t)[None, :], (128, 1)).astype(np.float32)
    posst = np.zeros((128, 2, 2), np.float32)
    for jt in range(2):
        posst[:, jt, 0] = 255 - (jt * 128 + p)
        posst[:, jt, 1] = jt * 128 + p
    c["posst"] = posst
    c["Mf"] = (p[:, None] <= p[None, :]).astype(np.float32)
    c["Mb"] = (p[:, None] >= p[None, :]).astype(np.float32)
    c["Sf"] = (p[:, None] < p[None, :]).astype(np.float32)
    c["Sb"] = (p[:, None] > p[None, :]).astype(np.float32)
    Elo = np.zeros((128, 7, 128), np.float32)
    for lv in range(7):
        sz = 1 << lv
        blk_i = p[:, None] // (2 * sz)
        blk_j = p[None, :] // (2 * sz)
        Elo[:, lv, :] = ((blk_i == blk_j) & ((p[:, None] % (2 * sz)) >= sz) & ((p[None, :] % (2 * sz)) < sz)).astype(np.float32)
    c["Elo"] = Elo
    c["Eup"] = np.ascontiguousarray(Elo.transpose(2, 1, 0))
    c["iota128"] = np.tile(np.arange(128, dtype=np.float32)[None, :], (128, 1))
    c["iota16"] = np.tile(np.arange(16, dtype=np.float32)[None, :], (128, 1))
    return c


CONST_SHAPES = {"ident": [128, 128], "ones": [128, 128], "hm": [128, 4], "cos": [128, LS], "sin": [128, LS],
                "dtab": [128, 3968], "posf": [128, LS], "posb": [128, LS], "posst": [128, 2, 2],
                "Mf": [128, 128], "Mb": [128, 128], "Sf": [128, 128], "Sb": [128, 128], "iota16": [128, 16],
                "Elo": [128, 7, 128], "Eup": [128, 7, 128], "iota128": [128, 128]}

IN_SHAPES = {
    "xT0": [8, 128, NT], "cT": [128, 8, 2], "ada_w": [4, 1024, 6144], "ada_bT": [128, 4, 48],
    "gmixT": [128, 4, 8], "gffnT": [128, 4, 8], "gfinT": [128, 8],
    "ev_w_in": [2, 1024, 3072], "ev_w_out": [2, 1024, 1024], "lgam": [2, 8], "retgT": [128, 2, 4],
    "scwT": [128, 2, 4, 3], "od_w_in": [2, 1024, 3088], "od_w_out": [2, 1024, 1024], "gcwT": [128, 2, 12, 3],
    "alog": [2, 8], "dtb": [2, 8], "gdngT": [128, 2], "cfwT": [128, 2, 4, 31], "cfbT": [128, 2, 4],
    "lngT": [128, 2, 4], "lnbT": [128, 2, 4], "peer_wq": [4, 1024, 2048], "keysT": [4, 16, 128, 128],
    "peer_u": [4, 128, 128, 1024], "peer_v": [4, 128, 128, 1024],
    "sret": [2, 2, 4, 64, 128], "sgdn": [2, 2, 4, 128, 128],
}


def build():
    nc = bass.Bass("TRN2", target_bir_lowering=False)
    es0 = ExitStack()
    b = Builder(nc, es0)
    dr = {}
    for k, s in IN_SHAPES.items():
        dr[k] = b.dram(k, s, F32, "ExternalInput")
    for k, s in CONST_SHAPES.items():
        dr[k] = b.dram("c_" + k, s, F32, "ExternalInput")
    yT = b.dram("yT", [8, 128, NT], F32, "ExternalOutput")
    nsr = b.dram("nsr", [4, 2, 2, 4, 64, 128], F32, "ExternalOutput")
    nsg = b.dram("nsg", [4, 2, 2, 4, 128, 128], F32, "ExternalOutput")
    xT = b.dram("xT", [8, 128, NT], F32, "Internal")
    pT = b.dram("pT", [24, 128, NT], F32, "ExternalOutput" if STOP else "Internal")
    vtok = b.dram("vtok", [NT, 512], BF16, "Internal")
    ktok = b.dram("ktok", [NPR, 256], BF16, "Internal")
    qkr = b.dram("qkr", [4, 128, NT], BF16, "Internal")
    yTd = b.dram("yTd", [8, 128, NT], BF16, "ExternalOutput" if STOP else "Internal")
    dmod = b.dram("dmod", [128, 4 * 48 * 2], F32, "ExternalOutput") if STOP else None
    gbtok = b.dram("gbtok", [NT, 16], F32, "Internal")
    qkvT = b.dram("qkvT", [12, 128, NT], F32, "Internal")
    ktok32 = b.dram("ktok32", [NT, 512], F32, "Internal")
    vtok32 = b.dram("vtok32", [NT, 512], F32, "Internal")
    oTd = b.dram("oTd", [2, 4, 128, NT], F32, "Internal")
    uTb = b.dram("uTb", [128, 128, 1024], BF16, "Internal")
    vbd = b.dram("vbd", [128, 128, 1024], BF16, "Internal")
    ekg = b.dram("ekg", [3, NT, 128], F32, "Internal")
    h2bd = b.dram("h2bd", [8, 128, NT], BF16, "Internal")
    scr = (uTb, vbd, ekg, h2bd)
    outs_written = []

    es = es0
    ident = b.sb(es, [128, 128], F32, "ident")
    ones = b.sb(es, [128, 128], F32, "ones")
    epsc = b.sb(es, [128, 1], F32, "epsc")
    onec = b.sb(es, [128, 1], F32, "onec")
    b.load(ident, ident[:], dr["ident"], dr["ident"].t)
    b.load(ones, ones[:], dr["ones"], dr["ones"].t)
    b.memset("dve", epsc, epsc[:], EPS)
    b.memset("dve", onec, onec[:], 1.0)
    modT = b.sb(es, [128, 4, 48, 2], F32, "modT")
    A1 = b.sb(es, [128, 4, 8, 2], F32, "A1")
    A2 = b.sb(es, [128, 4, 8, 2], F32, "A2")
    gfin = b.sb(es, [128, 8], F32, "gfin")
    b.load(gfin, gfin[:], dr["gfinT"], dr["gfinT"].t)

    with b.phase() as ph:
        cT = b.sb(ph, [128, 8, 2], F32, "cT")
        scT = b.sb(ph, [128, 8, 2], F32, "scT")
        abT = b.sb(ph, [128, 4, 48], F32, "abT")
        gmx = b.sb(ph, [128, 4, 8], F32, "gmx")
        gff = b.sb(ph, [128, 4, 8], F32, "gff")
        tmpm = b.sb(ph, [128, 4, 8, 2], F32, "tmpm")
        wts = [b.sb(ph, [128, 8, 768], F32, "adaw") for _ in range(4)]
        pm = b.ps(ph, [128, 512], F32, "pm")
        b.load(cT, cT[:], dr["cT"], dr["cT"].t)
        b.load(abT, abT[:], dr["ada_bT"], dr["ada_bT"].t)
        b.load(gmx, gmx[:], dr["gmixT"], dr["gmixT"].t)
        b.load(gff, gff[:], dr["gffnT"], dr["gffnT"].t)
        b.act(scT, scT[:], cT, cT[:], AF.Silu)
        it = 0
        for l in range(4):
            for grp in range(8):
                wt = wts[it % 4]
                it += 1
                src = dr["ada_w"].t[l, :, grp * 768:(grp + 1) * 768].rearrange("(kc p) n -> p kc n", p=128)
                b.load(wt, wt[:], dr["ada_w"], src)
                for cc in range(6):
                    for kc in range(8):
                        b.mm(pm, pm[:, cc * 2:cc * 2 + 2], wt, wt[:, kc, cc * 128:(cc + 1) * 128], scT, scT[:, kc, :],
                             start=(kc == 0), stop=(kc == 7))
                for cc in range(6):
                    ch = grp * 6 + cc
                    b.ts("dve", modT, modT[:, l, ch, :], pm, pm[:, cc * 2:cc * 2 + 2], abT[:, l, ch:ch + 1], ALU.add, extra=[abT])
        b.ts("dve", tmpm, tmpm[:], modT, modT[:, :, 8:16, :], 1.0, ALU.add)
        b.tt("dve", A1, A1[:], tmpm, tmpm[:], gmx, gmx[:].unsqueeze(3).to_broadcast([128, 4, 8, 2]), ALU.mult)
        b.ts("dve", tmpm, tmpm[:], modT, modT[:, :, 32:40, :], 1.0, ALU.add)
        b.tt("dve", A2, A2[:], tmpm, tmpm[:], gff, gff[:].unsqueeze(3).to_broadcast([128, 4, 8, 2]), ALU.mult)

    def norm_mod(ph, xg, W, Acol, Bcol, hb=None, hf=None, tmp=None, sq=None, ps_=None, rstd=None):
        b.act(sq, sq[:, :, 0:W], xg, xg[:, :, 0:W], AF.Square)
        for c in range(8):
            b.mm(ps_, ps_[:, 0:W], ones, ones[:], sq, sq[:, c, 0:W], start=(c == 0), stop=(c == 7))
        b.act(rstd, rstd[:, 0:W], ps_, ps_[:, 0:W], AF.Ln, scale=1.0 / D, bias=epsc[:, 0:1], extra=[epsc])
        b.act(rstd, rstd[:, 0:W], rstd, rstd[:, 0:W], AF.Exp, scale=-0.5)
        for c in range(8):
            b.tt("dve", tmp, tmp[:, 0:W], xg, xg[:, c, 0:W], rstd, rstd[:, 0:W], ALU.mult)
            tgt = hf if hf is not None else hb
            if Bcol is not None:
                b.ts("dve", tgt, tgt[:, c, 0:W], tmp, tmp[:, 0:W], Acol(c), ALU.mult, Bcol(c), ALU.add, extra=[A1, A2, modT, gfin])
            else:
                b.ts("dve", tgt, tgt[:, c, 0:W], tmp, tmp[:, 0:W], Acol(c), ALU.mult, extra=[A1, A2, modT, gfin])
            if hf is not None and hb is not None:
                b.cp("pool", hb, hb[:, c, 0:W], hf, hf[:, c, 0:W])

    def xview(dt_, lo, W):
        return dt_.t[:, :, lo:lo + W].rearrange("c p t -> p c t")

    with b.phase() as ph:
        xb = [b.sb(ph, [128, 8, 512], F32, "xcp") for _ in range(2)]
        for tg in range(6):
            x_ = xb[tg % 2]
            b.load(x_, x_[:], dr["xT0"], xview(dr["xT0"], tg * 512, 512))
            b.load(xT, xview(xT, tg * 512, 512), x_, x_[:])

    for l in range(4):
        i = l // 2
        even = (l % 2 == 0)
        if STOP == "M":
            break
        with b.phase() as ph:
            ncol = 3072 if even else 3088
            win = b.sb(ph, [128, 8, ncol], BF16, "win")
            wsrc = dr["ev_w_in" if even else "od_w_in"]
            cstg = [b.sb(ph, [128, 1024], F32, "cstg") for _ in range(3)]
            ck = 0
            for kc in range(8):
                for c0 in range(0, ncol, 1024):
                    c1 = min(ncol, c0 + 1024)
                    b.load_cast(cstg, win, win[:, kc, c0:c1], wsrc, wsrc.t[i, kc * 128:(kc + 1) * 128, c0:c1], c1 - c0, ck)
                    ck += 1
            xg = b.sb(ph, [128, 8, 512], F32, "xg")
            hT = b.sb(ph, [128, 8, 512], BF16, "hT")
            sq = b.sb(ph, [128, 8, 512], F32, "sq")
            tmp = b.sb(ph, [128, 512], F32, "tmp")
            rstd = b.sb(ph, [128, 512], F32, "rstd")
            psn = b.ps(ph, [128, 512], F32, "psn")
            pmm = [b.ps(ph, [128, 512], F32, "pmm") for _ in range(3)]
            stg = [b.sb(ph, [128, 512], F32, "stg") for _ in range(3)]
            stgb = [b.sb(ph, [128, 512], BF16, "stgb") for _ in range(2)]
            if even:
                fm_cols = [c * 128 for c in (0, 1, 2, 3)] + [1024 + c * 128 for c in range(16)]
                fm_dst = [0, 1, 2, 3] + list(range(8, 24))
            else:
                fm_cols = [c * 128 for c in range(16)] + [2064 + c * 128 for c in range(8)]
                fm_dst = list(range(24))
                nega = b.sb(ph, [128, 8], F32, "nega")
                dtbb = b.sb(ph, [128, 8], F32, "dtbb")
                gbs = b.sb(ph, [128, 16], F32, "gbs")
                b.load(nega, nega[:], dr["alog"], dr["alog"].t[i:i + 1, :].partition_broadcast(128))
                b.load(dtbb, dtbb[:], dr["dtb"], dr["dtb"].t[i:i + 1, :].partition_broadcast(128))
                b.act(nega, nega[:], nega, nega[:], AF.Exp)
                b.ts("dve", nega, nega[:], nega, nega[:], -1.0, ALU.mult)
            k = 0
            for tg in range(6):
                v = 0 if tg < 2 else 1
                b.load(xg, xg[:], xT, xview(xT, tg * 512, 512))
                norm_mod(ph, xg, 512, lambda c: A1[:, l, c, v:v + 1], lambda c: modT[:, l, c, v:v + 1], hb=hT, tmp=tmp, sq=sq, ps_=psn, rstd=rstd)
                for cc, dc in zip(fm_cols, fm_dst):
                    p_ = pmm[k % 3]
                    s_ = stg[k % 3]
                    k += 1
                    for kc in range(8):
                        b.mm(p_, p_[:], win, win[:, kc, cc:cc + 128], hT, hT[:, kc, :], start=(kc == 0), stop=(kc == 7))
                    b.cp("act", s_, s_[:], p_, p_[:])
                    b.load(pT, pT.t[dc, :, tg * 512:(tg + 1) * 512], s_, s_[:])
                for tt_ in range(4):
                    tok0 = tg * 512 + tt_ * 128
                    if even:
                        p_ = pmm[k % 3]
                        sb_ = stgb[k % 2]
                        k += 1
                        for kc in range(8):
                            b.mm(p_, p_[:], hT, hT[:, kc, tt_ * 128:(tt_ + 1) * 128], win, win[:, kc, 512:1024], start=(kc == 0), stop=(kc == 7))
                        b.cp("act", sb_, sb_[:], p_, p_[:])
                        b.load(vtok, vtok.t[tok0:tok0 + 128, :], sb_, sb_[:])
                        if tg < 2:
                            p_ = pmm[k % 3]
                            sb_ = stgb[k % 2]
                            k += 1
                            for kc in range(8):
                                b.mm(p_, p_[:, 0:256], hT, hT[:, kc, tt_ * 128:(tt_ + 1) * 128], win, win[:, kc, 256:512], start=(kc == 0), stop=(kc == 7))
                            b.cp("act", sb_, sb_[:, 0:256], p_, p_[:, 0:256])
                            b.load(ktok, ktok.t[tok0:tok0 + 128, :], sb_, sb_[:, 0:256])
                    else:
                        p_ = pmm[k % 3]
                        k += 1
                        for kc in range(8):
                            b.mm(p_, p_[:, 0:16], hT, hT[:, kc, tt_ * 128:(tt_ + 1) * 128], win, win[:, kc, 2048:2064], start=(kc == 0), stop=(kc == 7))
                        b.tt("dve", gbs, gbs[:, 0:8], p_, p_[:, 0:8], dtbb, dtbb[:], ALU.add)
                        b.act(gbs, gbs[:, 0:8], gbs, gbs[:, 0:8], AF.Exp)
                        b.act(gbs, gbs[:, 0:8], gbs, gbs[:, 0:8], AF.Ln, bias=onec[:, 0:1], extra=[onec])
                        b.tt("dve", gbs, gbs[:, 0:8], gbs, gbs[:, 0:8], nega, nega[:], ALU.mult)
                        b.act(gbs, gbs[:, 8:16], p_, p_[:, 8:16], AF.Sigmoid)
                        b.load(gbtok, gbtok.t[tok0:tok0 + 128, :], gbs, gbs[:])
        if STOP == "P2":
            break
        if even:
            even_mixer(b, nc, dr, i, pT, vtok, ktok, qkr, yTd, nsr, ident, ones, epsc, outs_written)
        else:
            odd_mixer(b, nc, dr, i, pT, gbtok, qkvT, ktok32, vtok32, oTd, yTd, nsg, ident, ones, epsc, onec, outs_written)
        peer_phase(b, nc, dr, l, i, even, xT, yTd, modT, A2, ident, ones, epsc, norm_mod, xview, do_peer=(STOP != "%da" % l), scr=scr)
        if STOP in ("%da" % l, "%db" % l):
            break

    with b.phase() as ph:
        xg = b.sb(ph, [128, 8, 512], F32, "xg")
        hf = b.sb(ph, [128, 8, 512], F32, "hf")
        if STOP:
            dbg = b.dram("dbg", [8, 128, NT], F32, "ExternalOutput")
            for tg in range(6):
                b.load(xg, xg[:], xT, xview(xT, tg * 512, 512))
                b.load(dbg, xview(dbg, tg * 512, 512), xg, xg[:])
            b.load(dmod, dmod.t, modT, modT[:].rearrange("p a b c -> p (a b c)"))
            b.final_wait([dbg, dmod, pT, yTd])
        sq = b.sb(ph, [128, 8, 512], F32, "sq")
        tmp = b.sb(ph, [128, 512], F32, "tmp")
        rstd = b.sb(ph, [128, 512], F32, "rstd")
        psn = b.ps(ph, [128, 512], F32, "psn")
        for tg in range(6):
            b.load(xg, xg[:], xT, xview(xT, tg * 512, 512))
            norm_mod(ph, xg, 512, lambda c: gfin[:, c:c + 1], None, hf=hf, tmp=tmp, sq=sq, ps_=psn, rstd=rstd)
            b.load(yT, xview(yT, tg * 512, 512), hf, hf[:])
    b.final_wait([yT, nsr, nsg])
    es0.close()
    return nc


def even_mixer(b, nc, dr, i, pT, vtok, ktok, qkr, yTd, nsr, ident, ones, epsc, outs_written):
    with b.phase() as ph:
        x1 = b.sb(ph, [128, 512], F32, "x1")
        x2 = b.sb(ph, [128, 512], F32, "x2")
        cs = b.sb(ph, [128, LS], F32, "cos")
        sn = b.sb(ph, [128, LS], F32, "sin")
        t1 = b.sb(ph, [128, 512], F32, "t1")
        t2 = b.sb(ph, [128, 512], F32, "t2")
        o1 = b.sb(ph, [128, 512], BF16, "o1")
        o2 = b.sb(ph, [128, 512], BF16, "o2")
        b.load(cs, cs[:], dr["cos"], dr["cos"].t)
        b.load(sn, sn[:], dr["sin"], dr["sin"].t)
        for tg in range(6):
            lat = tg >= 2
            pos0 = (tg - 2) * 512
            for qk in range(2):
                b.load(x1, x1[:], pT, pT.t[qk * 2, :, tg * 512:(tg + 1) * 512])
                b.load(x2, x2[:], pT, pT.t[qk * 2 + 1, :, tg * 512:(tg + 1) * 512])
                if lat:
                    c_ = cs[:, pos0:pos0 + 512]
                    s_ = sn[:, pos0:pos0 + 512]
                    b.tt("dve", t1, t1[:], x1, x1[:], cs, c_, ALU.mult)
                    b.tt("pool", t2, t2[:], x2, x2[:], sn, s_, ALU.mult)
                    b.tt("dve", o1, o1[:], t1, t1[:], t2, t2[:], ALU.subtract)
                    b.tt("dve", t1, t1[:], x2, x2[:], cs, c_, ALU.mult)
                    b.tt("pool", t2, t2[:], x1, x1[:], sn, s_, ALU.mult)
                    b.tt("dve", o2, o2[:], t1, t1[:], t2, t2[:], ALU.add)
                else:
                    b.cp("dve", o1, o1[:], x1, x1[:])
                    b.cp("pool", o2, o2[:], x2, x2[:])
                b.load(qkr, qkr.t[qk * 2, :, tg * 512:(tg + 1) * 512], o1, o1[:])
                b.load(qkr, qkr.t[qk * 2 + 1, :, tg * 512:(tg + 1) * 512], o2, o2[:])
    with b.phase() as ph:
        lgt = b.sb(ph, [128, 8], F32, "lgt")
        nlg = b.sb(ph, [128, 8], F32, "nlg")
        hm = b.sb(ph, [128, 4], F32, "hm")
        retg = b.sb(ph, [128, 2, 4], F32, "retg")
        dtab = b.sb(ph, [128, 3968], F32, "dtab")
        ta = b.sb(ph, [128, 3968], F32, "ta")
        tb = b.sb(ph, [128, 3968], F32, "tb")
        Th = b.sb(ph, [128, 3968], F32, "Th")
        posf = b.sb(ph, [128, LS], F32, "posf")
        posb = b.sb(ph, [128, LS], F32, "posb")
        wfr = b.sb(ph, [128, LS], F32, "wfr")
        wbr = b.sb(ph, [128, LS], F32, "wbr")
        posst = b.sb(ph, [128, 2, 2], F32, "posst")
        wst = b.sb(ph, [128, 2, 2], F32, "wst")
        Q = b.sb(ph, [128, 2, LS], BF16, "Q")
        K = b.sb(ph, [128, 2, LS], BF16, "K")
        Kh = b.sb(ph, [128, 2, LS], BF16, "Kh")
        V = b.sb(ph, [128, 16, 128], BF16, "V")
        Vs = b.sb(ph, [128, 2, 2, 128], BF16, "Vs")
        kt = b.sb(ph, [128, 2, 256], BF16, "kt")
        S0 = b.sb(ph, [128, 2, 2, 128], BF16, "S0")
        S0f = b.sb(ph, [128, 2, 2, 128], F32, "S0f")
        Sm = [b.sb(ph, [128, 512], BF16, "Sm") for _ in range(2)]
        pst = [b.ps(ph, [128, 512], F32, "pst") for _ in range(2)]
        po = b.ps(ph, [128, 512], F32, "po")
        pc = [b.ps(ph, [128, 512], F32, "pc") for _ in range(2)]
        pn = b.ps(ph, [128, 512], F32, "pn")
        pss = b.ps(ph, [128, 512], F32, "pss")
        o = b.sb(ph, [128, 512], F32, "o")
        osq = b.sb(ph, [128, 512], F32, "osq")
        rs = b.sb(ph, [128, 512], F32, "rs")
        tq = b.sb(ph, [128, 512], F32, "tq")
        gch = b.sb(ph, [128, 512], F32, "gch")
        ya = b.sb(ph, [128, 512], BF16, "ya")
        sts = b.sb(ph, [128, 128], F32, "sts")
        b.load(lgt, lgt[:], dr["lgam"], dr["lgam"].t[i:i + 1, :].partition_broadcast(128))
        b.load(hm, hm[:], dr["hm"], dr["hm"].t)
        b.load(retg, retg[:], dr["retgT"], dr["retgT"].t)
        b.load(dtab, dtab[:], dr["dtab"], dr["dtab"].t)
        b.load(posf, posf[:], dr["posf"], dr["posf"].t)
        b.load(posb, posb[:], dr["posb"], dr["posb"].t)
        b.load(posst, posst[:], dr["posst"], dr["posst"].t)
        b.act(nlg, nlg[:], lgt, lgt[:], AF.Exp, scale=-1.0)
        b.ts("dve", nlg, nlg[:], nlg, nlg[:], 1.0, ALU.add)
        b.act(nlg, nlg[:], nlg, nlg[:], AF.Ln)
        b.ts("dve", lgt, lgt[:], nlg, nlg[:], -1.0, ALU.mult)
        for h in range(4):
            lf = lgt[:, h:h + 1]
            lb = lgt[:, 4 + h:5 + h]
            nlb = nlg[:, 4 + h:5 + h]
            b.ts("dve", ta, ta[:], dtab, dtab[:], 0.0, ALU.max)
            b.act(ta, ta[:], ta, ta[:], AF.Exp, scale=lf, extra=[lgt])
            b.ts("dve", tb, tb[:], dtab, dtab[:], 0.0, ALU.is_ge)
            b.tt("dve", Th, Th[:], ta, ta[:], tb, tb[:], ALU.mult)
            b.ts("dve", ta, ta[:], dtab, dtab[:], 0.0, ALU.min)
            b.act(ta, ta[:], ta, ta[:], AF.Exp, scale=nlb, extra=[nlg])
            b.ts("dve", tb, tb[:], dtab, dtab[:], 0.0, ALU.is_le)
            b.tt("dve", ta, ta[:], ta, ta[:], tb, tb[:], ALU.mult)
            b.tt("dve", Th, Th[:], Th, Th[:], ta, ta[:], ALU.add)
            b.act(wfr, wfr[:], posf, posf[:], AF.Exp, scale=lf, extra=[lgt])
            b.act(wbr, wbr[:], posb, posb[:], AF.Exp, scale=lb, extra=[lgt])
            b.act(wst, wst[:, :, 0], posst, posst[:, :, 0], AF.Exp, scale=lf, extra=[lgt])
            b.act(wst, wst[:, :, 1], posst, posst[:, :, 1], AF.Exp, scale=lb, extra=[lgt])
            b.ts("dve", wst, wst[:], wst, wst[:], 0.125, ALU.mult)
            b.memset("pool", S0f, S0f[:], 0.0)
            for d in range(2):
                for s in range(2):
                    b.load(S0f, S0f[h * 32:(h + 1) * 32, d, s, :], dr["sret"], dr["sret"].t[i, d, h, s * 32:(s + 1) * 32, :])
            b.cp("pool", S0, S0[:], S0f, S0f[:])
            for si, (off, L, lat) in enumerate(SEQS):
                nj = L // 128
                IG = min(512, L)
                b.load(Q, Q[:, :, 0:L], qkr, qkr.t[0:2, :, off:off + L].rearrange("c p t -> p c t"))
                b.load(K, K[:, :, 0:L], qkr, qkr.t[2:4, :, off:off + L].rearrange("c p t -> p c t"))
                b.load(V, V[:, 0:nj, :], vtok, vtok.t[off:off + L, h * 128:(h + 1) * 128].rearrange("(j p) e -> p j e", p=128))
                b.ts("dve", Kh, Kh[:, :, 0:L], K, K[:, :, 0:L], hm[:, h:h + 1], ALU.mult, extra=[hm])
                it = 0
                for ig in range(L // IG):
                    i0 = ig * IG
                    for jt in range(nj):
                        p_ = pst[it % 2]
                        s_ = Sm[it % 2]
                        it += 1
                        b.mm(p_, p_[:, 0:IG], Kh, Kh[:, 0, jt * 128:(jt + 1) * 128], Q, Q[:, 0, i0:i0 + IG], start=True, stop=False)
                        b.mm(p_, p_[:, 0:IG], Kh, Kh[:, 1, jt * 128:(jt + 1) * 128], Q, Q[:, 1, i0:i0 + IG], start=False, stop=True)
                        m0 = i0 - jt * 128 + 1920
                        b.tt("dve", s_, s_[:, 0:IG], p_, p_[:, 0:IG], Th, Th[:, m0:m0 + IG], ALU.mult)
                        b.mm(po, po[:, 0:IG], V, V[:, jt, :], s_, s_[:, 0:IG], start=(jt == 0), stop=(jt == nj - 1))
                    b.cp("act", o, o[:, 0:IG], po, po[:, 0:IG])
                    if lat:
                        for d, wr in ((0, wfr), (1, wbr)):
                            b.mm(pc[d], pc[d][:, 0:IG], S0, S0[:, d, 0, :], Q, Q[:, 0, i0:i0 + IG], start=True, stop=False)
                            b.mm(pc[d], pc[d][:, 0:IG], S0, S0[:, d, 1, :], Q, Q[:, 1, i0:i0 + IG], start=False, stop=True)
                            b.tt("dve", tq, tq[:, 0:IG], pc[d], pc[d][:, 0:IG], wr, wr[:, i0:i0 + IG], ALU.mult)
                            b.tt("dve", o, o[:, 0:IG], o, o[:, 0:IG], tq, tq[:, 0:IG], ALU.add)
                    b.act(osq, osq[:, 0:IG], o, o[:, 0:IG], AF.Square)
                    b.mm(pn, pn[:, 0:IG], ones, ones[:], osq, osq[:, 0:IG])
                    b.act(rs, rs[:, 0:IG], pn, pn[:, 0:IG], AF.Ln, scale=1.0 / 128.0, bias=epsc[:, 0:1], extra=[epsc])
                    b.act(rs, rs[:, 0:IG], rs, rs[:, 0:IG], AF.Exp, scale=-0.5)
                    b.load(gch, gch[:, 0:IG], pT, pT.t[8 + h, :, off + i0:off + i0 + IG])
                    b.act(gch, gch[:, 0:IG], gch, gch[:, 0:IG], AF.Silu)
                    b.tt("dve", o, o[:, 0:IG], o, o[:, 0:IG], rs, rs[:, 0:IG], ALU.mult)
                    b.stt(ya, ya[:, 0:IG], o, o[:, 0:IG], retg[:, i, h:h + 1], gch, gch[:, 0:IG], ALU.mult, ALU.mult, extra=[retg])
                    b.load(yTd, yTd.t[h, :, off + i0:off + i0 + IG], ya, ya[:, 0:IG])
                if not lat:
                    b.load(kt, kt[:], ktok, ktok.t[off:off + L, :].rearrange("(j p) c -> p j c", p=128))
                    for d in range(2):
                        for jt in range(2):
                            b.ts("dve", Vs, Vs[:, d, jt, :], V, V[:, jt, :], wst[:, jt, d:d + 1], ALU.mult, extra=[wst])
                    for d in range(2):
                        for s in range(2):
                            for jt in range(2):
                                b.mm(pss, pss[:, 0:128], kt, kt[:, jt, s * 128:(s + 1) * 128], Vs, Vs[:, d, jt, :], start=(jt == 0), stop=(jt == 1))
                            b.cp("act", sts, sts[:], pss, pss[:, 0:128])
                            b.load(nsr, nsr.t[si, i, d, h, s * 32:(s + 1) * 32, :], sts, sts[h * 32:(h + 1) * 32, :])
    with b.phase() as ph:
        scw = b.sb(ph, [128, 2, 4, 3], F32, "scw")
        b.load(scw, scw[:], dr["scwT"], dr["scwT"].t)
        bg = b.sb(ph, [128, LS], F32, "bg")
        cg = b.sb(ph, [128, LS], F32, "cg")
        hb_ = b.sb(ph, [128, LS], F32, "hb")
        up = b.sb(ph, [128, LS + 2], F32, "up")
        acc = b.sb(ph, [128, LS], F32, "acc")
        yb = b.sb(ph, [128, LS], BF16, "yb")
        for (off, L, lat) in SEQS:
            for cb in range(4):
                b.load(bg, bg[:, 0:L], pT, pT.t[12 + cb, :, off:off + L])
                b.load(cg, cg[:, 0:L], pT, pT.t[16 + cb, :, off:off + L])
                b.load(hb_, hb_[:, 0:L], pT, pT.t[20 + cb, :, off:off + L])
                b.memset("pool", up, up[:, 0:1], 0.0)
                b.memset("pool", up, up[:, L + 1:L + 2], 0.0)
                b.tt("pool", up, up[:, 1:L + 1], cg, cg[:, 0:L], hb_, hb_[:, 0:L], ALU.mult)
                b.ts("dve", acc, acc[:, 0:L], up, up[:, 0:L], scw[:, i, cb, 0:1], ALU.mult, extra=[scw])
                b.stt(acc, acc[:, 0:L], up, up[:, 1:L + 1], scw[:, i, cb, 1:2], acc, acc[:, 0:L], ALU.mult, ALU.add, extra=[scw])
                b.stt(acc, acc[:, 0:L], up, up[:, 2:L + 2], scw[:, i, cb, 2:3], acc, acc[:, 0:L], ALU.mult, ALU.add, extra=[scw])
                b.tt("dve", yb, yb[:, 0:L], acc, acc[:, 0:L], bg, bg[:, 0:L], ALU.mult)
                b.load(yTd, yTd.t[4 + cb, :, off:off + L], yb, yb[:, 0:L])


def odd_mixer(b, nc, dr, i, pT, gbtok, qkvT, ktok32, vtok32, oTd, yTd, nsg, ident, ones, epsc, onec, outs_written):
    with b.phase() as ph:
        gcw = b.sb(ph, [128, 2, 12, 3], F32, "gcw")
        b.load(gcw, gcw[:], dr["gcwT"], dr["gcwT"].t)
        xp = b.sb(ph, [128, LS + 2], F32, "xp")
        acc = b.sb(ph, [128, LS], F32, "acc")
        sq = b.sb(ph, [128, 512], F32, "sq")
        rs = b.sb(ph, [128, 512], F32, "rs")
        pn = b.ps(ph, [128, 512], F32, "pn")
        ptr = b.ps(ph, [128, 512], F32, "ptr")
        tk = b.sb(ph, [128, 512], F32, "tk")
        for (off, L, lat) in SEQS:
            for ch in range(12):
                b.memset("pool", xp, xp[:, 0:1], 0.0)
                b.memset("pool", xp, xp[:, L + 1:L + 2], 0.0)
                b.load(xp, xp[:, 1:L + 1], pT, pT.t[ch, :, off:off + L])
                b.ts("dve", acc, acc[:, 0:L], xp, xp[:, 0:L], gcw[:, i, ch, 0:1], ALU.mult, extra=[gcw])
                b.stt(acc, acc[:, 0:L], xp, xp[:, 1:L + 1], gcw[:, i, ch, 1:2], acc, acc[:, 0:L], ALU.mult, ALU.add, extra=[gcw])
                b.stt(acc, acc[:, 0:L], xp, xp[:, 2:L + 2], gcw[:, i, ch, 2:3], acc, acc[:, 0:L], ALU.mult, ALU.add, extra=[gcw])
                b.act(acc, acc[:, 0:L], acc, acc[:, 0:L], AF.Silu)
                W = min(512, L)
                for pc_ in range(L // W):
                    sl = slice(pc_ * W, (pc_ + 1) * W)
                    if ch < 8:
                        b.act(sq, sq[:, 0:W], acc, acc[:, sl], AF.Square)
                        b.mm(pn, pn[:, 0:W], ones, ones[:], sq, sq[:, 0:W])
                        b.act(rs, rs[:, 0:W], pn, pn[:, 0:W], AF.Ln, bias=epsc[:, 0:1], extra=[epsc])
                        b.act(rs, rs[:, 0:W], rs, rs[:, 0:W], AF.Exp, scale=-0.5)
                        if ch < 4:
                            b.stt(acc, acc[:, sl], acc, acc[:, sl], 128.0 ** -0.5, rs, rs[:, 0:W], ALU.mult, ALU.mult)
                        else:
                            b.tt("dve", acc, acc[:, sl], acc, acc[:, sl], rs, rs[:, 0:W], ALU.mult)
                    if ch >= 4:
                        dst = ktok32 if ch < 8 else vtok32
                        hh = ch % 4
                        nt_ = W // 128
                        for t_ in range(nt_):
                            b.tr(ptr, ptr[:, t_ * 128:(t_ + 1) * 128], acc, acc[:, pc_ * W + t_ * 128: pc_ * W + (t_ + 1) * 128], ident)
                        b.cp("act", tk, tk[:, 0:W], ptr, ptr[:, 0:W])
                        tok0 = off + pc_ * W
                        b.load(dst, dst.t[tok0:tok0 + W, hh * 128:(hh + 1) * 128].rearrange("(t p) e -> p t e", p=128),
                               tk, tk[:, 0:W].rearrange("p (t e) -> p t e", e=128))
                b.load(qkvT, qkvT.t[ch, :, off:off + L], acc, acc[:, 0:L])
    with b.phase() as ph:
        Mf = b.sb(ph, [128, 128], F32, "Mf")
        Mb = b.sb(ph, [128, 128], F32, "Mb")
        Sf = b.sb(ph, [128, 128], F32, "Sf")
        Sb = b.sb(ph, [128, 128], F32, "Sb")
        for t_, k_ in ((Mf, "Mf"), (Mb, "Mb"), (Sf, "Sf"), (Sb, "Sb")):
            b.load(t_, t_[:], dr[k_], dr[k_].t)

        def t4(nm):
            return b.sb(ph, [128, 4, 128], F32, nm)
        qTt, kTt, ktk, vtk = t4("qTt"), t4("kTt"), t4("ktk"), t4("vtk")
        gb = b.sb(ph, [128, 16], F32, "gb")
        gc = b.sb(ph, [128, 4], F32, "gc")
        gl = b.sb(ph, [128, 4], F32, "gl")
        egl = b.sb(ph, [128, 4], F32, "egl")
        eg = b.sb(ph, [128, 4], F32, "eg")
        egd = b.sb(ph, [128, 4], F32, "egd")
        bge = b.sb(ph, [128, 4], F32, "bge")
        gB = [b.sb(ph, [128, 128], F32, "gB") for _ in range(2)]
        nd, dec, expbc, dincl, dstr = t4("nd"), t4("dec"), t4("expbc"), t4("dincl"), t4("dstr")
        A, AT, attn, attnT = t4("A"), t4("AT"), t4("attn"), t4("attnT")
        Pa, PTa, Pb, PTb, TT, Tm = t4("Pa"), t4("PTa"), t4("Pb"), t4("PTb"), t4("TT"), t4("Tm")
        Elo = b.sb(ph, [128, 7, 128], F32, "Elo")
        Eup = b.sb(ph, [128, 7, 128], F32, "Eup")
        b.load(Elo, Elo[:], dr["Elo"], dr["Elo"].t)
        b.load(Eup, Eup[:], dr["Eup"], dr["Eup"].t)
        kbg, vb, kd, u, wT, qgT, vnew, S, ost = t4("kbg"), t4("vb"), t4("kd"), t4("u"), t4("wT"), t4("qgT"), t4("vnew"), t4("S"), t4("ost")
        pS = b.ps(ph, [128, 512], F32, "pS")
        P_ = [b.ps(ph, [128, 512], F32, "pg") for _ in range(7)]

        def hs(t_, h):
            return t_[:, h, :]

        def ph_(p, h):
            return p[:, h * 128:(h + 1) * 128]

        def p3(p):
            return p[:].rearrange("p (h f) -> p h f", h=4)

        for si, (off, L, lat) in enumerate(SEQS):
            nt_ = L // 128
            for d in range(2):
                Mtri = Mf if d == 0 else Mb
                incl = Mb if d == 0 else Mf
                strict = Sb if d == 0 else Sf
                if lat:
                    b.load(S, S[:], dr["sgdn"], dr["sgdn"].t[i, d].rearrange("h k v -> k h v"))
                else:
                    b.memset("pool", S, S[:], 0.0)
                order = range(nt_) if d == 0 else range(nt_ - 1, -1, -1)
                for ti in order:
                    tok0 = off + ti * 128
                    b.load(qTt, qTt[:], qkvT, qkvT.t[0:4, :, tok0:tok0 + 128].rearrange("h p t -> p h t"))
                    b.load(kTt, kTt[:], qkvT, qkvT.t[4:8, :, tok0:tok0 + 128].rearrange("h p t -> p h t"))
                    b.load(ktk, ktk[:], ktok32, ktok32.t[tok0:tok0 + 128, :].rearrange("p (h e) -> p h e", h=4))
                    b.load(vtk, vtk[:], vtok32, vtok32.t[tok0:tok0 + 128, :].rearrange("p (h e) -> p h e", h=4))
                    b.load(gb, gb[:], gbtok, gbtok.t[tok0:tok0 + 128, :])
                    gcol = gb[:, d * 4:d * 4 + 4]
                    bcol = gb[:, 8 + d * 4:8 + d * 4 + 4]
                    b.mm(pS, pS[:, 0:4], Mtri, Mtri[:], gb, gcol)
                    b.mm(pS, pS[:, 4:8], ones, ones[:], gb, gcol)
                    b.cp("dve", gc, gc[:], pS, pS[:, 0:4])
                    b.cp("dve", gl, gl[:], pS, pS[:, 4:8])
                    b.act(egl, egl[:], gl, gl[:], AF.Exp)
                    b.act(eg, eg[:], gc, gc[:], AF.Exp)
                    b.tt("dve", egd, egd[:], gl, gl[:], gc, gc[:], ALU.subtract)
                    b.act(egd, egd[:], egd, egd[:], AF.Exp)
                    b.tt("dve", bge, bge[:], eg, eg[:], gb, bcol, ALU.mult)
                    for h in range(4):
                        g_ = gB[h % 2]
                        b.ts("pool", g_, g_[:], ones, ones[:], gb[:, d * 4 + h:d * 4 + h + 1], ALU.mult, extra=[gb])
                        b.mm(P_[0], ph_(P_[0], h), g_, g_[:], Mtri, Mtri[:])
                    for h in range(4):
                        b.ts("dve", nd, hs(nd, h), P_[0], ph_(P_[0], h), gc[:, h:h + 1], ALU.subtract, 0.0, ALU.max, extra=[gc])
                    b.act(dec, dec[:], nd, nd[:], AF.Exp, scale=-1.0)
                    b.act(expbc, expbc[:], P_[0], p3(P_[0]), AF.Exp)
                    b.tt("pool", dincl, dincl[:], dec, dec[:], incl, incl[:].unsqueeze(1).to_broadcast([128, 4, 128]), ALU.mult)
                    b.tt("pool", dstr, dstr[:], dec, dec[:], strict, strict[:].unsqueeze(1).to_broadcast([128, 4, 128]), ALU.mult)
                    for h in range(4):
                        b.mm(P_[1], ph_(P_[1], h), kTt, hs(kTt, h), kTt, hs(kTt, h))
                        b.mm(P_[2], ph_(P_[2], h), qTt, hs(qTt, h), kTt, hs(kTt, h))
                    for h in range(4):
                        b.stt(A, hs(A, h), P_[1], ph_(P_[1], h), gb[:, 8 + d * 4 + h:8 + d * 4 + h + 1], dstr, hs(dstr, h), ALU.mult, ALU.mult, extra=[gb])
                    b.tt("dve", attn, attn[:], P_[2], p3(P_[2]), dincl, dincl[:], ALU.mult)
                    for h in range(4):
                        b.tr(P_[3], ph_(P_[3], h), A, hs(A, h), ident)
                        b.tr(P_[4], ph_(P_[4], h), attn, hs(attn, h), ident)
                    b.cp("act", AT, AT[:], P_[3], p3(P_[3]))
                    b.cp("act", attnT, attnT[:], P_[4], p3(P_[4]))
                    b.cp("pool", Tm, Tm[:], ident, ident[:].unsqueeze(1).to_broadcast([128, 4, 128]))
                    b.cp("pool", TT, TT[:], ident, ident[:].unsqueeze(1).to_broadcast([128, 4, 128]))
                    EA_t = Elo if d == 0 else Eup
                    EAT_t = Eup if d == 0 else Elo
                    for lv in range(7):
                        b.tt("pool", Pa, Pa[:], A, A[:], EA_t, EA_t[:, lv, :].unsqueeze(1).to_broadcast([128, 4, 128]), ALU.mult)
                        b.tt("pool", PTa, PTa[:], AT, AT[:], EAT_t, EAT_t[:, lv, :].unsqueeze(1).to_broadcast([128, 4, 128]), ALU.mult)
                        for h in range(4):
                            b.mm(P_[1], ph_(P_[1], h), PTa, hs(PTa, h), Tm, hs(Tm, h))
                            b.mm(P_[2], ph_(P_[2], h), Pa, hs(Pa, h), TT, hs(TT, h))
                        b.cp("act", Pb, Pb[:], P_[1], p3(P_[1]))
                        b.cp("dve", PTb, PTb[:], P_[2], p3(P_[2]))
                        for h in range(4):
                            b.mm(P_[0], ph_(P_[0], h), TT, hs(TT, h), Pb, hs(Pb, h))
                            b.mm(P_[3], ph_(P_[3], h), Tm, hs(Tm, h), PTb, hs(PTb, h))
                        b.tt("dve", Tm, Tm[:], Tm, Tm[:], P_[0], p3(P_[0]), ALU.subtract)
                        b.tt("dve", TT, TT[:], TT, TT[:], P_[3], p3(P_[3]), ALU.subtract)
                    b.tt("pool", kbg, kbg[:], ktk, ktk[:], bge, bge[:].unsqueeze(2).to_broadcast([128, 4, 128]), ALU.mult)
                    b.tt("pool", vb, vb[:], vtk, vtk[:], gb, bcol.unsqueeze(2).to_broadcast([128, 4, 128]), ALU.mult)
                    b.tt("pool", kd, kd[:], ktk, ktk[:], egd, egd[:].unsqueeze(2).to_broadcast([128, 4, 128]), ALU.mult)
                    b.tt("dve", qgT, qgT[:], qTt, qTt[:], expbc, expbc[:], ALU.mult)
                    for h in range(4):
                        b.mm(P_[3], ph_(P_[3], h), TT, hs(TT, h), vb, hs(vb, h))
                        b.mm(P_[4], ph_(P_[4], h), kbg, hs(kbg, h), TT, hs(TT, h))
                    b.cp("act", u, u[:], P_[3], p3(P_[3]))
                    b.cp("act", wT, wT[:], P_[4], p3(P_[4]))
                    for h in range(4):
                        b.mm(P_[5], ph_(P_[5], h), wT, hs(wT, h), S, hs(S, h))
                    b.tt("dve", vnew, vnew[:], u, u[:], P_[5], p3(P_[5]), ALU.subtract)
                    for h in range(4):
                        b.mm(P_[6], ph_(P_[6], h), S, hs(S, h), qgT, hs(qgT, h), start=True, stop=False)
                        b.mm(P_[6], ph_(P_[6], h), vnew, hs(vnew, h), attnT, hs(attnT, h), start=False, stop=True)
                    for h in range(4):
                        b.mm(P_[5], ph_(P_[5], h), kd, hs(kd, h), vnew, hs(vnew, h))
                    b.cp("act", ost, ost[:], P_[6], p3(P_[6]))
                    b.load(oTd, oTd.t[d, :, :, tok0:tok0 + 128].rearrange("h p t -> p h t"), ost, ost[:])
                    for h in range(4):
                        b.stt(S, hs(S, h), S, hs(S, h), egl[:, h:h + 1], P_[5], ph_(P_[5], h), ALU.mult, ALU.add, extra=[egl])
                if not lat:
                    b.load(nsg, nsg.t[si, i, d].rearrange("h k v -> k h v"), S, S[:])
    with b.phase() as ph:
        gng = b.sb(ph, [128, 2], F32, "gng")
        cfw = b.sb(ph, [128, 2, 4, 31], F32, "cfw")
        cfb = b.sb(ph, [128, 2, 4], F32, "cfb")
        lng = b.sb(ph, [128, 2, 4], F32, "lng")
        lnb = b.sb(ph, [128, 2, 4], F32, "lnb")
        for t_, k_ in ((gng, "gdngT"), (cfw, "cfwT"), (cfb, "cfbT"), (lng, "lngT"), (lnb, "lnbT")):
            b.load(t_, t_[:], dr[k_], dr[k_].t)
        of = b.sb(ph, [128, 512], F32, "of")
        ob = b.sb(ph, [128, 512], F32, "ob")
        zz = b.sb(ph, [128, 512], F32, "zz")
        sq = b.sb(ph, [128, 512], F32, "sq")
        rs = b.sb(ph, [128, 512], F32, "rs")
        yc = b.sb(ph, [128, 512], BF16, "yc")
        pn = b.ps(ph, [128, 512], F32, "pn")
        pv = b.ps(ph, [128, 512], F32, "pv")
        for tg in range(6):
            cs_ = slice(tg * 512, (tg + 1) * 512)
            for h in range(4):
                b.load(of, of[:], oTd, oTd.t[0, h, :, cs_])
                b.load(ob, ob[:], oTd, oTd.t[1, h, :, cs_])
                b.load(zz, zz[:], pT, pT.t[12 + h, :, cs_])
                b.tt("dve", of, of[:], of, of[:], ob, ob[:], ALU.add)
                b.act(sq, sq[:], of, of[:], AF.Square)
                b.mm(pn, pn[:], ones, ones[:], sq, sq[:])
                b.act(rs, rs[:], pn, pn[:], AF.Ln, scale=1.0 / 128.0, bias=epsc[:, 0:1], extra=[epsc])
                b.act(rs, rs[:], rs, rs[:], AF.Exp, scale=-0.5)
                b.act(zz, zz[:], zz, zz[:], AF.Silu)
                b.tt("dve", of, of[:], of, of[:], rs, rs[:], ALU.mult)
                b.stt(yc, yc[:], of, of[:], gng[:, i:i + 1], zz, zz[:], ALU.mult, ALU.mult, extra=[gng])
                b.load(yTd, yTd.t[h, :, cs_], yc, yc[:])
        ca = b.sb(ph, [128, LS], F32, "ca")
        cg = b.sb(ph, [128, LS], F32, "cg")
        hp = b.sb(ph, [128, LS + 30], F32, "hp")
        cv = b.sb(ph, [128, 4, LS], F32, "cv")
        mean = b.sb(ph, [128, 512], F32, "mean")
        xc = b.sb(ph, [128, 4, 512], F32, "xc")
        sq4 = b.sb(ph, [128, 4, 512], F32, "sq4")
        for (off, L, lat) in SEQS:
            for cb in range(4):
                b.load(ca, ca[:, 0:L], pT, pT.t[16 + cb, :, off:off + L])
                b.load(cg, cg[:, 0:L], pT, pT.t[20 + cb, :, off:off + L])
                b.memset("pool", hp, hp[:, 0:15], 0.0)
                b.memset("pool", hp, hp[:, L + 15:L + 30], 0.0)
                b.act(cg, cg[:, 0:L], cg, cg[:, 0:L], AF.Sigmoid)
                b.tt("pool", hp, hp[:, 15:L + 15], ca, ca[:, 0:L], cg, cg[:, 0:L], ALU.mult)
                b.ts("dve", cv, cv[:, cb, 0:L], hp, hp[:, 0:L], cfw[:, i, cb, 0:1], ALU.mult, cfb[:, i, cb:cb + 1], ALU.add, extra=[cfw, cfb])
                for k in range(1, 31):
                    b.stt(cv, cv[:, cb, 0:L], hp, hp[:, k:k + L], cfw[:, i, cb, k:k + 1], cv, cv[:, cb, 0:L], ALU.mult, ALU.add, extra=[cfw])
            W = min(512, L)
            for pc_ in range(L // W):
                sl = slice(pc_ * W, (pc_ + 1) * W)
                for cb in range(4):
                    b.mm(pn, pn[:, 0:W], ones, ones[:], cv, cv[:, cb, sl], start=(cb == 0), stop=(cb == 3))
                b.act(mean, mean[:, 0:W], pn, pn[:, 0:W], AF.Copy, scale=1.0 / 512.0)
                for cb in range(4):
                    b.tt("dve", xc, xc[:, cb, 0:W], cv, cv[:, cb, sl], mean, mean[:, 0:W], ALU.subtract)
                b.act(sq4, sq4[:, :, 0:W], xc, xc[:, :, 0:W], AF.Square)
                for cb in range(4):
                    b.mm(pv, pv[:, 0:W], ones, ones[:], sq4, sq4[:, cb, 0:W], start=(cb == 0), stop=(cb == 3))
                b.act(rs, rs[:, 0:W], pv, pv[:, 0:W], AF.Ln, scale=1.0 / 512.0, bias=epsc[:, 0:1], extra=[epsc])
                b.act(rs, rs[:, 0:W], rs, rs[:, 0:W], AF.Exp, scale=-0.5)
                for cb in range(4):
                    b.tt("dve", xc, xc[:, cb, 0:W], xc, xc[:, cb, 0:W], rs, rs[:, 0:W], ALU.mult)
                    b.act(yc, yc[:, 0:W], xc, xc[:, cb, 0:W], AF.Silu, scale=lng[:, i, cb:cb + 1], bias=lnb[:, i, cb:cb + 1], extra=[lng, lnb])
                    b.load(yTd, yTd.t[4 + cb, :, off + pc_ * W:off + (pc_ + 1) * W], yc, yc[:, 0:W])


def peer_phase(b, nc, dr, l, i, even, xT, yTd, modT, A2, ident, ones, epsc, norm_mod, xview, do_peer=True, scr=None):
    uTb, vbd, ekg, h2bd = scr
    if do_peer:
        peer_cast(b, nc, dr, l, uTb, vbd)
    with b.phase() as ph:
        wout = b.sb(ph, [128, 8, 1024], BF16, "wout")
        wq = b.sb(ph, [128, 8, 2048], BF16, "wq")
        keys = b.sb(ph, [128, 16, 128], BF16, "keys")
        iota16 = b.sb(ph, [128, 16], F32, "iota16")
        wsrc = dr["ev_w_out" if even else "od_w_out"]
        cstg = [b.sb(ph, [128, 1024], F32, "cstg") for _ in range(2)]
        ck = 0
        for kc in range(8):
            b.load_cast(cstg, wout, wout[:, kc, :], wsrc, wsrc.t[i, kc * 128:(kc + 1) * 128, :], 1024, ck)
            ck += 1
            for c0 in (0, 1024):
                b.load_cast(cstg, wq, wq[:, kc, c0:c0 + 1024], dr["peer_wq"], dr["peer_wq"].t[l, kc * 128:(kc + 1) * 128, c0:c0 + 1024], 1024, ck)
                ck += 1
        for c8 in range(2):
            st = cstg[ck % 2]
            ck += 1
            b.load(st, st[:].rearrange("p (c n) -> p c n", c=8), dr["keysT"], dr["keysT"].t[l, c8 * 8:(c8 + 1) * 8].rearrange("c d n -> d c n"))
            b.cp("pool", keys, keys[:, c8 * 8:(c8 + 1) * 8, :], st, st[:].rearrange("p (c n) -> p c n", c=8))
        b.load(iota16, iota16[:], dr["iota16"], dr["iota16"].t)
        thr16 = b.sb(ph, [128, 16], F32, "thr16")
        b.ts("dve", thr16, thr16[:], iota16, iota16[:], 16.0, ALU.mult)
        xn = b.sb(ph, [128, 8, 512], F32, "xn")
        h2f = b.sb(ph, [128, 8, 512], F32, "h2f")
        h2b = b.sb(ph, [128, 8, 512], BF16, "h2b")
        yg, xg, sq = h2b, h2f, h2f
        tmp = b.sb(ph, [128, 512], F32, "tmp")
        rstd = b.sb(ph, [128, 512], F32, "rstd")
        psn = b.ps(ph, [128, 512], F32, "psn")
        pmm = [b.ps(ph, [128, 512], F32, "pmm") for _ in range(2)]
        psc = [b.ps(ph, [128, 512], F32, "psc") for _ in range(4)]
        qT = b.sb(ph, [128, 16, 512], BF16, "qT")
        h2t = b.sb(ph, [128, 1024], F32, "h2t")
        ssb = b.sb(ph, [128, 16, 128], F32, "ssb")
        wk = b.sb(ph, [128, 256], F32, "wk")
        mx = b.sb(ph, [128, 16, 16], F32, "mx")
        mi = b.sb(ph, [128, 16, 16], U32, "mi")
        mif = b.sb(ph, [128, 16, 16], F32, "mif")
        i1s = b.sb(ph, [128, 8, 16], F32, "i1s")
        cs_ = b.sb(ph, [128, 8, 256], F32, "cs")
        tsv = b.sb(ph, [128, 8, 16], F32, "tsv")
        sel = b.sb(ph, [128, 8, 16], U32, "sel")
        self_ = b.sb(ph, [128, 8, 16], F32, "self")
        aq = b.sb(ph, [128, 8, 16], F32, "aq")
        bq = b.sb(ph, [128, 8, 16], F32, "bq")
        oh = b.sb(ph, [128, 128, 16], F32, "oh")
        i1sel = b.sb(ph, [128, 128], F32, "i1sel")
        i2sel = b.sb(ph, [128, 128], F32, "i2sel")
        eidx = b.sb(ph, [128, 128], I32, "eidx")
        gt = b.sb(ph, [128, 8, 16], F32, "gt")
        zs = b.sb(ph, [128, 8], F32, "zs")
        pre = b.sb(ph, [128, 128], F32, "pre")
        wgt = b.sb(ph, [128, 128], F32, "wgt")
        junk = b.sb(ph, [128, 1024], F32, "junk")
        NB = 4
        gbuf = [b.sb(ph, [128, 1024], F32, "gbuf") for _ in range(NB)]
        acc = b.sb(ph, [128, 1024], F32, "acc")
        gk = 0
        mxv = mx[:].rearrange("p (h t) k -> p h t k", t=2)
        mifv = mif[:].rearrange("p (h t) k -> p h t k", t=2)
        for tg in range(6):
            v = 0 if tg < 2 else 1
            b.load(yg, yg[:], yTd, xview(yTd, tg * 512, 512))
            b.load(xg, xg[:], xT, xview(xT, tg * 512, 512))
            for dc in range(8):
                p_ = pmm[dc % 2]
                for kc in range(8):
                    b.mm(p_, p_[:], wout, wout[:, kc, dc * 128:(dc + 1) * 128], yg, yg[:, kc, :], start=(kc == 0), stop=(kc == 7))
                b.stt(xn, xn[:, dc, :], p_, p_[:], modT[:, l, 16 + dc, v:v + 1], xg, xg[:, dc, :], ALU.mult, ALU.add, extra=[modT])
            if not do_peer:
                b.load(xT, xview(xT, tg * 512, 512), xn, xn[:])
                continue
            norm_mod(ph, xn, 512, lambda c: A2[:, l, c, v:v + 1], lambda c: modT[:, l, 24 + c, v:v + 1], hb=h2b, hf=h2f, tmp=tmp, sq=sq, ps_=psn, rstd=rstd)
            for hp_ in range(16):
                p_ = pmm[hp_ % 2]
                for kc in range(8):
                    b.mm(p_, p_[:], wq, wq[:, kc, hp_ * 128:(hp_ + 1) * 128], h2b, h2b[:, kc, :], start=(kc == 0), stop=(kc == 7))
                b.cp("act", qT, qT[:, hp_, :], p_, p_[:])
            for tt_ in range(4):
                ts_ = slice(tt_ * 128, (tt_ + 1) * 128)
                for hp_ in range(16):
                    pq = psc[hp_ // 4]
                    b.mm(pq, pq[:, (hp_ % 4) * 128:(hp_ % 4 + 1) * 128], qT, qT[:, hp_, ts_], keys, keys[:, hp_, :])
                for q4 in range(4):
                    b.cp("act", ssb, ssb[:, q4 * 4:(q4 + 1) * 4, :], psc[q4], psc[q4][:].rearrange("p (a n) -> p a n", a=4))
                for hp_ in range(16):
                    b.I("dve", lambda hp_=hp_: nc.vector.max(out=mx[:, hp_, 0:8], in_=ssb[:, hp_, :]), r=[ssb], w=[mx])
                    b.I("dve", lambda hp_=hp_: nc.vector.max_index(out=mi[:, hp_, 0:8], in_max=mx[:, hp_, 0:8], in_values=ssb[:, hp_, :]), r=[ssb, mx], w=[mi])
                    b.I("dve", lambda hp_=hp_: nc.vector.match_replace(out=wk[:, 0:128], in_to_replace=mx[:, hp_, 0:8], in_values=ssb[:, hp_, :], imm_value=-1e30), r=[ssb, mx], w=[wk])
                    b.I("dve", lambda hp_=hp_: nc.vector.max(out=mx[:, hp_, 8:16], in_=wk[:, 0:128]), r=[wk], w=[mx])
                    b.I("dve", lambda hp_=hp_: nc.vector.max_index(out=mi[:, hp_, 8:16], in_max=mx[:, hp_, 8:16], in_values=wk[:, 0:128]), r=[wk, mx], w=[mi])
                b.cp("dve", mif, mif[:], mi, mi[:])
                csv = cs_[:].rearrange("p h (a c) -> p h a c", a=16)
                b.tt("dve", cs_, csv, mx, mxv[:, :, 0, :].unsqueeze(3).to_broadcast([128, 8, 16, 16]),
                     mx, mxv[:, :, 1, :].unsqueeze(2).to_broadcast([128, 8, 16, 16]), ALU.add)
                for h in range(8):
                    b.I("dve", lambda h=h: nc.vector.max(out=tsv[:, h, 0:8], in_=cs_[:, h, :]), r=[cs_], w=[tsv])
                    b.I("dve", lambda h=h: nc.vector.max_index(out=sel[:, h, 0:8], in_max=tsv[:, h, 0:8], in_values=cs_[:, h, :]), r=[cs_, tsv], w=[sel])
                    b.I("dve", lambda h=h: nc.vector.match_replace(out=wk[:], in_to_replace=tsv[:, h, 0:8], in_values=cs_[:, h, :], imm_value=-1e30), r=[cs_, tsv], w=[wk])
                    b.I("dve", lambda h=h: nc.vector.max(out=tsv[:, h, 8:16], in_=wk[:]), r=[wk], w=[tsv])
                    b.I("dve", lambda h=h: nc.vector.max_index(out=sel[:, h, 8:16], in_max=tsv[:, h, 8:16], in_values=wk[:]), r=[wk, tsv], w=[sel])
                b.cp("dve", self_, self_[:], sel, sel[:])
                ohv0 = oh[:].rearrange("p (h k) a -> p h k a", h=8)
                b.tt("dve", oh, ohv0, self_, self_[:].unsqueeze(3).to_broadcast([128, 8, 16, 16]),
                     thr16, thr16[:].unsqueeze(1).unsqueeze(1).to_broadcast([128, 8, 16, 16]), ALU.is_ge)
                b.I("dve", lambda: nc.vector.tensor_reduce(out=aq[:].rearrange("p h k -> p (h k)"), in_=oh[:], axis=AX.X, op=ALU.add), r=[oh], w=[aq])
                b.ts("dve", aq, aq[:], aq, aq[:], -1.0, ALU.add)
                b.stt(bq, bq[:], aq, aq[:], -16.0, self_, self_[:], ALU.mult, ALU.add)
                for (qq, half, dst) in ((aq, 0, i1sel), (bq, 1, i2sel)):
                    ohv = oh[:].rearrange("p (h k) a -> p h k a", h=8)
                    b.tt("dve", oh, ohv, qq, qq[:].unsqueeze(3).to_broadcast([128, 8, 16, 16]),
                         iota16, iota16[:].unsqueeze(1).unsqueeze(1).to_broadcast([128, 8, 16, 16]), ALU.is_equal)
                    b.tt("dve", oh, ohv, oh, ohv, mif, mifv[:, :, half, :].unsqueeze(2).to_broadcast([128, 8, 16, 16]), ALU.mult)
                    b.I("dve", lambda dst=dst: nc.vector.tensor_reduce(out=dst[:], in_=oh[:], axis=AX.X, op=ALU.add), r=[oh], w=[dst])
                b.tt("dve", gt, gt[:], tsv, tsv[:], tsv, tsv[:, :, 0:1].to_broadcast([128, 8, 16]), ALU.subtract)
                b.act(gt, gt[:], gt, gt[:], AF.Exp)
                b.I("dve", lambda: nc.vector.tensor_reduce(out=zs[:], in_=gt[:], axis=AX.X, op=ALU.add), r=[gt], w=[zs])
                b.I("dve", lambda: nc.vector.reciprocal(out=zs[:], in_=zs[:]), r=[zs], w=[zs])
                b.tt("dve", gt, gt[:], gt, gt[:], zs, zs[:].unsqueeze(2).to_broadcast([128, 8, 16]), ALU.mult)
                tok0 = tg * 512 + tt_ * 128
                b.load(ekg, ekg.t[0, tok0:tok0 + 128, :], i1sel, i1sel[:])
                b.load(ekg, ekg.t[1, tok0:tok0 + 128, :], i2sel, i2sel[:])
                b.load(ekg, ekg.t[2, tok0:tok0 + 128, :], gt, gt[:].rearrange("p h k -> p (h k)"))
            b.load(h2bd, xview(h2bd, tg * 512, 512), h2b, h2b[:])
            b.load(xT, xview(xT, tg * 512, 512), xn, xn[:])
    if do_peer:
        peer_dense(b, nc, dr, l, xT, modT, ident, uTb, vbd, ekg, h2bd, xview)


def peer_cast(b, nc, dr, l, uTb, vbd):
    with b.phase() as ph:
        st = [b.sb(ph, [128, 4, 1024], F32, "cst") for _ in range(3)]
        sb_ = [b.sb(ph, [128, 4, 1024], BF16, "csb") for _ in range(3)]
        k = 0
        engs = ("act", "pool", "dve")
        for (src, dst) in ((dr["peer_u"], uTb), (dr["peer_v"], vbd)):
            for g in range(32):
                f_ = st[k % 3]
                o_ = sb_[k % 3]
                b.load(f_, f_[:], src, src.t[l, g * 4:(g + 1) * 4].rearrange("c p n -> p c n"))
                b.cp(engs[k % 3], o_, o_[:], f_, f_[:])
                b.load(dst, dst.t[g * 4:(g + 1) * 4].rearrange("c p n -> p c n"), o_, o_[:])
                k += 1


def peer_dense(b, nc, dr, l, xT, modT, ident, uTb, vbd, ekg, h2bd, xview):
    TG = 384
    with b.phase() as ph:
        iota = b.sb(ph, [128, 128], F32, "iota128")
        b.load(iota, iota[:], dr["iota128"], dr["iota128"].t)
        GT = b.sb(ph, [128, 128, TG], BF16, "GT")
        h2b = b.sb(ph, [128, 8, TG], BF16, "h2b")
        xn = b.sb(ph, [128, 8, TG], F32, "xn")
        ek = b.sb(ph, [128, 3, 128], F32, "ek")
        ET = b.sb(ph, [128, 3, 128], F32, "ET")
        ohA = [b.sb(ph, [128, 16, 128], BF16, "ohA") for _ in range(2)]
        ohB = [b.sb(ph, [128, 16, 128], BF16, "ohB") for _ in range(2)]
        ohT = [b.sb(ph, [128, 16, 128], BF16, "ohT") for _ in range(2)]
        NW = 4
        Uc = [b.sb(ph, [128, 1024], BF16, "Uc") for _ in range(NW)]
        Vc = [b.sb(ph, [128, 1024], BF16, "Vc") for _ in range(NW)]
        gel = [b.sb(ph, [128, TG], BF16, "gel") for _ in range(4)]
        AT = [b.sb(ph, [128, TG], BF16, "AT") for _ in range(4)]
        fo = b.sb(ph, [128, 1024], F32, "fo")
        pss = [b.ps(ph, [128, 512], F32, "pss") for _ in range(2)]
        pg = pss
        po = [b.ps(ph, [128, 512], F32, "po") for _ in range(2 * (TG // 128))]
        for grp in range(NT // TG):
            g0 = grp * TG
            v = 0 if g0 < NPR else 1
            b.load(h2b, h2b[:], h2bd, xview(h2bd, g0, TG))
            b.load(xn, xn[:], xT, xview(xT, g0, TG))
            kq = 0
            for tl in range(TG // 128):
                tok0 = g0 + tl * 128
                b.load(ek, ek[:], ekg, ekg.t[:, tok0:tok0 + 128, :].rearrange("c t s -> t c s"))
                for c3 in range(3):
                    b.tr(pg[0], pg[0][:, c3 * 128:(c3 + 1) * 128], ek, ek[:, c3, :], ident)
                b.cp("act", ET, ET[:], pg[0], pg[0][:, 0:384].rearrange("p (c t) -> p c t", c=3))
                for sub in range(8):
                    t0 = sub * 16
                    A_ = ohA[sub % 2]
                    B_ = ohB[sub % 2]
                    T_ = ohT[sub % 2]
                    io3 = iota[:].unsqueeze(1).to_broadcast([128, 16, 128])
                    b.tt("dve", B_, B_[:], iota, io3, ET, ET[:, 0, t0:t0 + 16].unsqueeze(2).to_broadcast([128, 16, 128]), ALU.is_equal)
                    b.tt("dve", T_, T_[:], iota, io3, ET, ET[:, 1, t0:t0 + 16].unsqueeze(2).to_broadcast([128, 16, 128]), ALU.is_equal)
                    b.tt("pool", A_, A_[:], T_, T_[:], ET, ET[:, 2, t0:t0 + 16].unsqueeze(2).to_broadcast([128, 16, 128]), ALU.mult)
                    for q4 in range(4):
                        p_ = pg[kq % 2]
                        kq += 1
                        pv = p_[:].rearrange("p (i t) -> p i t", t=4)
                        for tq in range(4):
                            tt_ = q4 * 4 + tq
                            b.mm(p_, pv[:, :, tq], A_, A_[:, tt_, :], B_, B_[:, tt_, :])
                        tg0 = tl * 128 + t0 + q4 * 4
                        b.cp("act", GT, GT[:, :, tg0:tg0 + 4], p_, pv)
            for i1 in range(128):
                u_ = Uc[i1 % NW]
                v_ = Vc[i1 % NW]
                b.load(u_, u_[:], uTb, uTb.t[i1])
                b.load(v_, v_[:], vbd, vbd.t[i1])
                ps_ = pss[i1 % 2]
                for kc in range(8):
                    b.mm(ps_, ps_[:, 0:TG], u_, u_[:, kc * 128:(kc + 1) * 128], h2b, h2b[:, kc, :], start=(kc == 0), stop=(kc == 7))
                g_ = gel[i1 % 4]
                a_ = AT[i1 % 4]
                b.act(g_, g_[:], ps_, ps_[:, 0:TG], AF.Gelu_apprx_tanh)
                b.tt("pool" if i1 % 2 == 0 else "dve", a_, a_[:], g_, g_[:], GT, GT[:, i1, :], ALU.mult)
                for tl in range(TG // 128):
                    for half in range(2):
                        p_ = po[tl * 2 + half]
                        b.mm(p_, p_[:], a_, a_[:, tl * 128:(tl + 1) * 128], v_, v_[:, half * 512:(half + 1) * 512], start=(i1 == 0), stop=(i1 == 127))
            for tl in range(TG // 128):
                ts_ = slice(tl * 128, (tl + 1) * 128)
                v = 0 if (g0 + tl * 128) < NPR else 1
                for half in range(2):
                    b.cp("act", fo, fo[:, half * 512:(half + 1) * 512], po[tl * 2 + half], po[tl * 2 + half][:])
                for half in range(2):
                    p_ = pg[half]
                    for c4 in range(4):
                        c = half * 4 + c4
                        b.tr(p_, p_[:, c4 * 128:(c4 + 1) * 128], fo, fo[:, c * 128:(c + 1) * 128], ident)
                    for c4 in range(4):
                        c = half * 4 + c4
                        b.stt(xn, xn[:, c, ts_], p_, p_[:, c4 * 128:(c4 + 1) * 128], modT[:, l, 40 + c, v:v + 1], xn, xn[:, c, ts_], ALU.mult, ALU.add, extra=[modT])
            b.load(xT, xview(xT, g0, TG), xn, xn[:])


_CACHE = {}


def _perm_qk():
    idx = np.zeros(256, np.int64)
    for s in range(2):
        for h in range(4):
            for j in range(32):
                idx[s * 128 + h * 32 + j] = h * 64 + s * 32 + j
    return idx


def kernel(x_prompt, x_sample, state_ret, state_gdn, c, c_ctx, ada_w, ada_b, norm_mix_g, norm_ffn_g,
           final_norm_g, ev_w_in, ev_w_out, ret_gamma_logit, ret_norm_g, sc_conv_w, od_w_in, od_w_out,
           gdn_conv_w, gdn_a_log, gdn_dt_bias, gdn_norm_g, cf_dw_w, cf_dw_b, cf_ln_g, cf_ln_b,
           peer_wq, peer_keys, peer_u, peer_v):
    f = lambda a: np.ascontiguousarray(np.asarray(a, dtype=np.float32))
    x_prompt, x_sample = f(x_prompt), f(x_sample)
    shared = {}
    shared["ada_w"] = f(ada_w)
    shared["ada_bT"] = f(np.asarray(ada_b).reshape(4, 48, 128).transpose(2, 0, 1))
    shared["gmixT"] = f(np.asarray(norm_mix_g).reshape(4, 8, 128).transpose(2, 0, 1))
    shared["gffnT"] = f(np.asarray(norm_ffn_g).reshape(4, 8, 128).transpose(2, 0, 1))
    shared["gfinT"] = f(np.asarray(final_norm_g).reshape(8, 128).T)
    pi = _perm_qk()
    ew = np.array(ev_w_in, dtype=np.float32)
    ew[:, :, 0:256] = np.asarray(ev_w_in)[:, :, 0:256][:, :, pi]
    ew[:, :, 256:512] = np.asarray(ev_w_in)[:, :, 256:512][:, :, pi]
    shared["ev_w_in"] = f(ew)
    shared["ev_w_out"] = f(ev_w_out)
    shared["lgam"] = f(np.asarray(ret_gamma_logit).reshape(2, 8))
    shared["retgT"] = f(np.asarray(ret_norm_g).transpose(2, 0, 1))
    shared["scwT"] = f(np.asarray(sc_conv_w).reshape(2, 3, 4, 128).transpose(3, 0, 2, 1))
    shared["od_w_in"] = f(od_w_in)
    shared["od_w_out"] = f(od_w_out)
    shared["gcwT"] = f(np.asarray(gdn_conv_w).reshape(2, 3, 12, 128).transpose(3, 0, 2, 1))
    shared["alog"] = f(np.asarray(gdn_a_log).reshape(2, 8))
    shared["dtb"] = f(np.asarray(gdn_dt_bias).reshape(2, 8))
    shared["gdngT"] = f(np.asarray(gdn_norm_g).T)
    shared["cfwT"] = f(np.asarray(cf_dw_w).reshape(2, 31, 4, 128).transpose(3, 0, 2, 1))
    shared["cfbT"] = f(np.asarray(cf_dw_b).reshape(2, 4, 128).transpose(2, 0, 1))
    shared["lngT"] = f(np.asarray(cf_ln_g).reshape(2, 4, 128).transpose(2, 0, 1))
    shared["lnbT"] = f(np.asarray(cf_ln_b).reshape(2, 4, 128).transpose(2, 0, 1))
    shared["peer_wq"] = f(peer_wq)
    shared["keysT"] = f(np.asarray(peer_keys).reshape(4, 16, 128, 128).transpose(0, 1, 3, 2))
    shared["peer_u"] = f(np.asarray(peer_u).reshape(4, 128, 128, 8, 128).transpose(0, 1, 4, 3, 2)).reshape(4, 128, 128, 1024)
    shared["peer_v"] = f(peer_v).reshape(4, 128, 128, 1024)
    for k, v_ in host_consts().items():
        shared["c_" + k] = f(v_)
    in_maps = []
    cc = np.asarray(c, dtype=np.float32)
    cx = np.asarray(c_ctx, dtype=np.float32)
    for r in range(NCORES):
        m = dict(shared)
        xs = np.concatenate([x_prompt[4 * r:4 * r + 4].reshape(NPR, D), x_sample[r]], axis=0)
        m["xT0"] = f(xs.T.reshape(8, 128, NT))
        cv = np.stack([cx, cc[r]], axis=0)
        m["cT"] = f(cv.reshape(2, 8, 128).transpose(2, 1, 0))
        m["sret"] = f(np.asarray(state_ret)[r])
        m["sgdn"] = f(np.asarray(state_gdn)[r])
        in_maps.append(m)
    if "nc" not in _CACHE:
        _CACHE["nc"] = build()
    res = run_bass_kernel_spmd(_CACHE["nc"], in_maps, core_ids=list(range(NCORES)))
    y_prompt = np.zeros((32, 256, D), np.float32)
    y_sample = np.zeros((8, 2048, D), np.float32)
    nret = np.zeros((32, 2, 2, 4, 64, 128), np.float32)
    ngdn = np.zeros((32, 2, 2, 4, 128, 128), np.float32)
    if STOP:
        _CACHE["pT"] = [np.asarray(res.results[r]["pT"]) for r in range(NCORES)]
        _CACHE["yTd"] = [np.asarray(res.results[r]["yTd"]) for r in range(NCORES)]
        _CACHE["dmod"] = [np.asarray(res.results[r]["dmod"]) for r in range(NCORES)]
        _CACHE["dbg"] = [np.asarray(res.results[r]["dbg"]).reshape(D, NT).T for r in range(NCORES)]
    for r in range(NCORES):
        o = res.results[r]
        yt = np.asarray(o["yT"]).reshape(D, NT).T
        y_prompt[4 * r:4 * r + 4] = yt[:NPR].reshape(4, 256, D)
        y_sample[r] = yt[NPR:]
        nret[4 * r:4 * r + 4] = np.asarray(o["nsr"])
        ngdn[4 * r:4 * r + 4] = np.asarray(o["nsg"])
    return (y_prompt, y_sample, nret, ngdn)
```

```python
import os
import numpy as np
from contextlib import ExitStack, contextmanager
import concourse.bass as bass
import concourse.mybir as mybir
from concourse.bass_utils import run_bass_kernel_spmd

F32 = mybir.dt.float32
BF16 = mybir.dt.bfloat16
U32 = mybir.dt.uint32
I32 = mybir.dt.int32
ALU = mybir.AluOpType
AF = mybir.ActivationFunctionType
AX = mybir.AxisListType

NCORES = 8
D = 1024
NT = 3072
NPR = 1024
LP = 256
LS = 2048
EPS = 1e-6
SEQS = [(0, 256, False), (256, 256, False), (512, 256, False), (768, 256, False), (1024, 2048, True)]
STOP = os.environ.get("KSTOP", "")
SKIPG = os.environ.get("KSKIPG", "") == "1"


class Res:
    __slots__ = ("w", "rs")

    def __init__(self):
        self.w = None
        self.rs = []


class T:
    def __init__(self, t, res=None):
        self.t = t
        self.res = res if res is not None else Res()

    def __getitem__(self, k):
        return self.t[k]


class Builder:
    EPOCH = 16000
    NDMA = 24

    def __init__(self, nc, es):
        self.nc = nc
        self.es = es
        self.E = {"pe": nc.tensor, "act": nc.scalar, "dve": nc.vector, "pool": nc.gpsimd, "sp": nc.sync}
        self.cur = {}
        self.cnt = {}
        self.nsem = 0
        for e in self.E:
            self._newsem(e)
        self.seen = {e: {} for e in self.E}
        self.dsem = [es.enter_context(nc.semaphore("dq%d" % i)) for i in range(self.NDMA)]
        self.duse = [0] * self.NDMA
        self.dk = 0
        self.uid = 0
        self.rec = None

    def _newsem(self, e):
        self.nsem += 1
        self.cur[e] = self.es.enter_context(self.nc.semaphore("s_%s_%d" % (e, self.nsem)))
        self.cnt[e] = 0

    def name(self, p):
        self.uid += 1
        return "%s_%d" % (p, self.uid)

    def sb(self, es, shape, dt=F32, name="t"):
        return T(es.enter_context(self.nc.sbuf_tensor(self.name(name), list(shape), dt)))

    def ps(self, es, shape, dt=F32, name="p"):
        return T(es.enter_context(self.nc.psum_tensor(self.name(name), list(shape), dt)))

    def dram(self, name, shape, dt, kind):
        return T(self.nc.dram_tensor(name, list(shape), dt, kind=kind).ap())

    def _deps(self, eng, r, w):
        deps = {}

        def add(ev):
            if ev is None:
                return
            s, v, src = ev
            if eng == "pe" and src == "pe":
                return
            k = id(s)
            if k not in deps or deps[k][1] < v:
                deps[k] = (s, v)

        for x in r:
            add(x.res.w)
        for x in w:
            add(x.res.w)
            for ev in x.res.rs:
                add(ev)
        E = self.E[eng]
        sn = self.seen[eng]
        for k, (s, v) in deps.items():
            if sn.get(k, 0) >= v:
                continue
            E.wait_ge(s, v)
            sn[k] = v

    def _mark(self, ev, r, w):
        for x in r:
            x.res.rs.append(ev)
        for x in w:
            x.res.w = ev
            x.res.rs = []

    def I(self, eng, f, r=(), w=()):
        if self.rec is not None:
            self.rec.append((0, eng, f, r, w))
            return None
        self._deps(eng, r, w)
        inst = f()
        if self.cnt[eng] >= self.EPOCH:
            self._newsem(eng)
        self.cnt[eng] += 1
        ev = (self.cur[eng], self.cnt[eng], eng)
        inst.then_inc(ev[0], 1)
        self._mark(ev, r, w)
        return ev

    def DMA(self, eng, f, r=(), w=()):
        if self.rec is not None:
            self.rec.append((1, eng, f, r, w))
            return None
        self._deps(eng, r, w)
        k = self.dk % self.NDMA
        self.dk += 1
        s = self.dsem[k]
        E = self.E[eng]
        if self.duse[k] > 0:
            key = id(s)
            if self.seen[eng].get(key, 0) < 16 * self.duse[k]:
                E.wait_ge(s, 16 * self.duse[k])
                self.seen[eng][key] = 16 * self.duse[k]
        self.duse[k] += 1
        inst = f()
        ev = (s, 16 * self.duse[k], "dma")
        inst.then_inc(s, 16)
        self._mark(ev, r, w)
        return ev

    def load(self, dst, dst_ap, src, src_ap, eng="sp"):
        return self.DMA(eng, lambda: self.E[eng].dma_start(out=dst_ap, in_=src_ap), r=[src], w=[dst])

    def tt(self, eng, out, o_ap, a, a_ap, b, b_ap, op):
        return self.I(eng, lambda: self.E[eng].tensor_tensor(out=o_ap, in0=a_ap, in1=b_ap, op=op), r=[a, b], w=[out])

    def ts(self, eng, out, o_ap, a, a_ap, s1, op0, s2=None, op1=None, extra=()):
        if op1 is None:
            f = lambda: self.E[eng].tensor_scalar(out=o_ap, in0=a_ap, scalar1=s1, scalar2=None, op0=op0)
        else:
            f = lambda: self.E[eng].tensor_scalar(out=o_ap, in0=a_ap, scalar1=s1, scalar2=s2, op0=op0, op1=op1)
        return self.I(eng, f, r=[a] + list(extra), w=[out])

    def stt(self, out, o_ap, a, a_ap, sc, b, b_ap, op0, op1, extra=(), accum=None, accum_t=None):
        if accum is None:
            f = lambda: self.nc.vector.scalar_tensor_tensor(out=o_ap, in0=a_ap, scalar=sc, in1=b_ap, op0=op0, op1=op1)
            w = [out]
        else:
            f = lambda: self.nc.vector.scalar_tensor_tensor(out=o_ap, in0=a_ap, scalar=sc, in1=b_ap, op0=op0, op1=op1, accum_out=accum)
            w = [out, accum_t]
        return self.I("dve", f, r=[a, b] + list(extra), w=w)

    def act(self, out, o_ap, a, a_ap, func, scale=1.0, bias=0.0, extra=()):
        return self.I("act", lambda: self.nc.scalar.activation(out=o_ap, in_=a_ap, func=func, bias=bias, scale=scale),
                      r=[a] + list(extra), w=[out])

    def cp(self, eng, out, o_ap, a, a_ap):
        if eng == "act":
            return self.I("act", lambda: self.nc.scalar.copy(out=o_ap, in_=a_ap), r=[a], w=[out])
        return self.I(eng, lambda: self.E[eng].tensor_copy(out=o_ap, in_=a_ap), r=[a], w=[out])

    def mm(self, out, o_ap, l, l_ap, rr, r_ap, start=True, stop=True):
        return self.I("pe", lambda: self.nc.tensor.matmul(o_ap, lhsT=l_ap, rhs=r_ap, start=start, stop=stop), r=[l, rr], w=[out])

    def tr(self, out, o_ap, a, a_ap, ident):
        return self.I("pe", lambda: self.nc.tensor.transpose(o_ap, a_ap, ident[:]), r=[a, ident], w=[out])

    def memset(self, eng, out, o_ap, v):
        return self.I(eng, lambda: self.E[eng].memset(o_ap, v), r=[], w=[out])

    def load_cast(self, stg, dst, dst_ap, src, src_ap, width, k):
        st = stg[k % len(stg)]
        self.load(st, st[:, 0:width], src, src_ap)
        self.cp("pool" if k % 2 == 0 else "act", dst, dst_ap, st, st[:, 0:width])

    def record(self, fn):
        self.rec = []
        fn()
        lst, self.rec = self.rec, None
        return lst

    def emit_interleaved(self, lists):
        n = max(len(l) for l in lists)
        for k in range(n):
            for l in lists:
                if k < len(l):
                    kind, eng, f, r, w = l[k]
                    (self.DMA if kind else self.I)(eng, f, r, w)

    def barrier(self):
        evs = [(self.cur[e], self.cnt[e]) for e in self.E if self.cnt[e] > 0]
        evs += [(self.dsem[k], 16 * self.duse[k]) for k in range(self.NDMA) if self.duse[k] > 0]
        for eng in self.E:
            E = self.E[eng]
            sn = self.seen[eng]
            for (s, v) in evs:
                if sn.get(id(s), 0) >= v:
                    continue
                E.wait_ge(s, v)
                sn[id(s)] = v

    @contextmanager
    def phase(self):
        with ExitStack() as ph:
            yield ph
            self.barrier()

    def final_wait(self, outs):
        for o in outs:
            self._deps("sp", [o], [])


def host_consts():
    c = {}
    c["ident"] = np.eye(128, dtype=np.float32)
    c["ones"] = np.ones((128, 128), np.float32)
    p = np.arange(128)
    hm = np.zeros((128, 4), np.float32)
    for h in range(4):
        hm[h * 32:(h + 1) * 32, h] = 0.125
    c["hm"] = hm
    t = np.arange(LS, dtype=np.float32)
    row = np.floor(t / 64.0)
    col = t - row * 64.0
    nf = 16
    freqs = (10000.0 ** (-np.arange(nf, dtype=np.float32) / nf)).astype(np.float32)
    ang = np.concatenate([row[:, None] * freqs, col[:, None] * freqs], axis=-1).astype(np.float32)
    c["cos"] = np.tile(np.cos(ang).T.astype(np.float32), (4, 1))
    c["sin"] = np.tile(np.sin(ang).T.astype(np.float32), (4, 1))
    m = np.arange(3968)
    c["dtab"] = (m[None, :] - 1920 - p[:, None]).astype(np.float32)
    c["posf"] = np.tile((t + 1.0)[None, :], (128, 1)).astype(np.float32)
    c["posb"] = np.tile((LS - t)[None, :], (128, 1)).astype(np.float32)
    posst = np.zeros((128, 2, 2), np.float32)
    for jt in range(2):
        posst[:, jt, 0] = 255 - (jt * 128 + p)
        posst[:, jt, 1] = jt * 128 + p
    c["posst"] = posst
    c["Mf"] = (p[:, None] <= p[None, :]).astype(np.float32)
    c["Mb"] = (p[:, None] >= p[None, :]).astype(np.float32)
    c["Sf"] = (p[:, None] < p[None, :]).astype(np.float32)
    c["Sb"] = (p[:, None] > p[None, :]).astype(np.float32)
    Elo = np.zeros((128, 7, 128), np.float32)
    for lv in range(7):
        sz = 1 << lv
        blk_i = p[:, None] // (2 * sz)
        blk_j = p[None, :] // (2 * sz)
        Elo[:, lv, :] = ((blk_i == blk_j) & ((p[:, None] % (2 * sz)) >= sz) & ((p[None, :] % (2 * sz)) < sz)).astype(np.float32)
    c["Elo"] = Elo
    c["Eup"] = np.ascontiguousarray(Elo.transpose(2, 1, 0))
    c["iota128"] = np.tile(np.arange(128, dtype=np.float32)[None, :], (128, 1))
    c["iota16"] = np.tile(np.arange(16, dtype=np.float32)[None, :], (128, 1))
    return c


CONST_SHAPES = {"ident": [128, 128], "ones": [128, 128], "hm": [128, 4], "cos": [128, LS], "sin": [128, LS],
                "dtab": [128, 3968], "posf": [128, LS], "posb": [128, LS], "posst": [128, 2, 2],
                "Mf": [128, 128], "Mb": [128, 128], "Sf": [128, 128], "Sb": [128, 128], "iota16": [128, 16],
                "Elo": [128, 7, 128], "Eup": [128, 7, 128], "iota128": [128, 128]}

IN_SHAPES = {
    "xT0": [8, 128, NT], "cT": [128, 8, 2], "ada_w": [4, 1024, 6144], "ada_bT": [128, 4, 48],
    "gmixT": [128, 4, 8], "gffnT": [128, 4, 8], "gfinT": [128, 8],
    "ev_w_in": [2, 1024, 3072], "ev_w_out": [2, 1024, 1024], "lgam": [2, 8], "retgT": [128, 2, 4],
    "scwT": [128, 2, 4, 3], "od_w_in": [2, 1024, 3088], "od_w_out": [2, 1024, 1024], "gcwT": [128, 2, 12, 3],
    "alog": [2, 8], "dtb": [2, 8], "gdngT": [128, 2], "cfwT": [128, 2, 4, 31], "cfbT": [128, 2, 4],
    "lngT": [128, 2, 4], "lnbT": [128, 2, 4], "peer_wq": [4, 1024, 2048], "keysT": [4, 16, 128, 128],
    "peer_u": [4, 128, 128, 1024], "peer_v": [4, 128, 128, 1024],
    "sret": [2, 2, 4, 64, 128], "sgdn": [2, 2, 4, 128, 128],
}


def build():
    nc = bass.Bass("TRN2", target_bir_lowering=False)
    es0 = ExitStack()
    b = Builder(nc, es0)
    dr = {}
    for k, s in IN_SHAPES.items():
        dr[k] = b.dram(k, s, F32, "ExternalInput")
    for k, s in CONST_SHAPES.items():
        dr[k] = b.dram("c_" + k, s, F32, "ExternalInput")
    yT = b.dram("yT", [8, 128, NT], F32, "ExternalOutput")
    nsr = b.dram("nsr", [4, 2, 2, 4, 64, 128], F32, "ExternalOutput")
    nsg = b.dram("nsg", [4, 2, 2, 4, 128, 128], F32, "ExternalOutput")
    xT = b.dram("xT", [8, 128, NT], F32, "Internal")
    pT = b.dram("pT", [24, 128, NT], F32, "ExternalOutput" if STOP else "Internal")
    vtok = b.dram("vtok", [NT, 512], BF16, "Internal")
    ktok = b.dram("ktok", [NPR, 256], BF16, "Internal")
    qkr = b.dram("qkr", [4, 128, NT], BF16, "Internal")
    yTd = b.dram("yTd", [8, 128, NT], BF16, "ExternalOutput" if STOP else "Internal")
    dmod = b.dram("dmod", [128, 4 * 48 * 2], F32, "ExternalOutput") if STOP else None
    gbtok = b.dram("gbtok", [NT, 16], F32, "Internal")
    qkvT = b.dram("qkvT", [12, 128, NT], F32, "Internal")
    ktok32 = b.dram("ktok32", [NT, 512], F32, "Internal")
    vtok32 = b.dram("vtok32", [NT, 512], F32, "Internal")
    oTd = b.dram("oTd", [2, 4, 128, NT], F32, "Internal")
    uTb = b.dram("uTb", [128, 128, 1024], BF16, "Internal")
    vbd = b.dram("vbd", [128, 128, 1024], BF16, "Internal")
    ekg = b.dram("ekg", [3, NT, 128], F32, "Internal")
    h2bd = b.dram("h2bd", [8, 128, NT], BF16, "Internal")
    scr = (uTb, vbd, ekg, h2bd)
    outs_written = []

    es = es0
    ident = b.sb(es, [128, 128], F32, "ident")
    ones = b.sb(es, [128, 128], F32, "ones")
    epsc = b.sb(es, [128, 1], F32, "epsc")
    onec = b.sb(es, [128, 1], F32, "onec")
    b.load(ident, ident[:], dr["ident"], dr["ident"].t)
    b.load(ones, ones[:], dr["ones"], dr["ones"].t)
    b.memset("dve", epsc, epsc[:], EPS)
    b.memset("dve", onec, onec[:], 1.0)
    modT = b.sb(es, [128, 4, 48, 2], F32, "modT")
    A1 = b.sb(es, [128, 4, 8, 2], F32, "A1")
    A2 = b.sb(es, [128, 4, 8, 2], F32, "A2")
    gfin = b.sb(es, [128, 8], F32, "gfin")
    b.load(gfin, gfin[:], dr["gfinT"], dr["gfinT"].t)

    with b.phase() as ph:
        cT = b.sb(ph, [128, 8, 2], F32, "cT")
        scT = b.sb(ph, [128, 8, 2], F32, "scT")
        abT = b.sb(ph, [128, 4, 48], F32, "abT")
        gmx = b.sb(ph, [128, 4, 8], F32, "gmx")
        gff = b.sb(ph, [128, 4, 8], F32, "gff")
        tmpm = b.sb(ph, [128, 4, 8, 2], F32, "tmpm")
        wts = [b.sb(ph, [128, 8, 768], F32, "adaw") for _ in range(4)]
        pm = b.ps(ph, [128, 512], F32, "pm")
        b.load(cT, cT[:], dr["cT"], dr["cT"].t)
        b.load(abT, abT[:], dr["ada_bT"], dr["ada_bT"].t)
        b.load(gmx, gmx[:], dr["gmixT"], dr["gmixT"].t)
        b.load(gff, gff[:], dr["gffnT"], dr["gffnT"].t)
        b.act(scT, scT[:], cT, cT[:], AF.Silu)
        it = 0
        for l in range(4):
            for grp in range(8):
                wt = wts[it % 4]
                it += 1
                src = dr["ada_w"].t[l, :, grp * 768:(grp + 1) * 768].rearrange("(kc p) n -> p kc n", p=128)
                b.load(wt, wt[:], dr["ada_w"], src)
                for cc in range(6):
                    for kc in range(8):
                        b.mm(pm, pm[:, cc * 2:cc * 2 + 2], wt, wt[:, kc, cc * 128:(cc + 1) * 128], scT, scT[:, kc, :],
                             start=(kc == 0), stop=(kc == 7))
                for cc in range(6):
                    ch = grp * 6 + cc
                    b.ts("dve", modT, modT[:, l, ch, :], pm, pm[:, cc * 2:cc * 2 + 2], abT[:, l, ch:ch + 1], ALU.add, extra=[abT])
        b.ts("dve", tmpm, tmpm[:], modT, modT[:, :, 8:16, :], 1.0, ALU.add)
        b.tt("dve", A1, A1[:], tmpm, tmpm[:], gmx, gmx[:].unsqueeze(3).to_broadcast([128, 4, 8, 2]), ALU.mult)
        b.ts("dve", tmpm, tmpm[:], modT, modT[:, :, 32:40, :], 1.0, ALU.add)
        b.tt("dve", A2, A2[:], tmpm, tmpm[:], gff, gff[:].unsqueeze(3).to_broadcast([128, 4, 8, 2]), ALU.mult)

    def norm_mod(ph, xg, W, Acol, Bcol, hb=None, hf=None, tmp=None, sq=None, ps_=None, rstd=None):
        b.act(sq, sq[:, :, 0:W], xg, xg[:, :, 0:W], AF.Square)
        for c in range(8):
            b.mm(ps_, ps_[:, 0:W], ones, ones[:], sq, sq[:, c, 0:W], start=(c == 0), stop=(c == 7))
        b.act(rstd, rstd[:, 0:W], ps_, ps_[:, 0:W], AF.Ln, scale=1.0 / D, bias=epsc[:, 0:1], extra=[epsc])
        b.act(rstd, rstd[:, 0:W], rstd, rstd[:, 0:W], AF.Exp, scale=-0.5)
        for c in range(8):
            b.tt("dve", tmp, tmp[:, 0:W], xg, xg[:, c, 0:W], rstd, rstd[:, 0:W], ALU.mult)
            tgt = hf if hf is not None else hb
            if Bcol is not None:
                b.ts("dve", tgt, tgt[:, c, 0:W], tmp, tmp[:, 0:W], Acol(c), ALU.mult, Bcol(c), ALU.add, extra=[A1, A2, modT, gfin])
            else:
                b.ts("dve", tgt, tgt[:, c, 0:W], tmp, tmp[:, 0:W], Acol(c), ALU.mult, extra=[A1, A2, modT, gfin])
            if hf is not None and hb is not None:
                b.cp("pool", hb, hb[:, c, 0:W], hf, hf[:, c, 0:W])

    def xview(dt_, lo, W):
        return dt_.t[:, :, lo:lo + W].rearrange("c p t -> p c t")

    with b.phase() as ph:
        xb = [b.sb(ph, [128, 8, 512], F32, "xcp") for _ in range(2)]
        for tg in range(6):
            x_ = xb[tg % 2]
            b.load(x_, x_[:], dr["xT0"], xview(dr["xT0"], tg * 512, 512))
            b.load(xT, xview(xT, tg * 512, 512), x_, x_[:])

    for l in range(4):
        i = l // 2
        even = (l % 2 == 0)
        if STOP == "M":
            break
        with b.phase() as ph:
            ncol = 3072 if even else 3088
            win = b.sb(ph, [128, 8, ncol], BF16, "win")
            wsrc = dr["ev_w_in" if even else "od_w_in"]
            cstg = [b.sb(ph, [128, 1024], F32, "cstg") for _ in range(3)]
            ck = 0
            for kc in range(8):
                for c0 in range(0, ncol, 1024):
                    c1 = min(ncol, c0 + 1024)
                    b.load_cast(cstg, win, win[:, kc, c0:c1], wsrc, wsrc.t[i, kc * 128:(kc + 1) * 128, c0:c1], c1 - c0, ck)
                    ck += 1
            xg = b.sb(ph, [128, 8, 512], F32, "xg")
            hT = b.sb(ph, [128, 8, 512], BF16, "hT")
            sq = b.sb(ph, [128, 8, 512], F32, "sq")
            tmp = b.sb(ph, [128, 512], F32, "tmp")
            rstd = b.sb(ph, [128, 512], F32, "rstd")
            psn = b.ps(ph, [128, 512], F32, "psn")
            pmm = [b.ps(ph, [128, 512], F32, "pmm") for _ in range(3)]
            stg = [b.sb(ph, [128, 512], F32, "stg") for _ in range(3)]
            stgb = [b.sb(ph, [128, 512], BF16, "stgb") for _ in range(2)]
            if even:
                fm_cols = [c * 128 for c in (0, 1, 2, 3)] + [1024 + c * 128 for c in range(16)]
                fm_dst = [0, 1, 2, 3] + list(range(8, 24))
            else:
                fm_cols = [c * 128 for c in range(16)] + [2064 + c * 128 for c in range(8)]
                fm_dst = list(range(24))
                nega = b.sb(ph, [128, 8], F32, "nega")
                dtbb = b.sb(ph, [128, 8], F32, "dtbb")
                gbs = b.sb(ph, [128, 16], F32, "gbs")
                b.load(nega, nega[:], dr["alog"], dr["alog"].t[i:i + 1, :].partition_broadcast(128))
                b.load(dtbb, dtbb[:], dr["dtb"], dr["dtb"].t[i:i + 1, :].partition_broadcast(128))
                b.act(nega, nega[:], nega, nega[:], AF.Exp)
                b.ts("dve", nega, nega[:], nega, nega[:], -1.0, ALU.mult)
            k = 0
            for tg in range(6):
                v = 0 if tg < 2 else 1
                b.load(xg, xg[:], xT, xview(xT, tg * 512, 512))
                norm_mod(ph, xg, 512, lambda c: A1[:, l, c, v:v + 1], lambda c: modT[:, l, c, v:v + 1], hb=hT, tmp=tmp, sq=sq, ps_=psn, rstd=rstd)
                for cc, dc in zip(fm_cols, fm_dst):
                    p_ = pmm[k % 3]
                    s_ = stg[k % 3]
                    k += 1
                    for kc in range(8):
                        b.mm(p_, p_[:], win, win[:, kc, cc:cc + 128], hT, hT[:, kc, :], start=(kc == 0), stop=(kc == 7))
                    b.cp("act", s_, s_[:], p_, p_[:])
                    b.load(pT, pT.t[dc, :, tg * 512:(tg + 1) * 512], s_, s_[:])
                for tt_ in range(4):
                    tok0 = tg * 512 + tt_ * 128
                    if even:
                        p_ = pmm[k % 3]
                        sb_ = stgb[k % 2]
                        k += 1
                        for kc in range(8):
                            b.mm(p_, p_[:], hT, hT[:, kc, tt_ * 128:(tt_ + 1) * 128], win, win[:, kc, 512:1024], start=(kc == 0), stop=(kc == 7))
                        b.cp("act", sb_, sb_[:], p_, p_[:])
                        b.load(vtok, vtok.t[tok0:tok0 + 128, :], sb_, sb_[:])
                        if tg < 2:
                            p_ = pmm[k % 3]
                            sb_ = stgb[k % 2]
                            k += 1
                            for kc in range(8):
                                b.mm(p_, p_[:, 0:256], hT, hT[:, kc, tt_ * 128:(tt_ + 1) * 128], win, win[:, kc, 256:512], start=(kc == 0), stop=(kc == 7))
                            b.cp("act", sb_, sb_[:, 0:256], p_, p_[:, 0:256])
                            b.load(ktok, ktok.t[tok0:tok0 + 128, :], sb_, sb_[:, 0:256])
                    else:
                        p_ = pmm[k % 3]
                        k += 1
                        for kc in range(8):
                            b.mm(p_, p_[:, 0:16], hT, hT[:, kc, tt_ * 128:(tt_ + 1) * 128], win, win[:, kc, 2048:2064], start=(kc == 0), stop=(kc == 7))
                        b.tt("dve", gbs, gbs[:, 0:8], p_, p_[:, 0:8], dtbb, dtbb[:], ALU.add)
                        b.act(gbs, gbs[:, 0:8], gbs, gbs[:, 0:8], AF.Exp)
                        b.act(gbs, gbs[:, 0:8], gbs, gbs[:, 0:8], AF.Ln, bias=onec[:, 0:1], extra=[onec])
                        b.tt("dve", gbs, gbs[:, 0:8], gbs, gbs[:, 0:8], nega, nega[:], ALU.mult)
                        b.act(gbs, gbs[:, 8:16], p_, p_[:, 8:16], AF.Sigmoid)
                        b.load(gbtok, gbtok.t[tok0:tok0 + 128, :], gbs, gbs[:])
        if STOP == "P2":
            break
        if even:
            even_mixer(b, nc, dr, i, pT, vtok, ktok, qkr, yTd, nsr, ident, ones, epsc, outs_written)
        else:
            odd_mixer(b, nc, dr, i, pT, gbtok, qkvT, ktok32, vtok32, oTd, yTd, nsg, ident, ones, epsc, onec, outs_written)
        peer_phase(b, nc, dr, l, i, even, xT, yTd, modT, A2, ident, ones, epsc, norm_mod, xview, do_peer=(STOP != "%da" % l), scr=scr)
        if STOP in ("%da" % l, "%db" % l):
            break

    with b.phase() as ph:
        xg = b.sb(ph, [128, 8, 512], F32, "xg")
        hf = b.sb(ph, [128, 8, 512], F32, "hf")
        if STOP:
            dbg = b.dram("dbg", [8, 128, NT], F32, "ExternalOutput")
            for tg in range(6):
                b.load(xg, xg[:], xT, xview(xT, tg * 512, 512))
                b.load(dbg, xview(dbg, tg * 512, 512), xg, xg[:])
            b.load(dmod, dmod.t, modT, modT[:].rearrange("p a b c -> p (a b c)"))
            b.final_wait([dbg, dmod, pT, yTd])
        sq = b.sb(ph, [128, 8, 512], F32, "sq")
        tmp = b.sb(ph, [128, 512], F32, "tmp")
        rstd = b.sb(ph, [128, 512], F32, "rstd")
        psn = b.ps(ph, [128, 512], F32, "psn")
        for tg in range(6):
            b.load(xg, xg[:], xT, xview(xT, tg * 512, 512))
            norm_mod(ph, xg, 512, lambda c: gfin[:, c:c + 1], None, hf=hf, tmp=tmp, sq=sq, ps_=psn, rstd=rstd)
            b.load(yT, xview(yT, tg * 512, 512), hf, hf[:])
    b.final_wait([yT, nsr, nsg])
    es0.close()
    return nc


def even_mixer(b, nc, dr, i, pT, vtok, ktok, qkr, yTd, nsr, ident, ones, epsc, outs_written):
    with b.phase() as ph:
        x1 = b.sb(ph, [128, 512], F32, "x1")
        x2 = b.sb(ph, [128, 512], F32, "x2")
        cs = b.sb(ph, [128, LS], F32, "cos")
        sn = b.sb(ph, [128, LS], F32, "sin")
        t1 = b.sb(ph, [128, 512], F32, "t1")
        t2 = b.sb(ph, [128, 512], F32, "t2")
        o1 = b.sb(ph, [128, 512], BF16, "o1")
        o2 = b.sb(ph, [128, 512], BF16, "o2")
        b.load(cs, cs[:], dr["cos"], dr["cos"].t)
        b.load(sn, sn[:], dr["sin"], dr["sin"].t)
        for tg in range(6):
            lat = tg >= 2
            pos0 = (tg - 2) * 512
            for qk in range(2):
                b.load(x1, x1[:], pT, pT.t[qk * 2, :, tg * 512:(tg + 1) * 512])
                b.load(x2, x2[:], pT, pT.t[qk * 2 + 1, :, tg * 512:(tg + 1) * 512])
                if lat:
                    c_ = cs[:, pos0:pos0 + 512]
                    s_ = sn[:, pos0:pos0 + 512]
                    b.tt("dve", t1, t1[:], x1, x1[:], cs, c_, ALU.mult)
                    b.tt("pool", t2, t2[:], x2, x2[:], sn, s_, ALU.mult)
                    b.tt("dve", o1, o1[:], t1, t1[:], t2, t2[:], ALU.subtract)
                    b.tt("dve", t1, t1[:], x2, x2[:], cs, c_, ALU.mult)
                    b.tt("pool", t2, t2[:], x1, x1[:], sn, s_, ALU.mult)
                    b.tt("dve", o2, o2[:], t1, t1[:], t2, t2[:], ALU.add)
                else:
                    b.cp("dve", o1, o1[:], x1, x1[:])
                    b.cp("pool", o2, o2[:], x2, x2[:])
                b.load(qkr, qkr.t[qk * 2, :, tg * 512:(tg + 1) * 512], o1, o1[:])
                b.load(qkr, qkr.t[qk * 2 + 1, :, tg * 512:(tg + 1) * 512], o2, o2[:])
    with b.phase() as ph:
        lgt = b.sb(ph, [128, 8], F32, "lgt")
        nlg = b.sb(ph, [128, 8], F32, "nlg")
        hm = b.sb(ph, [128, 4], F32, "hm")
        retg = b.sb(ph, [128, 2, 4], F32, "retg")
        dtab = b.sb(ph, [128, 3968], F32, "dtab")
        ta = b.sb(ph, [128, 3968], F32, "ta")
        tb = b.sb(ph, [128, 3968], F32, "tb")
        Th = b.sb(ph, [128, 3968], F32, "Th")
        posf = b.sb(ph, [128, LS], F32, "posf")
        posb = b.sb(ph, [128, LS], F32, "posb")
        wfr = b.sb(ph, [128, LS], F32, "wfr")
        wbr = b.sb(ph, [128, LS], F32, "wbr")
        posst = b.sb(ph, [128, 2, 2], F32, "posst")
        wst = b.sb(ph, [128, 2, 2], F32, "wst")
        Q = b.sb(ph, [128, 2, LS], BF16, "Q")
        K = b.sb(ph, [128, 2, LS], BF16, "K")
        Kh = b.sb(ph, [128, 2, LS], BF16, "Kh")
        V = b.sb(ph, [128, 16, 128], BF16, "V")
        Vs = b.sb(ph, [128, 2, 2, 128], BF16, "Vs")
        kt = b.sb(ph, [128, 2, 256], BF16, "kt")
        S0 = b.sb(ph, [128, 2, 2, 128], BF16, "S0")
        S0f = b.sb(ph, [128, 2, 2, 128], F32, "S0f")
        Sm = [b.sb(ph, [128, 512], BF16, "Sm") for _ in range(2)]
        pst = [b.ps(ph, [128, 512], F32, "pst") for _ in range(2)]
        po = b.ps(ph, [128, 512], F32, "po")
        pc = [b.ps(ph, [128, 512], F32, "pc") for _ in range(2)]
        pn = b.ps(ph, [128, 512], F32, "pn")
        pss = b.ps(ph, [128, 512], F32, "pss")
        o = b.sb(ph, [128, 512], F32, "o")
        osq = b.sb(ph, [128, 512], F32, "osq")
        rs = b.sb(ph, [128, 512], F32, "rs")
        tq = b.sb(ph, [128, 512], F32, "tq")
        gch = b.sb(ph, [128, 512], F32, "gch")
        ya = b.sb(ph, [128, 512], BF16, "ya")
        sts = b.sb(ph, [128, 128], F32, "sts")
        b.load(lgt, lgt[:], dr["lgam"], dr["lgam"].t[i:i + 1, :].partition_broadcast(128))
        b.load(hm, hm[:], dr["hm"], dr["hm"].t)
        b.load(retg, retg[:], dr["retgT"], dr["retgT"].t)
        b.load(dtab, dtab[:], dr["dtab"], dr["dtab"].t)
        b.load(posf, posf[:], dr["posf"], dr["posf"].t)
        b.load(posb, posb[:], dr["posb"], dr["posb"].t)
        b.load(posst, posst[:], dr["posst"], dr["posst"].t)
        b.act(nlg, nlg[:], lgt, lgt[:], AF.Exp, scale=-1.0)
        b.ts("dve", nlg, nlg[:], nlg, nlg[:], 1.0, ALU.add)
        b.act(nlg, nlg[:], nlg, nlg[:], AF.Ln)
        b.ts("dve", lgt, lgt[:], nlg, nlg[:], -1.0, ALU.mult)
        for h in range(4):
            lf = lgt[:, h:h + 1]
            lb = lgt[:, 4 + h:5 + h]
            nlb = nlg[:, 4 + h:5 + h]
            b.ts("dve", ta, ta[:], dtab, dtab[:], 0.0, ALU.max)
            b.act(ta, ta[:], ta, ta[:], AF.Exp, scale=lf, extra=[lgt])
            b.ts("dve", tb, tb[:], dtab, dtab[:], 0.0, ALU.is_ge)
            b.tt("dve", Th, Th[:], ta, ta[:], tb, tb[:], ALU.mult)
            b.ts("dve", ta, ta[:], dtab, dtab[:], 0.0, ALU.min)
            b.act(ta, ta[:], ta, ta[:], AF.Exp, scale=nlb, extra=[nlg])
            b.ts("dve", tb, tb[:], dtab, dtab[:], 0.0, ALU.is_le)
            b.tt("dve", ta, ta[:], ta, ta[:], tb, tb[:], ALU.mult)
            b.tt("dve", Th, Th[:], Th, Th[:], ta, ta[:], ALU.add)
            b.act(wfr, wfr[:], posf, posf[:], AF.Exp, scale=lf, extra=[lgt])
            b.act(wbr, wbr[:], posb, posb[:], AF.Exp, scale=lb, extra=[lgt])
            b.act(wst, wst[:, :, 0], posst, posst[:, :, 0], AF.Exp, scale=lf, extra=[lgt])
            b.act(wst, wst[:, :, 1], posst, posst[:, :, 1], AF.Exp, scale=lb, extra=[lgt])
            b.ts("dve", wst, wst[:], wst, wst[:], 0.125, ALU.mult)
            b.memset("pool", S0f, S0f[:], 0.0)
            for d in range(2):
                for s in range(2):
                    b.load(S0f, S0f[h * 32:(h + 1) * 32, d, s, :], dr["sret"], dr["sret"].t[i, d, h, s * 32:(s + 1) * 32, :])
            b.cp("pool", S0, S0[:], S0f, S0f[:])
            for si, (off, L, lat) in enumerate(SEQS):
                nj = L // 128
                IG = min(512, L)
                b.load(Q, Q[:, :, 0:L], qkr, qkr.t[0:2, :, off:off + L].rearrange("c p t -> p c t"))
                b.load(K, K[:, :, 0:L], qkr, qkr.t[2:4, :, off:off + L].rearrange("c p t -> p c t"))
                b.load(V, V[:, 0:nj, :], vtok, vtok.t[off:off + L, h * 128:(h + 1) * 128].rearrange("(j p) e -> p j e", p=128))
                b.ts("dve", Kh, Kh[:, :, 0:L], K, K[:, :, 0:L], hm[:, h:h + 1], ALU.mult, extra=[hm])
                it = 0
                for ig in range(L // IG):
                    i0 = ig * IG
                    for jt in range(nj):
                        p_ = pst[it % 2]
                        s_ = Sm[it % 2]
                        it += 1
                        b.mm(p_, p_[:, 0:IG], Kh, Kh[:, 0, jt * 128:(jt + 1) * 128], Q, Q[:, 0, i0:i0 + IG], start=True, stop=False)
                        b.mm(p_, p_[:, 0:IG], Kh, Kh[:, 1, jt * 128:(jt + 1) * 128], Q, Q[:, 1, i0:i0 + IG], start=False, stop=True)
                        m0 = i0 - jt * 128 + 1920
                        b.tt("dve", s_, s_[:, 0:IG], p_, p_[:, 0:IG], Th, Th[:, m0:m0 + IG], ALU.mult)
                        b.mm(po, po[:, 0:IG], V, V[:, jt, :], s_, s_[:, 0:IG], start=(jt == 0), stop=(jt == nj - 1))
                    b.cp("act", o, o[:, 0:IG], po, po[:, 0:IG])
                    if lat:
                        for d, wr in ((0, wfr), (1, wbr)):
                            b.mm(pc[d], pc[d][:, 0:IG], S0, S0[:, d, 0, :], Q, Q[:, 0, i0:i0 + IG], start=True, stop=False)
                            b.mm(pc[d], pc[d][:, 0:IG], S0, S0[:, d, 1, :], Q, Q[:, 1, i0:i0 + IG], start=False, stop=True)
                            b.tt("dve", tq, tq[:, 0:IG], pc[d], pc[d][:, 0:IG], wr, wr[:, i0:i0 + IG], ALU.mult)
                            b.tt("dve", o, o[:, 0:IG], o, o[:, 0:IG], tq, tq[:, 0:IG], ALU.add)
                    b.act(osq, osq[:, 0:IG], o, o[:, 0:IG], AF.Square)
                    b.mm(pn, pn[:, 0:IG], ones, ones[:], osq, osq[:, 0:IG])
                    b.act(rs, rs[:, 0:IG], pn, pn[:, 0:IG], AF.Ln, scale=1.0 / 128.0, bias=epsc[:, 0:1], extra=[epsc])
                    b.act(rs, rs[:, 0:IG], rs, rs[:, 0:IG], AF.Exp, scale=-0.5)
                    b.load(gch, gch[:, 0:IG], pT, pT.t[8 + h, :, off + i0:off + i0 + IG])
                    b.act(gch, gch[:, 0:IG], gch, gch[:, 0:IG], AF.Silu)
                    b.tt("dve", o, o[:, 0:IG], o, o[:, 0:IG], rs, rs[:, 0:IG], ALU.mult)
                    b.stt(ya, ya[:, 0:IG], o, o[:, 0:IG], retg[:, i, h:h + 1], gch, gch[:, 0:IG], ALU.mult, ALU.mult, extra=[retg])
                    b.load(yTd, yTd.t[h, :, off + i0:off + i0 + IG], ya, ya[:, 0:IG])
                if not lat:
                    b.load(kt, kt[:], ktok, ktok.t[off:off + L, :].rearrange("(j p) c -> p j c", p=128))
                    for d in range(2):
                        for jt in range(2):
                            b.ts("dve", Vs, Vs[:, d, jt, :], V, V[:, jt, :], wst[:, jt, d:d + 1], ALU.mult, extra=[wst])
                    for d in range(2):
                        for s in range(2):
                            for jt in range(2):
                                b.mm(pss, pss[:, 0:128], kt, kt[:, jt, s * 128:(s + 1) * 128], Vs, Vs[:, d, jt, :], start=(jt == 0), stop=(jt == 1))
                            b.cp("act", sts, sts[:], pss, pss[:, 0:128])
                            b.load(nsr, nsr.t[si, i, d, h, s * 32:(s + 1) * 32, :], sts, sts[h * 32:(h + 1) * 32, :])
    with b.phase() as ph:
        scw = b.sb(ph, [128, 2, 4, 3], F32, "scw")
        b.load(scw, scw[:], dr["scwT"], dr["scwT"].t)
        bg = b.sb(ph, [128, LS], F32, "bg")
        cg = b.sb(ph, [128, LS], F32, "cg")
        hb_ = b.sb(ph, [128, LS], F32, "hb")
        up = b.sb(ph, [128, LS + 2], F32, "up")
        acc = b.sb(ph, [128, LS], F32, "acc")
        yb = b.sb(ph, [128, LS], BF16, "yb")
        for (off, L, lat) in SEQS:
            for cb in range(4):
                b.load(bg, bg[:, 0:L], pT, pT.t[12 + cb, :, off:off + L])
                b.load(cg, cg[:, 0:L], pT, pT.t[16 + cb, :, off:off + L])
                b.load(hb_, hb_[:, 0:L], pT, pT.t[20 + cb, :, off:off + L])
                b.memset("pool", up, up[:, 0:1], 0.0)
                b.memset("pool", up, up[:, L + 1:L + 2], 0.0)
                b.tt("pool", up, up[:, 1:L + 1], cg, cg[:, 0:L], hb_, hb_[:, 0:L], ALU.mult)
                b.ts("dve", acc, acc[:, 0:L], up, up[:, 0:L], scw[:, i, cb, 0:1], ALU.mult, extra=[scw])
                b.stt(acc, acc[:, 0:L], up, up[:, 1:L + 1], scw[:, i, cb, 1:2], acc, acc[:, 0:L], ALU.mult, ALU.add, extra=[scw])
                b.stt(acc, acc[:, 0:L], up, up[:, 2:L + 2], scw[:, i, cb, 2:3], acc, acc[:, 0:L], ALU.mult, ALU.add, extra=[scw])
                b.tt("dve", yb, yb[:, 0:L], acc, acc[:, 0:L], bg, bg[:, 0:L], ALU.mult)
                b.load(yTd, yTd.t[4 + cb, :, off:off + L], yb, yb[:, 0:L])


def odd_mixer(b, nc, dr, i, pT, gbtok, qkvT, ktok32, vtok32, oTd, yTd, nsg, ident, ones, epsc, onec, outs_written):
    with b.phase() as ph:
        gcw = b.sb(ph, [128, 2, 12, 3], F32, "gcw")
        b.load(gcw, gcw[:], dr["gcwT"], dr["gcwT"].t)
        xp = b.sb(ph, [128, LS + 2], F32, "xp")
        acc = b.sb(ph, [128, LS], F32, "acc")
        sq = b.sb(ph, [128, 512], F32, "sq")
        rs = b.sb(ph, [128, 512], F32, "rs")
        pn = b.ps(ph, [128, 512], F32, "pn")
        ptr = b.ps(ph, [128, 512], F32, "ptr")
        tk = b.sb(ph, [128, 512], F32, "tk")
        for (off, L, lat) in SEQS:
            for ch in range(12):
                b.memset("pool", xp, xp[:, 0:1], 0.0)
                b.memset("pool", xp, xp[:, L + 1:L + 2], 0.0)
                b.load(xp, xp[:, 1:L + 1], pT, pT.t[ch, :, off:off + L])
                b.ts("dve", acc, acc[:, 0:L], xp, xp[:, 0:L], gcw[:, i, ch, 0:1], ALU.mult, extra=[gcw])
                b.stt(acc, acc[:, 0:L], xp, xp[:, 1:L + 1], gcw[:, i, ch, 1:2], acc, acc[:, 0:L], ALU.mult, ALU.add, extra=[gcw])
                b.stt(acc, acc[:, 0:L], xp, xp[:, 2:L + 2], gcw[:, i, ch, 2:3], acc, acc[:, 0:L], ALU.mult, ALU.add, extra=[gcw])
                b.act(acc, acc[:, 0:L], acc, acc[:, 0:L], AF.Silu)
                W = min(512, L)
                for pc_ in range(L // W):
                    sl = slice(pc_ * W, (pc_ + 1) * W)
                    if ch < 8:
                        b.act(sq, sq[:, 0:W], acc, acc[:, sl], AF.Square)
                        b.mm(pn, pn[:, 0:W], ones, ones[:], sq, sq[:, 0:W])
                        b.act(rs, rs[:, 0:W], pn, pn[:, 0:W], AF.Ln, bias=epsc[:, 0:1], extra=[epsc])
                        b.act(rs, rs[:, 0:W], rs, rs[:, 0:W], AF.Exp, scale=-0.5)
                        if ch < 4:
                            b.stt(acc, acc[:, sl], acc, acc[:, sl], 128.0 ** -0.5, rs, rs[:, 0:W], ALU.mult, ALU.mult)
                        else:
                            b.tt("dve", acc, acc[:, sl], acc, acc[:, sl], rs, rs[:, 0:W], ALU.mult)
                    if ch >= 4:
                        dst = ktok32 if ch < 8 else vtok32
                        hh = ch % 4
                        nt_ = W // 128
                        for t_ in range(nt_):
                            b.tr(ptr, ptr[:, t_ * 128:(t_ + 1) * 128], acc, acc[:, pc_ * W + t_ * 128: pc_ * W + (t_ + 1) * 128], ident)
                        b.cp("act", tk, tk[:, 0:W], ptr, ptr[:, 0:W])
                        tok0 = off + pc_ * W
                        b.load(dst, dst.t[tok0:tok0 + W, hh * 128:(hh + 1) * 128].rearrange("(t p) e -> p t e", p=128),
                               tk, tk[:, 0:W].rearrange("p (t e) -> p t e", e=128))
                b.load(qkvT, qkvT.t[ch, :, off:off + L], acc, acc[:, 0:L])
    with b.phase() as ph:
        Mf = b.sb(ph, [128, 128], F32, "Mf")
        Mb = b.sb(ph, [128, 128], F32, "Mb")
        Sf = b.sb(ph, [128, 128], F32, "Sf")
        Sb = b.sb(ph, [128, 128], F32, "Sb")
        for t_, k_ in ((Mf, "Mf"), (Mb, "Mb"), (Sf, "Sf"), (Sb, "Sb")):
            b.load(t_, t_[:], dr[k_], dr[k_].t)

        def t4(nm):
            return b.sb(ph, [128, 4, 128], F32, nm)
        qTt, kTt, ktk, vtk = t4("qTt"), t4("kTt"), t4("ktk"), t4("vtk")
        gb = b.sb(ph, [128, 16], F32, "gb")
        gc = b.sb(ph, [128, 4], F32, "gc")
        gl = b.sb(ph, [128, 4], F32, "gl")
        egl = b.sb(ph, [128, 4], F32, "egl")
        eg = b.sb(ph, [128, 4], F32, "eg")
        egd = b.sb(ph, [128, 4], F32, "egd")
        bge = b.sb(ph, [128, 4], F32, "bge")
        gB = [b.sb(ph, [128, 128], F32, "gB") for _ in range(2)]
        nd, dec, expbc, dincl, dstr = t4("nd"), t4("dec"), t4("expbc"), t4("dincl"), t4("dstr")
        A, AT, attn, attnT = t4("A"), t4("AT"), t4("attn"), t4("attnT")
        Pa, PTa, Pb, PTb, TT, Tm = t4("Pa"), t4("PTa"), t4("Pb"), t4("PTb"), t4("TT"), t4("Tm")
        Elo = b.sb(ph, [128, 7, 128], F32, "Elo")
        Eup = b.sb(ph, [128, 7, 128], F32, "Eup")
        b.load(Elo, Elo[:], dr["Elo"], dr["Elo"].t)
        b.load(Eup, Eup[:], dr["Eup"], dr["Eup"].t)
        kbg, vb, kd, u, wT, qgT, vnew, S, ost = t4("kbg"), t4("vb"), t4("kd"), t4("u"), t4("wT"), t4("qgT"), t4("vnew"), t4("S"), t4("ost")
        pS = b.ps(ph, [128, 512], F32, "pS")
        P_ = [b.ps(ph, [128, 512], F32, "pg") for _ in range(7)]

        def hs(t_, h):
            return t_[:, h, :]

        def ph_(p, h):
            return p[:, h * 128:(h + 1) * 128]

        def p3(p):
            return p[:].rearrange("p (h f) -> p h f", h=4)

        for si, (off, L, lat) in enumerate(SEQS):
            nt_ = L // 128
            for d in range(2):
                Mtri = Mf if d == 0 else Mb
                incl = Mb if d == 0 else Mf
                strict = Sb if d == 0 else Sf
                if lat:
                    b.load(S, S[:], dr["sgdn"], dr["sgdn"].t[i, d].rearrange("h k v -> k h v"))
                else:
                    b.memset("pool", S, S[:], 0.0)
                order = range(nt_) if d == 0 else range(nt_ - 1, -1, -1)
                for ti in order:
                    tok0 = off + ti * 128
                    b.load(qTt, qTt[:], qkvT, qkvT.t[0:4, :, tok0:tok0 + 128].rearrange("h p t -> p h t"))
                    b.load(kTt, kTt[:], qkvT, qkvT.t[4:8, :, tok0:tok0 + 128].rearrange("h p t -> p h t"))
                    b.load(ktk, ktk[:], ktok32, ktok32.t[tok0:tok0 + 128, :].rearrange("p (h e) -> p h e", h=4))
                    b.load(vtk, vtk[:], vtok32, vtok32.t[tok0:tok0 + 128, :].rearrange("p (h e) -> p h e", h=4))
                    b.load(gb, gb[:], gbtok, gbtok.t[tok0:tok0 + 128, :])
                    gcol = gb[:, d * 4:d * 4 + 4]
                    bcol = gb[:, 8 + d * 4:8 + d * 4 + 4]
                    b.mm(pS, pS[:, 0:4], Mtri, Mtri[:], gb, gcol)
                    b.mm(pS, pS[:, 4:8], ones, ones[:], gb, gcol)
                    b.cp("dve", gc, gc[:], pS, pS[:, 0:4])
                    b.cp("dve", gl, gl[:], pS, pS[:, 4:8])
                    b.act(egl, egl[:], gl, gl[:], AF.Exp)
                    b.act(eg, eg[:], gc, gc[:], AF.Exp)
                    b.tt("dve", egd, egd[:], gl, gl[:], gc, gc[:], ALU.subtract)
                    b.act(egd, egd[:], egd, egd[:], AF.Exp)
                    b.tt("dve", bge, bge[:], eg, eg[:], gb, bcol, ALU.mult)
                    for h in range(4):
                        g_ = gB[h % 2]
                        b.ts("pool", g_, g_[:], ones, ones[:], gb[:, d * 4 + h:d * 4 + h + 1], ALU.mult, extra=[gb])
                        b.mm(P_[0], ph_(P_[0], h), g_, g_[:], Mtri, Mtri[:])
                    for h in range(4):
                        b.ts("dve", nd, hs(nd, h), P_[0], ph_(P_[0], h), gc[:, h:h + 1], ALU.subtract, 0.0, ALU.max, extra=[gc])
                    b.act(dec, dec[:], nd, nd[:], AF.Exp, scale=-1.0)
                    b.act(expbc, expbc[:], P_[0], p3(P_[0]), AF.Exp)
                    b.tt("pool", dincl, dincl[:], dec, dec[:], incl, incl[:].unsqueeze(1).to_broadcast([128, 4, 128]), ALU.mult)
                    b.tt("pool", dstr, dstr[:], dec, dec[:], strict, strict[:].unsqueeze(1).to_broadcast([128, 4, 128]), ALU.mult)
                    for h in range(4):
                        b.mm(P_[1], ph_(P_[1], h), kTt, hs(kTt, h), kTt, hs(kTt, h))
                        b.mm(P_[2], ph_(P_[2], h), qTt, hs(qTt, h), kTt, hs(kTt, h))
                    for h in range(4):
                        b.stt(A, hs(A, h), P_[1], ph_(P_[1], h), gb[:, 8 + d * 4 + h:8 + d * 4 + h + 1], dstr, hs(dstr, h), ALU.mult, ALU.mult, extra=[gb])
                    b.tt("dve", attn, attn[:], P_[2], p3(P_[2]), dincl, dincl[:], ALU.mult)
                    for h in range(4):
                        b.tr(P_[3], ph_(P_[3], h), A, hs(A, h), ident)
                        b.tr(P_[4], ph_(P_[4], h), attn, hs(attn, h), ident)
                    b.cp("act", AT, AT[:], P_[3], p3(P_[3]))
                    b.cp("act", attnT, attnT[:], P_[4], p3(P_[4]))
                    b.cp("pool", Tm, Tm[:], ident, ident[:].unsqueeze(1).to_broadcast([128, 4, 128]))
                    b.cp("pool", TT, TT[:], ident, ident[:].unsqueeze(1).to_broadcast([128, 4, 128]))
                    EA_t = Elo if d == 0 else Eup
                    EAT_t = Eup if d == 0 else Elo
                    for lv in range(7):
                        b.tt("pool", Pa, Pa[:], A, A[:], EA_t, EA_t[:, lv, :].unsqueeze(1).to_broadcast([128, 4, 128]), ALU.mult)
                        b.tt("pool", PTa, PTa[:], AT, AT[:], EAT_t, EAT_t[:, lv, :].unsqueeze(1).to_broadcast([128, 4, 128]), ALU.mult)
                        for h in range(4):
                            b.mm(P_[1], ph_(P_[1], h), PTa, hs(PTa, h), Tm, hs(Tm, h))
                            b.mm(P_[2], ph_(P_[2], h), Pa, hs(Pa, h), TT, hs(TT, h))
                        b.cp("act", Pb, Pb[:], P_[1], p3(P_[1]))
                        b.cp("dve", PTb, PTb[:], P_[2], p3(P_[2]))
                        for h in range(4):
                            b.mm(P_[0], ph_(P_[0], h), TT, hs(TT, h), Pb, hs(Pb, h))
                            b.mm(P_[3], ph_(P_[3], h), Tm, hs(Tm, h), PTb, hs(PTb, h))
                        b.tt("dve", Tm, Tm[:], Tm, Tm[:], P_[0], p3(P_[0]), ALU.subtract)
                        b.tt("dve", TT, TT[:], TT, TT[:], P_[3], p3(P_[3]), ALU.subtract)
                    b.tt("pool", kbg, kbg[:], ktk, ktk[:], bge, bge[:].unsqueeze(2).to_broadcast([128, 4, 128]), ALU.mult)
                    b.tt("pool", vb, vb[:], vtk, vtk[:], gb, bcol.unsqueeze(2).to_broadcast([128, 4, 128]), ALU.mult)
                    b.tt("pool", kd, kd[:], ktk, ktk[:], egd, egd[:].unsqueeze(2).to_broadcast([128, 4, 128]), ALU.mult)
                    b.tt("dve", qgT, qgT[:], qTt, qTt[:], expbc, expbc[:], ALU.mult)
                    for h in range(4):
                        b.mm(P_[3], ph_(P_[3], h), TT, hs(TT, h), vb, hs(vb, h))
                        b.mm(P_[4], ph_(P_[4], h), kbg, hs(kbg, h), TT, hs(TT, h))
                    b.cp("act", u, u[:], P_[3], p3(P_[3]))
                    b.cp("act", wT, wT[:], P_[4], p3(P_[4]))
                    for h in range(4):
                        b.mm(P_[5], ph_(P_[5], h), wT, hs(wT, h), S, hs(S, h))
                    b.tt("dve", vnew, vnew[:], u, u[:], P_[5], p3(P_[5]), ALU.subtract)
                    for h in range(4):
                        b.mm(P_[6], ph_(P_[6], h), S, hs(S, h), qgT, hs(qgT, h), start=True, stop=False)
                        b.mm(P_[6], ph_(P_[6], h), vnew, hs(vnew, h), attnT, hs(attnT, h), start=False, stop=True)
                    for h in range(4):
                        b.mm(P_[5], ph_(P_[5], h), kd, hs(kd, h), vnew, hs(vnew, h))
                    b.cp("act", ost, ost[:], P_[6], p3(P_[6]))
                    b.load(oTd, oTd.t[d, :, :, tok0:tok0 + 128].rearrange("h p t -> p h t"), ost, ost[:])
                    for h in range(4):
                        b.stt(S, hs(S, h), S, hs(S, h), egl[:, h:h + 1], P_[5], ph_(P_[5], h), ALU.mult, ALU.add, extra=[egl])
                if not lat:
                    b.load(nsg, nsg.t[si, i, d].rearrange("h k v -> k h v"), S, S[:])
    with b.phase() as ph:
        gng = b.sb(ph, [128, 2], F32, "gng")
        cfw = b.sb(ph, [128, 2, 4, 31], F32, "cfw")
        cfb = b.sb(ph, [128, 2, 4], F32, "cfb")
        lng = b.sb(ph, [128, 2, 4], F32, "lng")
        lnb = b.sb(ph, [128, 2, 4], F32, "lnb")
        for t_, k_ in ((gng, "gdngT"), (cfw, "cfwT"), (cfb, "cfbT"), (lng, "lngT"), (lnb, "lnbT")):
            b.load(t_, t_[:], dr[k_], dr[k_].t)
        of = b.sb(ph, [128, 512], F32, "of")
        ob = b.sb(ph, [128, 512], F32, "ob")
        zz = b.sb(ph, [128, 512], F32, "zz")
        sq = b.sb(ph, [128, 512], F32, "sq")
        rs = b.sb(ph, [128, 512], F32, "rs")
        yc = b.sb(ph, [128, 512], BF16, "yc")
        pn = b.ps(ph, [128, 512], F32, "pn")
        pv = b.ps(ph, [128, 512], F32, "pv")
        for tg in range(6):
            cs_ = slice(tg * 512, (tg + 1) * 512)
            for h in range(4):
                b.load(of, of[:], oTd, oTd.t[0, h, :, cs_])
                b.load(ob, ob[:], oTd, oTd.t[1, h, :, cs_])
                b.load(zz, zz[:], pT, pT.t[12 + h, :, cs_])
                b.tt("dve", of, of[:], of, of[:], ob, ob[:], ALU.add)
                b.act(sq, sq[:], of, of[:], AF.Square)
                b.mm(pn, pn[:], ones, ones[:], sq, sq[:])
                b.act(rs, rs[:], pn, pn[:], AF.Ln, scale=1.0 / 128.0, bias=epsc[:, 0:1], extra=[epsc])
                b.act(rs, rs[:], rs, rs[:], AF.Exp, scale=-0.5)
                b.act(zz, zz[:], zz, zz[:], AF.Silu)
                b.tt("dve", of, of[:], of, of[:], rs, rs[:], ALU.mult)
                b.stt(yc, yc[:], of, of[:], gng[:, i:i + 1], zz, zz[:], ALU.mult, ALU.mult, extra=[gng])
                b.load(yTd, yTd.t[h, :, cs_], yc, yc[:])
        ca = b.sb(ph, [128, LS], F32, "ca")
        cg = b.sb(ph, [128, LS], F32, "cg")
        hp = b.sb(ph, [128, LS + 30], F32, "hp")
        cv = b.sb(ph, [128, 4, LS], F32, "cv")
        mean = b.sb(ph, [128, 512], F32, "mean")
        xc = b.sb(ph, [128, 4, 512], F32, "xc")
        sq4 = b.sb(ph, [128, 4, 512], F32, "sq4")
        for (off, L, lat) in SEQS:
            for cb in range(4):
                b.load(ca, ca[:, 0:L], pT, pT.t[16 + cb, :, off:off + L])
                b.load(cg, cg[:, 0:L], pT, pT.t[20 + cb, :, off:off + L])
                b.memset("pool", hp, hp[:, 0:15], 0.0)
                b.memset("pool", hp, hp[:, L + 15:L + 30], 0.0)
                b.act(cg, cg[:, 0:L], cg, cg[:, 0:L], AF.Sigmoid)
                b.tt("pool", hp, hp[:, 15:L + 15], ca, ca[:, 0:L], cg, cg[:, 0:L], ALU.mult)
                b.ts("dve", cv, cv[:, cb, 0:L], hp, hp[:, 0:L], cfw[:, i, cb, 0:1], ALU.mult, cfb[:, i, cb:cb + 1], ALU.add, extra=[cfw, cfb])
                for k in range(1, 31):
                    b.stt(cv, cv[:, cb, 0:L], hp, hp[:, k:k + L], cfw[:, i, cb, k:k + 1], cv, cv[:, cb, 0:L], ALU.mult, ALU.add, extra=[cfw])
            W = min(512, L)
            for pc_ in range(L // W):
                sl = slice(pc_ * W, (pc_ + 1) * W)
                for cb in range(4):
                    b.mm(pn, pn[:, 0:W], ones, ones[:], cv, cv[:, cb, sl], start=(cb == 0), stop=(cb == 3))
                b.act(mean, mean[:, 0:W], pn, pn[:, 0:W], AF.Copy, scale=1.0 / 512.0)
                for cb in range(4):
                    b.tt("dve", xc, xc[:, cb, 0:W], cv, cv[:, cb, sl], mean, mean[:, 0:W], ALU.subtract)
                b.act(sq4, sq4[:, :, 0:W], xc, xc[:, :, 0:W], AF.Square)
                for cb in range(4):
                    b.mm(pv, pv[:, 0:W], ones, ones[:], sq4, sq4[:, cb, 0:W], start=(cb == 0), stop=(cb == 3))
                b.act(rs, rs[:, 0:W], pv, pv[:, 0:W], AF.Ln, scale=1.0 / 512.0, bias=epsc[:, 0:1], extra=[epsc])
                b.act(rs, rs[:, 0:W], rs, rs[:, 0:W], AF.Exp, scale=-0.5)
                for cb in range(4):
                    b.tt("dve", xc, xc[:, cb, 0:W], xc, xc[:, cb, 0:W], rs, rs[:, 0:W], ALU.mult)
                    b.act(yc, yc[:, 0:W], xc, xc[:, cb, 0:W], AF.Silu, scale=lng[:, i, cb:cb + 1], bias=lnb[:, i, cb:cb + 1], extra=[lng, lnb])
                    b.load(yTd, yTd.t[4 + cb, :, off + pc_ * W:off + (pc_ + 1) * W], yc, yc[:, 0:W])


def peer_phase(b, nc, dr, l, i, even, xT, yTd, modT, A2, ident, ones, epsc, norm_mod, xview, do_peer=True, scr=None):
    uTb, vbd, ekg, h2bd = scr
    if do_peer:
        peer_cast(b, nc, dr, l, uTb, vbd)
    with b.phase() as ph:
        wout = b.sb(ph, [128, 8, 1024], BF16, "wout")
        wq = b.sb(ph, [128, 8, 2048], BF16, "wq")
        keys = b.sb(ph, [128, 16, 128], BF16, "keys")
        iota16 = b.sb(ph, [128, 16], F32, "iota16")
        wsrc = dr["ev_w_out" if even else "od_w_out"]
        cstg = [b.sb(ph, [128, 1024], F32, "cstg") for _ in range(2)]
        ck = 0
        for kc in range(8):
            b.load_cast(cstg, wout, wout[:, kc, :], wsrc, wsrc.t[i, kc * 128:(kc + 1) * 128, :], 1024, ck)
            ck += 1
            for c0 in (0, 1024):
                b.load_cast(cstg, wq, wq[:, kc, c0:c0 + 1024], dr["peer_wq"], dr["peer_wq"].t[l, kc * 128:(kc + 1) * 128, c0:c0 + 1024], 1024, ck)
                ck += 1
        for c8 in range(2):
            st = cstg[ck % 2]
            ck += 1
            b.load(st, st[:].rearrange("p (c n) -> p c n", c=8), dr["keysT"], dr["keysT"].t[l, c8 * 8:(c8 + 1) * 8].rearrange("c d n -> d c n"))
            b.cp("pool", keys, keys[:, c8 * 8:(c8 + 1) * 8, :], st, st[:].rearrange("p (c n) -> p c n", c=8))
        b.load(iota16, iota16[:], dr["iota16"], dr["iota16"].t)
        thr16 = b.sb(ph, [128, 16], F32, "thr16")
        b.ts("dve", thr16, thr16[:], iota16, iota16[:], 16.0, ALU.mult)
        xn = b.sb(ph, [128, 8, 512], F32, "xn")
        h2f = b.sb(ph, [128, 8, 512], F32, "h2f")
        h2b = b.sb(ph, [128, 8, 512], BF16, "h2b")
        yg, xg, sq = h2b, h2f, h2f
        tmp = b.sb(ph, [128, 512], F32, "tmp")
        rstd = b.sb(ph, [128, 512], F32, "rstd")
        psn = b.ps(ph, [128, 512], F32, "psn")
        pmm = [b.ps(ph, [128, 512], F32, "pmm") for _ in range(2)]
        psc = [b.ps(ph, [128, 512], F32, "psc") for _ in range(4)]
        qT = b.sb(ph, [128, 16, 512], BF16, "qT")
        h2t = b.sb(ph, [128, 1024], F32, "h2t")
        def mkset():
            return (b.sb(ph, [128, 16, 128], F32, "ssb"), b.sb(ph, [128, 256], F32, "wk"), b.sb(ph, [128, 16, 16], F32, "mx"),
                    b.sb(ph, [128, 16, 16], U32, "mi"), b.sb(ph, [128, 16, 16], F32, "mif"), b.sb(ph, [128, 8, 256], F32, "cs"),
                    b.sb(ph, [128, 8, 16], F32, "tsv"), b.sb(ph, [128, 8, 16], U32, "sel"), b.sb(ph, [128, 8, 16], F32, "self"),
                    b.sb(ph, [128, 8, 16], F32, "aq"), b.sb(ph, [128, 8, 16], F32, "bq"), b.sb(ph, [128, 128, 16], F32, "oh"),
                    b.sb(ph, [128, 128], F32, "i1sel"), b.sb(ph, [128, 128], F32, "i2sel"), b.sb(ph, [128, 8, 16], F32, "gt"),
                    b.sb(ph, [128, 8], F32, "zs"))
        bufsets = [mkset() + (psc[0:2],), mkset() + (psc[2:4],)]
        for tg in range(6):
            v = 0 if tg < 2 else 1
            b.load(yg, yg[:], yTd, xview(yTd, tg * 512, 512))
            b.load(xg, xg[:], xT, xview(xT, tg * 512, 512))
            for dc in range(8):
                p_ = pmm[dc % 2]
                for kc in range(8):
                    b.mm(p_, p_[:], wout, wout[:, kc, dc * 128:(dc + 1) * 128], yg, yg[:, kc, :], start=(kc == 0), stop=(kc == 7))
                b.stt(xn, xn[:, dc, :], p_, p_[:], modT[:, l, 16 + dc, v:v + 1], xg, xg[:, dc, :], ALU.mult, ALU.add, extra=[modT])
            if not do_peer:
                b.load(xT, xview(xT, tg * 512, 512), xn, xn[:])
                continue
            norm_mod(ph, xn, 512, lambda c: A2[:, l, c, v:v + 1], lambda c: modT[:, l, 24 + c, v:v + 1], hb=h2b, hf=h2f, tmp=tmp, sq=sq, ps_=psn, rstd=rstd)
            for hp_ in range(16):
                p_ = pmm[hp_ % 2]
                for kc in range(8):
                    b.mm(p_, p_[:], wq, wq[:, kc, hp_ * 128:(hp_ + 1) * 128], h2b, h2b[:, kc, :], start=(kc == 0), stop=(kc == 7))
                b.cp("act", qT, qT[:, hp_, :], p_, p_[:])
            def tile_body(tt_, B):
                (ssb, wk, mx, mi, mif, cs_, tsv, sel, self_, aq, bq, oh, i1sel, i2sel, gt, zs, PSC) = B
                mxv = mx[:].rearrange("p (h t) k -> p h t k", t=2)
                mifv = mif[:].rearrange("p (h t) k -> p h t k", t=2)
                ts_ = slice(tt_ * 128, (tt_ + 1) * 128)
                for h8 in range(2):
                    for hh in range(8):
                        hp_ = h8 * 8 + hh
                        pq = PSC[hh // 4]
                        b.mm(pq, pq[:, (hh % 4) * 128:(hh % 4 + 1) * 128], qT, qT[:, hp_, ts_], keys, keys[:, hp_, :])
                    for q2 in range(2):
                        b.cp("act", ssb, ssb[:, h8 * 8 + q2 * 4:h8 * 8 + (q2 + 1) * 4, :], PSC[q2], PSC[q2][:].rearrange("p (a n) -> p a n", a=4))
                for hp_ in range(16):
                    b.I("dve", lambda hp_=hp_: nc.vector.max(out=mx[:, hp_, 0:8], in_=ssb[:, hp_, :]), r=[ssb], w=[mx])
                    b.I("dve", lambda hp_=hp_: nc.vector.max_index(out=mi[:, hp_, 0:8], in_max=mx[:, hp_, 0:8], in_values=ssb[:, hp_, :]), r=[ssb, mx], w=[mi])
                    b.I("dve", lambda hp_=hp_: nc.vector.match_replace(out=wk[:, 0:128], in_to_replace=mx[:, hp_, 0:8], in_values=ssb[:, hp_, :], imm_value=-1e30), r=[ssb, mx], w=[wk])
                    b.I("dve", lambda hp_=hp_: nc.vector.max(out=mx[:, hp_, 8:16], in_=wk[:, 0:128]), r=[wk], w=[mx])
                    b.I("dve", lambda hp_=hp_: nc.vector.max_index(out=mi[:, hp_, 8:16], in_max=mx[:, hp_, 8:16], in_values=wk[:, 0:128]), r=[wk, mx], w=[mi])
                b.cp("dve", mif, mif[:], mi, mi[:])
                csv = cs_[:].rearrange("p h (a c) -> p h a c", a=16)
                b.tt("dve", cs_, csv, mx, mxv[:, :, 0, :].unsqueeze(3).to_broadcast([128, 8, 16, 16]),
                     mx, mxv[:, :, 1, :].unsqueeze(2).to_broadcast([128, 8, 16, 16]), ALU.add)
                for h in range(8):
                    b.I("dve", lambda h=h: nc.vector.max(out=tsv[:, h, 0:8], in_=cs_[:, h, :]), r=[cs_], w=[tsv])
                    b.I("dve", lambda h=h: nc.vector.max_index(out=sel[:, h, 0:8], in_max=tsv[:, h, 0:8], in_values=cs_[:, h, :]), r=[cs_, tsv], w=[sel])
                    b.I("dve", lambda h=h: nc.vector.match_replace(out=wk[:], in_to_replace=tsv[:, h, 0:8], in_values=cs_[:, h, :], imm_value=-1e30), r=[cs_, tsv], w=[wk])
                    b.I("dve", lambda h=h: nc.vector.max(out=tsv[:, h, 8:16], in_=wk[:]), r=[wk], w=[tsv])
                    b.I("dve", lambda h=h: nc.vector.max_index(out=sel[:, h, 8:16], in_max=tsv[:, h, 8:16], in_values=wk[:]), r=[wk, tsv], w=[sel])
                b.cp("dve", self_, self_[:], sel, sel[:])
                ohv0 = oh[:].rearrange("p (h k) a -> p h k a", h=8)
                b.tt("dve", oh, ohv0, self_, self_[:].unsqueeze(3).to_broadcast([128, 8, 16, 16]),
                     thr16, thr16[:].unsqueeze(1).unsqueeze(1).to_broadcast([128, 8, 16, 16]), ALU.is_ge)
                b.I("dve", lambda: nc.vector.tensor_reduce(out=aq[:].rearrange("p h k -> p (h k)"), in_=oh[:], axis=AX.X, op=ALU.add), r=[oh], w=[aq])
                b.ts("dve", aq, aq[:], aq, aq[:], -1.0, ALU.add)
                b.stt(bq, bq[:], aq, aq[:], -16.0, self_, self_[:], ALU.mult, ALU.add)
                for (qq, half, dst) in ((aq, 0, i1sel), (bq, 1, i2sel)):
                    ohv = oh[:].rearrange("p (h k) a -> p h k a", h=8)
                    b.tt("dve", oh, ohv, qq, qq[:].unsqueeze(3).to_broadcast([128, 8, 16, 16]),
                         iota16, iota16[:].unsqueeze(1).unsqueeze(1).to_broadcast([128, 8, 16, 16]), ALU.is_equal)
                    b.tt("dve", oh, ohv, oh, ohv, mif, mifv[:, :, half, :].unsqueeze(2).to_broadcast([128, 8, 16, 16]), ALU.mult)
                    b.I("dve", lambda dst=dst: nc.vector.tensor_reduce(out=dst[:], in_=oh[:], axis=AX.X, op=ALU.add), r=[oh], w=[dst])
                b.tt("dve", gt, gt[:], tsv, tsv[:], tsv, tsv[:, :, 0:1].to_broadcast([128, 8, 16]), ALU.subtract)
                b.act(gt, gt[:], gt, gt[:], AF.Exp)
                b.I("dve", lambda: nc.vector.tensor_reduce(out=zs[:], in_=gt[:], axis=AX.X, op=ALU.add), r=[gt], w=[zs])
                b.I("dve", lambda: nc.vector.reciprocal(out=zs[:], in_=zs[:]), r=[zs], w=[zs])
                b.tt("dve", gt, gt[:], gt, gt[:], zs, zs[:].unsqueeze(2).to_broadcast([128, 8, 16]), ALU.mult)
                tok0 = tg * 512 + tt_ * 128
                b.load(ekg, ekg.t[0, tok0:tok0 + 128, :], i1sel, i1sel[:])
                b.load(ekg, ekg.t[1, tok0:tok0 + 128, :], i2sel, i2sel[:])
                b.load(ekg, ekg.t[2, tok0:tok0 + 128, :], gt, gt[:].rearrange("p h k -> p (h k)"))

            for t2 in range(2):
                la = b.record(lambda: tile_body(2 * t2, bufsets[0]))
                lb = b.record(lambda: tile_body(2 * t2 + 1, bufsets[1]))
                b.emit_interleaved([la, lb])
            b.load(h2bd, xview(h2bd, tg * 512, 512), h2b, h2b[:])
            b.load(xT, xview(xT, tg * 512, 512), xn, xn[:])
    if do_peer:
        peer_dense(b, nc, dr, l, xT, modT, ident, uTb, vbd, ekg, h2bd, xview)


def peer_cast(b, nc, dr, l, uTb, vbd):
    with b.phase() as ph:
        st = [b.sb(ph, [128, 4, 1024], F32, "cst") for _ in range(3)]
        sb_ = [b.sb(ph, [128, 4, 1024], BF16, "csb") for _ in range(3)]
        k = 0
        engs = ("act", "pool", "dve")
        for (src, dst) in ((dr["peer_u"], uTb), (dr["peer_v"], vbd)):
            for g in range(32):
                f_ = st[k % 3]
                o_ = sb_[k % 3]
                b.load(f_, f_[:], src, src.t[l, g * 4:(g + 1) * 4].rearrange("c p n -> p c n"))
                b.cp(engs[k % 3], o_, o_[:], f_, f_[:])
                b.load(dst, dst.t[g * 4:(g + 1) * 4].rearrange("c p n -> p c n"), o_, o_[:])
                k += 1


def peer_dense(b, nc, dr, l, xT, modT, ident, uTb, vbd, ekg, h2bd, xview):
    TG = 384
    with b.phase() as ph:
        iota = b.sb(ph, [128, 128], F32, "iota128")
        b.load(iota, iota[:], dr["iota128"], dr["iota128"].t)
        GT = b.sb(ph, [128, 128, TG], BF16, "GT")
        h2b = b.sb(ph, [128, 8, TG], BF16, "h2b")
        xn = b.sb(ph, [128, 8, TG], F32, "xn")
        ek = b.sb(ph, [128, 3, 128], F32, "ek")
        ET = b.sb(ph, [128, 3, 128], F32, "ET")
        ohA = [b.sb(ph, [128, 16, 128], BF16, "ohA") for _ in range(2)]
        ohB = [b.sb(ph, [128, 16, 128], BF16, "ohB") for _ in range(2)]
        ohT = [b.sb(ph, [128, 16, 128], BF16, "ohT") for _ in range(2)]
        NW = 4
        Uc = [b.sb(ph, [128, 1024], BF16, "Uc") for _ in range(NW)]
        Vc = [b.sb(ph, [128, 1024], BF16, "Vc") for _ in range(NW)]
        gel = [b.sb(ph, [128, TG], BF16, "gel") for _ in range(4)]
        AT = [b.sb(ph, [128, TG], BF16, "AT") for _ in range(4)]
        fo = b.sb(ph, [128, 1024], F32, "fo")
        pss = [b.ps(ph, [128, 512], F32, "pss") for _ in range(2)]
        pg = pss
        po = [b.ps(ph, [128, 512], F32, "po") for _ in range(2 * (TG // 128))]
        for grp in range(NT // TG):
            g0 = grp * TG
            v = 0 if g0 < NPR else 1
            b.load(h2b, h2b[:], h2bd, xview(h2bd, g0, TG))
            b.load(xn, xn[:], xT, xview(xT, g0, TG))
            kq = 0
            for tl in range(TG // 128):
                tok0 = g0 + tl * 128
                b.load(ek, ek[:], ekg, ekg.t[:, tok0:tok0 + 128, :].rearrange("c t s -> t c s"))
                for c3 in range(3):
                    b.tr(pg[0], pg[0][:, c3 * 128:(c3 + 1) * 128], ek, ek[:, c3, :], ident)
                b.cp("act", ET, ET[:], pg[0], pg[0][:, 0:384].rearrange("p (c t) -> p c t", c=3))
                for sub in range(8):
                    t0 = sub * 16
                    A_ = ohA[sub % 2]
                    B_ = ohB[sub % 2]
                    T_ = ohT[sub % 2]
                    io3 = iota[:].unsqueeze(1).to_broadcast([128, 16, 128])
                    b.tt("dve", B_, B_[:], iota, io3, ET, ET[:, 0, t0:t0 + 16].unsqueeze(2).to_broadcast([128, 16, 128]), ALU.is_equal)
                    b.tt("dve", T_, T_[:], iota, io3, ET, ET[:, 1, t0:t0 + 16].unsqueeze(2).to_broadcast([128, 16, 128]), ALU.is_equal)
                    b.tt("pool", A_, A_[:], T_, T_[:], ET, ET[:, 2, t0:t0 + 16].unsqueeze(2).to_broadcast([128, 16, 128]), ALU.mult)
                    for q4 in range(4):
                        p_ = pg[kq % 2]
                        kq += 1
                        pv = p_[:].rearrange("p (i t) -> p i t", t=4)
                        for tq in range(4):
                            tt_ = q4 * 4 + tq
                            b.mm(p_, pv[:, :, tq], A_, A_[:, tt_, :], B_, B_[:, tt_, :])
                        tg0 = tl * 128 + t0 + q4 * 4
                        b.cp("act", GT, GT[:, :, tg0:tg0 + 4], p_, pv)
            for i1 in range(128):
                u_ = Uc[i1 % NW]
                v_ = Vc[i1 % NW]
                b.load(u_, u_[:], uTb, uTb.t[i1])
                b.load(v_, v_[:], vbd, vbd.t[i1])
                ps_ = pss[i1 % 2]
                for kc in range(8):
                    b.mm(ps_, ps_[:, 0:TG], u_, u_[:, kc * 128:(kc + 1) * 128], h2b, h2b[:, kc, :], start=(kc == 0), stop=(kc == 7))
                g_ = gel[i1 % 4]
                a_ = AT[i1 % 4]
                b.act(g_, g_[:], ps_, ps_[:, 0:TG], AF.Gelu_apprx_tanh)
                b.tt("pool" if i1 % 2 == 0 else "dve", a_, a_[:], g_, g_[:], GT, GT[:, i1, :], ALU.mult)
                for tl in range(TG // 128):
                    for half in range(2):
                        p_ = po[tl * 2 + half]
                        b.mm(p_, p_[:], a_, a_[:, tl * 128:(tl + 1) * 128], v_, v_[:, half * 512:(half + 1) * 512], start=(i1 == 0), stop=(i1 == 127))
            for tl in range(TG // 128):
                ts_ = slice(tl * 128, (tl + 1) * 128)
                v = 0 if (g0 + tl * 128) < NPR else 1
                for half in range(2):
                    b.cp("act", fo, fo[:, half * 512:(half + 1) * 512], po[tl * 2 + half], po[tl * 2 + half][:])
                for half in range(2):
                    p_ = pg[half]
                    for c4 in range(4):
                        c = half * 4 + c4
                        b.tr(p_, p_[:, c4 * 128:(c4 + 1) * 128], fo, fo[:, c * 128:(c + 1) * 128], ident)
                    for c4 in range(4):
                        c = half * 4 + c4
                        b.stt(xn, xn[:, c, ts_], p_, p_[:, c4 * 128:(c4 + 1) * 128], modT[:, l, 40 + c, v:v + 1], xn, xn[:, c, ts_], ALU.mult, ALU.add, extra=[modT])
            b.load(xT, xview(xT, g0, TG), xn, xn[:])


_CACHE = {}


def _perm_qk():
    idx = np.zeros(256, np.int64)
    for s in range(2):
        for h in range(4):
            for j in range(32):
                idx[s * 128 + h * 32 + j] = h * 64 + s * 32 + j
    return idx


def kernel(x_prompt, x_sample, state_ret, state_gdn, c, c_ctx, ada_w, ada_b, norm_mix_g, norm_ffn_g,
           final_norm_g, ev_w_in, ev_w_out, ret_gamma_logit, ret_norm_g, sc_conv_w, od_w_in, od_w_out,
           gdn_conv_w, gdn_a_log, gdn_dt_bias, gdn_norm_g, cf_dw_w, cf_dw_b, cf_ln_g, cf_ln_b,
           peer_wq, peer_keys, peer_u, peer_v):
    f = lambda a: np.ascontiguousarray(np.asarray(a, dtype=np.float32))
    x_prompt, x_sample = f(x_prompt), f(x_sample)
    shared = {}
    shared["ada_w"] = f(ada_w)
    shared["ada_bT"] = f(np.asarray(ada_b).reshape(4, 48, 128).transpose(2, 0, 1))
    shared["gmixT"] = f(np.asarray(norm_mix_g).reshape(4, 8, 128).transpose(2, 0, 1))
    shared["gffnT"] = f(np.asarray(norm_ffn_g).reshape(4, 8, 128).transpose(2, 0, 1))
    shared["gfinT"] = f(np.asarray(final_norm_g).reshape(8, 128).T)
    pi = _perm_qk()
    ew = np.array(ev_w_in, dtype=np.float32)
    ew[:, :, 0:256] = np.asarray(ev_w_in)[:, :, 0:256][:, :, pi]
    ew[:, :, 256:512] = np.asarray(ev_w_in)[:, :, 256:512][:, :, pi]
    shared["ev_w_in"] = f(ew)
    shared["ev_w_out"] = f(ev_w_out)
    shared["lgam"] = f(np.asarray(ret_gamma_logit).reshape(2, 8))
    shared["retgT"] = f(np.asarray(ret_norm_g).transpose(2, 0, 1))
    shared["scwT"] = f(np.asarray(sc_conv_w).reshape(2, 3, 4, 128).transpose(3, 0, 2, 1))
    shared["od_w_in"] = f(od_w_in)
    shared["od_w_out"] = f(od_w_out)
    shared["gcwT"] = f(np.asarray(gdn_conv_w).reshape(2, 3, 12, 128).transpose(3, 0, 2, 1))
    shared["alog"] = f(np.asarray(gdn_a_log).reshape(2, 8))
    shared["dtb"] = f(np.asarray(gdn_dt_bias).reshape(2, 8))
    shared["gdngT"] = f(np.asarray(gdn_norm_g).T)
    shared["cfwT"] = f(np.asarray(cf_dw_w).reshape(2, 31, 4, 128).transpose(3, 0, 2, 1))
    shared["cfbT"] = f(np.asarray(cf_dw_b).reshape(2, 4, 128).transpose(2, 0, 1))
    shared["lngT"] = f(np.asarray(cf_ln_g).reshape(2, 4, 128).transpose(2, 0, 1))
    shared["lnbT"] = f(np.asarray(cf_ln_b).reshape(2, 4, 128).transpose(2, 0, 1))
    shared["peer_wq"] = f(peer_wq)
    shared["keysT"] = f(np.asarray(peer_keys).reshape(4, 16, 128, 128).transpose(0, 1, 3, 2))
    shared["peer_u"] = f(np.asarray(peer_u).reshape(4, 128, 128, 8, 128).transpose(0, 1, 4, 3, 2)).reshape(4, 128, 128, 1024)
    shared["peer_v"] = f(peer_v).reshape(4, 128, 128, 1024)
    for k, v_ in host_consts().items():
        shared["c_" + k] = f(v_)
    in_maps = []
    cc = np.asarray(c, dtype=np.float32)
    cx = np.asarray(c_ctx, dtype=np.float32)
    for r in range(NCORES):
        m = dict(shared)
        xs = np.concatenate([x_prompt[4 * r:4 * r + 4].reshape(NPR, D), x_sample[r]], axis=0)
        m["xT0"] = f(xs.T.reshape(8, 128, NT))
        cv = np.stack([cx, cc[r]], axis=0)
        m["cT"] = f(cv.reshape(2, 8, 128).transpose(2, 1, 0))
        m["sret"] = f(np.asarray(state_ret)[r])
        m["sgdn"] = f(np.asarray(state_gdn)[r])
        in_maps.append(m)
    if "nc" not in _CACHE:
        _CACHE["nc"] = build()
    res = run_bass_kernel_spmd(_CACHE["nc"], in_maps, core_ids=list(range(NCORES)))
    y_prompt = np.zeros((32, 256, D), np.float32)
    y_sample = np.zeros((8, 2048, D), np.float32)
    nret = np.zeros((32, 2, 2, 4, 64, 128), np.float32)
    ngdn = np.zeros((32, 2, 2, 4, 128, 128), np.float32)
    if STOP:
        _CACHE["pT"] = [np.asarray(res.results[r]["pT"]) for r in range(NCORES)]
        _CACHE["yTd"] = [np.asarray(res.results[r]["yTd"]) for r in range(NCORES)]
        _CACHE["dmod"] = [np.asarray(res.results[r]["dmod"]) for r in range(NCORES)]
        _CACHE["dbg"] = [np.asarray(res.results[r]["dbg"]).reshape(D, NT).T for r in range(NCORES)]
    for r in range(NCORES):
        o = res.results[r]
        yt = np.asarray(o["yT"]).reshape(D, NT).T
        y_prompt[4 * r:4 * r + 4] = yt[:NPR].reshape(4, 256, D)
        y_sample[r] = yt[NPR:]
        nret[4 * r:4 * r + 4] = np.asarray(o["nsr"])
        ngdn[4 * r:4 * r + 4] = np.asarray(o["nsg"])
    return (y_prompt, y_sample, nret, ngdn)
```

```python
import os
import numpy as np
from contextlib import ExitStack, contextmanager
import concourse.bass as bass
import concourse.mybir as mybir
from concourse.bass_utils import run_bass_kernel_spmd

F32 = mybir.dt.float32
BF16 = mybir.dt.bfloat16
U32 = mybir.dt.uint32
I32 = mybir.dt.int32
ALU = mybir.AluOpType
AF = mybir.ActivationFunctionType
AX = mybir.AxisListType

NCORES = 8
D = 1024
NT = 3072
NPR = 1024
LP = 256
LS = 2048
EPS = 1e-6
SEQS = [(0, 256, False), (256, 256, False), (512, 256, False), (768, 256, False), (1024, 2048, True)]
STOP = os.environ.get("KSTOP", "")
SKIPG = os.environ.get("KSKIPG", "") == "1"


class Res:
    __slots__ = ("w", "rs")

    def __init__(self):
        self.w = None
        self.rs = []


class T:
    def __init__(self, t, res=None):
        self.t = t
        self.res = res if res is not None else Res()

    def __getitem__(self, k):
        return self.t[k]


class Builder:
    EPOCH = 16000
    NDMA = 24

    def __init__(self, nc, es):
        self.nc = nc
        self.es = es
        self.E = {"pe": nc.tensor, "act": nc.scalar, "dve": nc.vector, "pool": nc.gpsimd, "sp": nc.sync}
        self.cur = {}
        self.cnt = {}
        self.nsem = 0
        for e in self.E:
            self._newsem(e)
        self.seen = {e: {} for e in self.E}
        self.dsem = [es.enter_context(nc.semaphore("dq%d" % i)) for i in range(self.NDMA)]
        self.duse = [0] * self.NDMA
        self.dk = 0
        self.uid = 0
        self.rec = None

    def _newsem(self, e):
        self.nsem += 1
        self.cur[e] = self.es.enter_context(self.nc.semaphore("s_%s_%d" % (e, self.nsem)))
        self.cnt[e] = 0

    def name(self, p):
        self.uid += 1
        return "%s_%d" % (p, self.uid)

    def sb(self, es, shape, dt=F32, name="t"):
        return T(es.enter_context(self.nc.sbuf_tensor(self.name(name), list(shape), dt)))

    def ps(self, es, shape, dt=F32, name="p"):
        return T(es.enter_context(self.nc.psum_tensor(self.name(name), list(shape), dt)))

    def dram(self, name, shape, dt, kind):
        return T(self.nc.dram_tensor(name, list(shape), dt, kind=kind).ap())

    def _deps(self, eng, r, w):
        deps = {}

        def add(ev):
            if ev is None:
                return
            s, v, src = ev
            if eng == "pe" and src == "pe":
                return
            k = id(s)
            if k not in deps or deps[k][1] < v:
                deps[k] = (s, v)

        for x in r:
            add(x.res.w)
        for x in w:
            add(x.res.w)
            for ev in x.res.rs:
                add(ev)
        E = self.E[eng]
        sn = self.seen[eng]
        for k, (s, v) in deps.items():
            if sn.get(k, 0) >= v:
                continue
            E.wait_ge(s, v)
            sn[k] = v

    def _mark(self, ev, r, w):
        for x in r:
            x.res.rs.append(ev)
        for x in w:
            x.res.w = ev
            x.res.rs = []

    def I(self, eng, f, r=(), w=()):
        if self.rec is not None:
            self.rec.append((0, eng, f, r, w))
            return None
        self._deps(eng, r, w)
        inst = f()
        if self.cnt[eng] >= self.EPOCH:
            self._newsem(eng)
        self.cnt[eng] += 1
        ev = (self.cur[eng], self.cnt[eng], eng)
        inst.then_inc(ev[0], 1)
        self._mark(ev, r, w)
        return ev

    def DMA(self, eng, f, r=(), w=()):
        if self.rec is not None:
            self.rec.append((1, eng, f, r, w))
            return None
        self._deps(eng, r, w)
        k = self.dk % self.NDMA
        self.dk += 1
        s = self.dsem[k]
        E = self.E[eng]
        if self.duse[k] > 0:
            key = id(s)
            if self.seen[eng].get(key, 0) < 16 * self.duse[k]:
                E.wait_ge(s, 16 * self.duse[k])
                self.seen[eng][key] = 16 * self.duse[k]
        self.duse[k] += 1
        inst = f()
        ev = (s, 16 * self.duse[k], "dma")
        inst.then_inc(s, 16)
        self._mark(ev, r, w)
        return ev

    def load(self, dst, dst_ap, src, src_ap, eng="sp"):
        return self.DMA(eng, lambda: self.E[eng].dma_start(out=dst_ap, in_=src_ap), r=[src], w=[dst])

    def tt(self, eng, out, o_ap, a, a_ap, b, b_ap, op):
        return self.I(eng, lambda: self.E[eng].tensor_tensor(out=o_ap, in0=a_ap, in1=b_ap, op=op), r=[a, b], w=[out])

    def ts(self, eng, out, o_ap, a, a_ap, s1, op0, s2=None, op1=None, extra=()):
        if op1 is None:
            f = lambda: self.E[eng].tensor_scalar(out=o_ap, in0=a_ap, scalar1=s1, scalar2=None, op0=op0)
        else:
            f = lambda: self.E[eng].tensor_scalar(out=o_ap, in0=a_ap, scalar1=s1, scalar2=s2, op0=op0, op1=op1)
        return self.I(eng, f, r=[a] + list(extra), w=[out])

    def stt(self, out, o_ap, a, a_ap, sc, b, b_ap, op0, op1, extra=(), accum=None, accum_t=None):
        if accum is None:
            f = lambda: self.nc.vector.scalar_tensor_tensor(out=o_ap, in0=a_ap, scalar=sc, in1=b_ap, op0=op0, op1=op1)
            w = [out]
        else:
            f = lambda: self.nc.vector.scalar_tensor_tensor(out=o_ap, in0=a_ap, scalar=sc, in1=b_ap, op0=op0, op1=op1, accum_out=accum)
            w = [out, accum_t]
        return self.I("dve", f, r=[a, b] + list(extra), w=w)

    def act(self, out, o_ap, a, a_ap, func, scale=1.0, bias=0.0, extra=()):
        return self.I("act", lambda: self.nc.scalar.activation(out=o_ap, in_=a_ap, func=func, bias=bias, scale=scale),
                      r=[a] + list(extra), w=[out])

    def cp(self, eng, out, o_ap, a, a_ap):
        if eng == "act":
            return self.I("act", lambda: self.nc.scalar.copy(out=o_ap, in_=a_ap), r=[a], w=[out])
        return self.I(eng, lambda: self.E[eng].tensor_copy(out=o_ap, in_=a_ap), r=[a], w=[out])

    def mm(self, out, o_ap, l, l_ap, rr, r_ap, start=True, stop=True):
        return self.I("pe", lambda: self.nc.tensor.matmul(o_ap, lhsT=l_ap, rhs=r_ap, start=start, stop=stop), r=[l, rr], w=[out])

    def tr(self, out, o_ap, a, a_ap, ident):
        return self.I("pe", lambda: self.nc.tensor.transpose(o_ap, a_ap, ident[:]), r=[a, ident], w=[out])

    def memset(self, eng, out, o_ap, v):
        return self.I(eng, lambda: self.E[eng].memset(o_ap, v), r=[], w=[out])

    def load_cast(self, stg, dst, dst_ap, src, src_ap, width, k):
        st = stg[k % len(stg)]
        self.load(st, st[:, 0:width], src, src_ap)
        self.cp("pool" if k % 2 == 0 else "act", dst, dst_ap, st, st[:, 0:width])

    def record(self, fn):
        self.rec = []
        fn()
        lst, self.rec = self.rec, None
        return lst

    def emit_interleaved(self, lists):
        n = max(len(l) for l in lists)
        for k in range(n):
            for l in lists:
                if k < len(l):
                    kind, eng, f, r, w = l[k]
                    (self.DMA if kind else self.I)(eng, f, r, w)

    def barrier(self):
        evs = [(self.cur[e], self.cnt[e]) for e in self.E if self.cnt[e] > 0]
        evs += [(self.dsem[k], 16 * self.duse[k]) for k in range(self.NDMA) if self.duse[k] > 0]
        for eng in self.E:
            E = self.E[eng]
            sn = self.seen[eng]
            for (s, v) in evs:
                if sn.get(id(s), 0) >= v:
                    continue
                E.wait_ge(s, v)
                sn[id(s)] = v

    @contextmanager
    def phase(self):
        with ExitStack() as ph:
            yield ph
            self.barrier()

    def final_wait(self, outs):
        for o in outs:
            self._deps("sp", [o], [])


def host_consts():
    c = {}
    c["ident"] = np.eye(128, dtype=np.float32)
    c["ones"] = np.ones((128, 128), np.float32)
    p = np.arange(128)
    hm = np.zeros((128, 4), np.float32)
    for h in range(4):
        hm[h * 32:(h + 1) * 32, h] = 0.125
    c["hm"] = hm
    t = np.arange(LS, dtype=np.float32)
    row = np.floor(t / 64.0)
    col = t - row * 64.0
    nf = 16
    freqs = (10000.0 ** (-np.arange(nf, dtype=np.float32) / nf)).astype(np.float32)
    ang = np.concatenate([row[:, None] * freqs, col[:, None] * freqs], axis=-1).astype(np.float32)
    c["cos"] = np.tile(np.cos(ang).T.astype(np.float32), (4, 1))
    c["sin"] = np.tile(np.sin(ang).T.astype(np.float32), (4, 1))
    m = np.arange(3968)
    c["dtab"] = (m[None, :] - 1920 - p[:, None]).astype(np.float32)
    c["posf"] = np.tile((t + 1.0)[None, :], (128, 1)).astype(np.float32)
    c["posb"] = np.tile((LS - t)[None, :], (128, 1)).astype(np.float32)
    posst = np.zeros((128, 2, 2), np.float32)
    for jt in range(2):
        posst[:, jt, 0] = 255 - (jt * 128 + p)
        posst[:, jt, 1] = jt * 128 + p
    c["posst"] = posst
    c["Mf"] = (p[:, None] <= p[None, :]).astype(np.float32)
    c["Mb"] = (p[:, None] >= p[None, :]).astype(np.float32)
    c["Sf"] = (p[:, None] < p[None, :]).astype(np.float32)
    c["Sb"] = (p[:, None] > p[None, :]).astype(np.float32)
    Elo = np.zeros((128, 7, 128), np.float32)
    for lv in range(7):
        sz = 1 << lv
        blk_i = p[:, None] // (2 * sz)
        blk_j = p[None, :] // (2 * sz)
        Elo[:, lv, :] = ((blk_i == blk_j) & ((p[:, None] % (2 * sz)) >= sz) & ((p[None, :] % (2 * sz)) < sz)).astype(np.float32)
    c["Elo"] = Elo
    c["Eup"] = np.ascontiguousarray(Elo.transpose(2, 1, 0))
    c["iota128"] = np.tile(np.arange(128, dtype=np.float32)[None, :], (128, 1))
    c["iota16"] = np.tile(np.arange(16, dtype=np.float32)[None, :], (128, 1))
    return c


CONST_SHAPES = {"ident": [128, 128], "ones": [128, 128], "hm": [128, 4], "cos": [128, LS], "sin": [128, LS],
                "dtab": [128, 3968], "posf": [128, LS], "posb": [128, LS], "posst": [128, 2, 2],
                "Mf": [128, 128], "Mb": [128, 128], "Sf": [128, 128], "Sb": [128, 128], "iota16": [128, 16],
                "Elo": [128, 7, 128], "Eup": [128, 7, 128], "iota128": [128, 128]}

IN_SHAPES = {
    "xT0": [8, 128, NT], "cT": [128, 8, 2], "ada_w": [4, 1024, 6144], "ada_bT": [128, 4, 48],
    "gmixT": [128, 4, 8], "gffnT": [128, 4, 8], "gfinT": [128, 8],
    "ev_w_in": [2, 1024, 3072], "ev_w_out": [2, 1024, 1024], "lgam": [2, 8], "retgT": [128, 2, 4],
    "scwT": [128, 2, 4, 3], "od_w_in": [2, 1024, 3088], "od_w_out": [2, 1024, 1024], "gcwT": [128, 2, 12, 3],
    "alog": [2, 8], "dtb": [2, 8], "gdngT": [128, 2], "cfwT": [128, 2, 4, 31], "cfbT": [128, 2, 4],
    "lngT": [128, 2, 4], "lnbT": [128, 2, 4], "peer_wq": [4, 1024, 2048], "keysT": [4, 16, 128, 128],
    "peer_u": [4, 128, 128, 1024], "peer_v": [4, 128, 128, 1024],
    "sret": [2, 2, 4, 64, 128], "sgdn": [2, 2, 4, 128, 128],
}


def build():
    nc = bass.Bass("TRN2", target_bir_lowering=False)
    es0 = ExitStack()
    b = Builder(nc, es0)
    dr = {}
    for k, s in IN_SHAPES.items():
        dr[k] = b.dram(k, s, F32, "ExternalInput")
    for k, s in CONST_SHAPES.items():
        dr[k] = b.dram("c_" + k, s, F32, "ExternalInput")
    yT = b.dram("yT", [8, 128, NT], F32, "ExternalOutput")
    nsr = b.dram("nsr", [4, 2, 2, 4, 64, 128], F32, "ExternalOutput")
    nsg = b.dram("nsg", [4, 2, 2, 4, 128, 128], F32, "ExternalOutput")
    xT = b.dram("xT", [8, 128, NT], F32, "Internal")
    pT = b.dram("pT", [24, 128, NT], F32, "ExternalOutput" if STOP else "Internal")
    vtok = b.dram("vtok", [NT, 512], BF16, "Internal")
    ktok = b.dram("ktok", [NPR, 256], BF16, "Internal")
    qkr = b.dram("qkr", [4, 128, NT], BF16, "Internal")
    yTd = b.dram("yTd", [8, 128, NT], BF16, "ExternalOutput" if STOP else "Internal")
    dmod = b.dram("dmod", [128, 4 * 48 * 2], F32, "ExternalOutput") if STOP else None
    gbtok = b.dram("gbtok", [NT, 16], F32, "Internal")
    qkvT = b.dram("qkvT", [12, 128, NT], F32, "Internal")
    ktok32 = b.dram("ktok32", [NT, 512], F32, "Internal")
    vtok32 = b.dram("vtok32", [NT, 512], F32, "Internal")
    oTd = b.dram("oTd", [2, 4, 128, NT], F32, "Internal")
    uTb = b.dram("uTb", [128, 128, 1024], BF16, "Internal")
    vbd = b.dram("vbd", [128, 128, 1024], BF16, "Internal")
    ekg = b.dram("ekg", [3, NT, 128], F32, "Internal")
    h2bd = b.dram("h2bd", [8, 128, NT], BF16, "Internal")
    scr = (uTb, vbd, ekg, h2bd)
    outs_written = []

    es = es0
    ident = b.sb(es, [128, 128], F32, "ident")
    ones = b.sb(es, [128, 128], F32, "ones")
    epsc = b.sb(es, [128, 1], F32, "epsc")
    onec = b.sb(es, [128, 1], F32, "onec")
    b.load(ident, ident[:], dr["ident"], dr["ident"].t)
    b.load(ones, ones[:], dr["ones"], dr["ones"].t)
    b.memset("dve", epsc, epsc[:], EPS)
    b.memset("dve", onec, onec[:], 1.0)
    modT = b.sb(es, [128, 4, 48, 2], F32, "modT")
    A1 = b.sb(es, [128, 4, 8, 2], F32, "A1")
    A2 = b.sb(es, [128, 4, 8, 2], F32, "A2")
    gfin = b.sb(es, [128, 8], F32, "gfin")
    b.load(gfin, gfin[:], dr["gfinT"], dr["gfinT"].t)

    with b.phase() as ph:
        cT = b.sb(ph, [128, 8, 2], F32, "cT")
        scT = b.sb(ph, [128, 8, 2], F32, "scT")
        abT = b.sb(ph, [128, 4, 48], F32, "abT")
        gmx = b.sb(ph, [128, 4, 8], F32, "gmx")
        gff = b.sb(ph, [128, 4, 8], F32, "gff")
        tmpm = b.sb(ph, [128, 4, 8, 2], F32, "tmpm")
        wts = [b.sb(ph, [128, 8, 768], F32, "adaw") for _ in range(4)]
        pm = b.ps(ph, [128, 512], F32, "pm")
        b.load(cT, cT[:], dr["cT"], dr["cT"].t)
        b.load(abT, abT[:], dr["ada_bT"], dr["ada_bT"].t)
        b.load(gmx, gmx[:], dr["gmixT"], dr["gmixT"].t)
        b.load(gff, gff[:], dr["gffnT"], dr["gffnT"].t)
        b.act(scT, scT[:], cT, cT[:], AF.Silu)
        it = 0
        for l in range(4):
            for grp in range(8):
                wt = wts[it % 4]
                it += 1
                src = dr["ada_w"].t[l, :, grp * 768:(grp + 1) * 768].rearrange("(kc p) n -> p kc n", p=128)
                b.load(wt, wt[:], dr["ada_w"], src)
                for cc in range(6):
                    for kc in range(8):
                        b.mm(pm, pm[:, cc * 2:cc * 2 + 2], wt, wt[:, kc, cc * 128:(cc + 1) * 128], scT, scT[:, kc, :],
                             start=(kc == 0), stop=(kc == 7))
                for cc in range(6):
                    ch = grp * 6 + cc
                    b.ts("dve", modT, modT[:, l, ch, :], pm, pm[:, cc * 2:cc * 2 + 2], abT[:, l, ch:ch + 1], ALU.add, extra=[abT])
        b.ts("dve", tmpm, tmpm[:], modT, modT[:, :, 8:16, :], 1.0, ALU.add)
        b.tt("dve", A1, A1[:], tmpm, tmpm[:], gmx, gmx[:].unsqueeze(3).to_broadcast([128, 4, 8, 2]), ALU.mult)
        b.ts("dve", tmpm, tmpm[:], modT, modT[:, :, 32:40, :], 1.0, ALU.add)
        b.tt("dve", A2, A2[:], tmpm, tmpm[:], gff, gff[:].unsqueeze(3).to_broadcast([128, 4, 8, 2]), ALU.mult)

    def norm_mod(ph, xg, W, Acol, Bcol, hb=None, hf=None, tmp=None, sq=None, ps_=None, rstd=None):
        b.act(sq, sq[:, :, 0:W], xg, xg[:, :, 0:W], AF.Square)
        for c in range(8):
            b.mm(ps_, ps_[:, 0:W], ones, ones[:], sq, sq[:, c, 0:W], start=(c == 0), stop=(c == 7))
        b.act(rstd, rstd[:, 0:W], ps_, ps_[:, 0:W], AF.Ln, scale=1.0 / D, bias=epsc[:, 0:1], extra=[epsc])
        b.act(rstd, rstd[:, 0:W], rstd, rstd[:, 0:W], AF.Exp, scale=-0.5)
        for c in range(8):
            b.tt("dve", tmp, tmp[:, 0:W], xg, xg[:, c, 0:W], rstd, rstd[:, 0:W], ALU.mult)
            tgt = hf if hf is not None else hb
            if Bcol is not None:
                b.ts("dve", tgt, tgt[:, c, 0:W], tmp, tmp[:, 0:W], Acol(c), ALU.mult, Bcol(c), ALU.add, extra=[A1, A2, modT, gfin])
            else:
                b.ts("dve", tgt, tgt[:, c, 0:W], tmp, tmp[:, 0:W], Acol(c), ALU.mult, extra=[A1, A2, modT, gfin])
            if hf is not None and hb is not None:
                b.cp("pool", hb, hb[:, c, 0:W], hf, hf[:, c, 0:W])

    def xview(dt_, lo, W):
        return dt_.t[:, :, lo:lo + W].rearrange("c p t -> p c t")

    with b.phase() as ph:
        xb = [b.sb(ph, [128, 8, 512], F32, "xcp") for _ in range(2)]
        for tg in range(6):
            x_ = xb[tg % 2]
            b.load(x_, x_[:], dr["xT0"], xview(dr["xT0"], tg * 512, 512))
            b.load(xT, xview(xT, tg * 512, 512), x_, x_[:])

    for l in range(4):
        i = l // 2
        even = (l % 2 == 0)
        if STOP == "M":
            break
        with b.phase() as ph:
            ncol = 3072 if even else 3088
            win = b.sb(ph, [128, 8, ncol], BF16, "win")
            wsrc = dr["ev_w_in" if even else "od_w_in"]
            cstg = [b.sb(ph, [128, 1024], F32, "cstg") for _ in range(3)]
            ck = 0
            for kc in range(8):
                for c0 in range(0, ncol, 1024):
                    c1 = min(ncol, c0 + 1024)
                    b.load_cast(cstg, win, win[:, kc, c0:c1], wsrc, wsrc.t[i, kc * 128:(kc + 1) * 128, c0:c1], c1 - c0, ck)
                    ck += 1
            xg = b.sb(ph, [128, 8, 512], F32, "xg")
            hT = b.sb(ph, [128, 8, 512], BF16, "hT")
            sq = b.sb(ph, [128, 8, 512], F32, "sq")
            tmp = b.sb(ph, [128, 512], F32, "tmp")
            rstd = b.sb(ph, [128, 512], F32, "rstd")
            psn = b.ps(ph, [128, 512], F32, "psn")
            pmm = [b.ps(ph, [128, 512], F32, "pmm") for _ in range(3)]
            stg = [b.sb(ph, [128, 512], F32, "stg") for _ in range(3)]
            stgb = [b.sb(ph, [128, 512], BF16, "stgb") for _ in range(2)]
            if even:
                fm_cols = [c * 128 for c in (0, 1, 2, 3)] + [1024 + c * 128 for c in range(16)]
                fm_dst = [0, 1, 2, 3] + list(range(8, 24))
            else:
                fm_cols = [c * 128 for c in range(16)] + [2064 + c * 128 for c in range(8)]
                fm_dst = list(range(24))
                nega = b.sb(ph, [128, 8], F32, "nega")
                dtbb = b.sb(ph, [128, 8], F32, "dtbb")
                gbs = b.sb(ph, [128, 16], F32, "gbs")
                b.load(nega, nega[:], dr["alog"], dr["alog"].t[i:i + 1, :].partition_broadcast(128))
                b.load(dtbb, dtbb[:], dr["dtb"], dr["dtb"].t[i:i + 1, :].partition_broadcast(128))
                b.act(nega, nega[:], nega, nega[:], AF.Exp)
                b.ts("dve", nega, nega[:], nega, nega[:], -1.0, ALU.mult)
            k = 0
            for tg in range(6):
                v = 0 if tg < 2 else 1
                b.load(xg, xg[:], xT, xview(xT, tg * 512, 512))
                norm_mod(ph, xg, 512, lambda c: A1[:, l, c, v:v + 1], lambda c: modT[:, l, c, v:v + 1], hb=hT, tmp=tmp, sq=sq, ps_=psn, rstd=rstd)
                for cc, dc in zip(fm_cols, fm_dst):
                    p_ = pmm[k % 3]
                    s_ = stg[k % 3]
                    k += 1
                    for kc in range(8):
                        b.mm(p_, p_[:], win, win[:, kc, cc:cc + 128], hT, hT[:, kc, :], start=(kc == 0), stop=(kc == 7))
                    b.cp("act", s_, s_[:], p_, p_[:])
                    b.load(pT, pT.t[dc, :, tg * 512:(tg + 1) * 512], s_, s_[:])
                for tt_ in range(4):
                    tok0 = tg * 512 + tt_ * 128
                    if even:
                        p_ = pmm[k % 3]
                        sb_ = stgb[k % 2]
                        k += 1
                        for kc in range(8):
                            b.mm(p_, p_[:], hT, hT[:, kc, tt_ * 128:(tt_ + 1) * 128], win, win[:, kc, 512:1024], start=(kc == 0), stop=(kc == 7))
                        b.cp("act", sb_, sb_[:], p_, p_[:])
                        b.load(vtok, vtok.t[tok0:tok0 + 128, :], sb_, sb_[:])
                        if tg < 2:
                            p_ = pmm[k % 3]
                            sb_ = stgb[k % 2]
                            k += 1
                            for kc in range(8):
                                b.mm(p_, p_[:, 0:256], hT, hT[:, kc, tt_ * 128:(tt_ + 1) * 128], win, win[:, kc, 256:512], start=(kc == 0), stop=(kc == 7))
                            b.cp("act", sb_, sb_[:, 0:256], p_, p_[:, 0:256])
                            b.load(ktok, ktok.t[tok0:tok0 + 128, :], sb_, sb_[:, 0:256])
                    else:
                        p_ = pmm[k % 3]
                        k += 1
                        for kc in range(8):
                            b.mm(p_, p_[:, 0:16], hT, hT[:, kc, tt_ * 128:(tt_ + 1) * 128], win, win[:, kc, 2048:2064], start=(kc == 0), stop=(kc == 7))
                        b.tt("dve", gbs, gbs[:, 0:8], p_, p_[:, 0:8], dtbb, dtbb[:], ALU.add)
                        b.act(gbs, gbs[:, 0:8], gbs, gbs[:, 0:8], AF.Exp)
                        b.act(gbs, gbs[:, 0:8], gbs, gbs[:, 0:8], AF.Ln, bias=onec[:, 0:1], extra=[onec])
                        b.tt("dve", gbs, gbs[:, 0:8], gbs, gbs[:, 0:8], nega, nega[:], ALU.mult)
                        b.act(gbs, gbs[:, 8:16], p_, p_[:, 8:16], AF.Sigmoid)
                        b.load(gbtok, gbtok.t[tok0:tok0 + 128, :], gbs, gbs[:])
        if STOP == "P2":
            break
        if even:
            even_mixer(b, nc, dr, i, pT, vtok, ktok, qkr, yTd, nsr, ident, ones, epsc, outs_written, l_=l, scr=scr)
        else:
            odd_mixer(b, nc, dr, i, pT, gbtok, qkvT, ktok32, vtok32, oTd, yTd, nsg, ident, ones, epsc, onec, outs_written, l_=l, scr=scr)
        peer_phase(b, nc, dr, l, i, even, xT, yTd, modT, A2, ident, ones, epsc, norm_mod, xview, do_peer=(STOP != "%da" % l), scr=scr)
        if STOP in ("%da" % l, "%db" % l):
            break

    with b.phase() as ph:
        xg = b.sb(ph, [128, 8, 512], F32, "xg")
        hf = b.sb(ph, [128, 8, 512], F32, "hf")
        if STOP:
            dbg = b.dram("dbg", [8, 128, NT], F32, "ExternalOutput")
            for tg in range(6):
                b.load(xg, xg[:], xT, xview(xT, tg * 512, 512))
                b.load(dbg, xview(dbg, tg * 512, 512), xg, xg[:])
            b.load(dmod, dmod.t, modT, modT[:].rearrange("p a b c -> p (a b c)"))
            b.final_wait([dbg, dmod, pT, yTd])
        sq = b.sb(ph, [128, 8, 512], F32, "sq")
        tmp = b.sb(ph, [128, 512], F32, "tmp")
        rstd = b.sb(ph, [128, 512], F32, "rstd")
        psn = b.ps(ph, [128, 512], F32, "psn")
        for tg in range(6):
            b.load(xg, xg[:], xT, xview(xT, tg * 512, 512))
            norm_mod(ph, xg, 512, lambda c: gfin[:, c:c + 1], None, hf=hf, tmp=tmp, sq=sq, ps_=psn, rstd=rstd)
            b.load(yT, xview(yT, tg * 512, 512), hf, hf[:])
    b.final_wait([yT, nsr, nsg])
    es0.close()
    return nc


def even_mixer(b, nc, dr, i, pT, vtok, ktok, qkr, yTd, nsr, ident, ones, epsc, outs_written, l_=0, scr=None):
    with b.phase() as ph:
        x1 = b.sb(ph, [128, 512], F32, "x1")
        x2 = b.sb(ph, [128, 512], F32, "x2")
        cs = b.sb(ph, [128, LS], F32, "cos")
        sn = b.sb(ph, [128, LS], F32, "sin")
        t1 = b.sb(ph, [128, 512], F32, "t1")
        t2 = b.sb(ph, [128, 512], F32, "t2")
        o1 = b.sb(ph, [128, 512], BF16, "o1")
        o2 = b.sb(ph, [128, 512], BF16, "o2")
        b.load(cs, cs[:], dr["cos"], dr["cos"].t)
        b.load(sn, sn[:], dr["sin"], dr["sin"].t)
        for tg in range(6):
            lat = tg >= 2
            pos0 = (tg - 2) * 512
            for qk in range(2):
                b.load(x1, x1[:], pT, pT.t[qk * 2, :, tg * 512:(tg + 1) * 512])
                b.load(x2, x2[:], pT, pT.t[qk * 2 + 1, :, tg * 512:(tg + 1) * 512])
                if lat:
                    c_ = cs[:, pos0:pos0 + 512]
                    s_ = sn[:, pos0:pos0 + 512]
                    b.tt("dve", t1, t1[:], x1, x1[:], cs, c_, ALU.mult)
                    b.tt("pool", t2, t2[:], x2, x2[:], sn, s_, ALU.mult)
                    b.tt("dve", o1, o1[:], t1, t1[:], t2, t2[:], ALU.subtract)
                    b.tt("dve", t1, t1[:], x2, x2[:], cs, c_, ALU.mult)
                    b.tt("pool", t2, t2[:], x1, x1[:], sn, s_, ALU.mult)
                    b.tt("dve", o2, o2[:], t1, t1[:], t2, t2[:], ALU.add)
                else:
                    b.cp("dve", o1, o1[:], x1, x1[:])
                    b.cp("pool", o2, o2[:], x2, x2[:])
                b.load(qkr, qkr.t[qk * 2, :, tg * 512:(tg + 1) * 512], o1, o1[:])
                b.load(qkr, qkr.t[qk * 2 + 1, :, tg * 512:(tg + 1) * 512], o2, o2[:])
    with b.phase() as ph:
        lgt = b.sb(ph, [128, 8], F32, "lgt")
        nlg = b.sb(ph, [128, 8], F32, "nlg")
        hm = b.sb(ph, [128, 4], F32, "hm")
        retg = b.sb(ph, [128, 2, 4], F32, "retg")
        dtab = b.sb(ph, [128, 3968], F32, "dtab")
        ta = b.sb(ph, [128, 3968], F32, "ta")
        tb = b.sb(ph, [128, 3968], F32, "tb")
        Th = b.sb(ph, [128, 3968], F32, "Th")
        posf = b.sb(ph, [128, LS], F32, "posf")
        posb = b.sb(ph, [128, LS], F32, "posb")
        wfr = b.sb(ph, [128, LS], F32, "wfr")
        wbr = b.sb(ph, [128, LS], F32, "wbr")
        posst = b.sb(ph, [128, 2, 2], F32, "posst")
        wst = b.sb(ph, [128, 2, 2], F32, "wst")
        Q = b.sb(ph, [128, 2, LS], BF16, "Q")
        K = b.sb(ph, [128, 2, LS], BF16, "K")
        Kh = b.sb(ph, [128, 2, LS], BF16, "Kh")
        V = b.sb(ph, [128, 16, 128], BF16, "V")
        Vs = b.sb(ph, [128, 2, 2, 128], BF16, "Vs")
        kt = b.sb(ph, [128, 2, 256], BF16, "kt")
        S0 = b.sb(ph, [128, 2, 2, 128], BF16, "S0")
        S0f = b.sb(ph, [128, 2, 2, 128], F32, "S0f")
        Sm = [b.sb(ph, [128, 512], BF16, "Sm") for _ in range(2)]
        pst = [b.ps(ph, [128, 512], F32, "pst") for _ in range(2)]
        po = b.ps(ph, [128, 512], F32, "po")
        pc = [b.ps(ph, [128, 512], F32, "pc") for _ in range(2)]
        pn = b.ps(ph, [128, 512], F32, "pn")
        pss = b.ps(ph, [128, 512], F32, "pss")
        o = b.sb(ph, [128, 512], F32, "o")
        osq = b.sb(ph, [128, 512], F32, "osq")
        rs = b.sb(ph, [128, 512], F32, "rs")
        tq = b.sb(ph, [128, 512], F32, "tq")
        gch = b.sb(ph, [128, 512], F32, "gch")
        ya = b.sb(ph, [128, 512], BF16, "ya")
        sts = b.sb(ph, [128, 128], F32, "sts")
        b.load(lgt, lgt[:], dr["lgam"], dr["lgam"].t[i:i + 1, :].partition_broadcast(128))
        b.load(hm, hm[:], dr["hm"], dr["hm"].t)
        b.load(retg, retg[:], dr["retgT"], dr["retgT"].t)
        b.load(dtab, dtab[:], dr["dtab"], dr["dtab"].t)
        b.load(posf, posf[:], dr["posf"], dr["posf"].t)
        b.load(posb, posb[:], dr["posb"], dr["posb"].t)
        b.load(posst, posst[:], dr["posst"], dr["posst"].t)
        b.act(nlg, nlg[:], lgt, lgt[:], AF.Exp, scale=-1.0)
        b.ts("dve", nlg, nlg[:], nlg, nlg[:], 1.0, ALU.add)
        b.act(nlg, nlg[:], nlg, nlg[:], AF.Ln)
        b.ts("dve", lgt, lgt[:], nlg, nlg[:], -1.0, ALU.mult)
        for h in range(4):
            lf = lgt[:, h:h + 1]
            lb = lgt[:, 4 + h:5 + h]
            nlb = nlg[:, 4 + h:5 + h]
            b.ts("dve", ta, ta[:], dtab, dtab[:], 0.0, ALU.max)
            b.act(ta, ta[:], ta, ta[:], AF.Exp, scale=lf, extra=[lgt])
            b.ts("dve", tb, tb[:], dtab, dtab[:], 0.0, ALU.is_ge)
            b.tt("dve", Th, Th[:], ta, ta[:], tb, tb[:], ALU.mult)
            b.ts("dve", ta, ta[:], dtab, dtab[:], 0.0, ALU.min)
            b.act(ta, ta[:], ta, ta[:], AF.Exp, scale=nlb, extra=[nlg])
            b.ts("dve", tb, tb[:], dtab, dtab[:], 0.0, ALU.is_le)
            b.tt("dve", ta, ta[:], ta, ta[:], tb, tb[:], ALU.mult)
            b.tt("dve", Th, Th[:], Th, Th[:], ta, ta[:], ALU.add)
            b.act(wfr, wfr[:], posf, posf[:], AF.Exp, scale=lf, extra=[lgt])
            b.act(wbr, wbr[:], posb, posb[:], AF.Exp, scale=lb, extra=[lgt])
            b.act(wst, wst[:, :, 0], posst, posst[:, :, 0], AF.Exp, scale=lf, extra=[lgt])
            b.act(wst, wst[:, :, 1], posst, posst[:, :, 1], AF.Exp, scale=lb, extra=[lgt])
            b.ts("dve", wst, wst[:], wst, wst[:], 0.125, ALU.mult)
            b.memset("pool", S0f, S0f[:], 0.0)
            for d in range(2):
                for s in range(2):
                    b.load(S0f, S0f[h * 32:(h + 1) * 32, d, s, :], dr["sret"], dr["sret"].t[i, d, h, s * 32:(s + 1) * 32, :])
            b.cp("pool", S0, S0[:], S0f, S0f[:])
            for si, (off, L, lat) in enumerate(SEQS):
                nj = L // 128
                IG = min(512, L)
                b.load(Q, Q[:, :, 0:L], qkr, qkr.t[0:2, :, off:off + L].rearrange("c p t -> p c t"))
                b.load(K, K[:, :, 0:L], qkr, qkr.t[2:4, :, off:off + L].rearrange("c p t -> p c t"))
                b.load(V, V[:, 0:nj, :], vtok, vtok.t[off:off + L, h * 128:(h + 1) * 128].rearrange("(j p) e -> p j e", p=128))
                b.ts("dve", Kh, Kh[:, :, 0:L], K, K[:, :, 0:L], hm[:, h:h + 1], ALU.mult, extra=[hm])
                it = 0
                for ig in range(L // IG):
                    i0 = ig * IG
                    for jt in range(nj):
                        p_ = pst[it % 2]
                        s_ = Sm[it % 2]
                        it += 1
                        b.mm(p_, p_[:, 0:IG], Kh, Kh[:, 0, jt * 128:(jt + 1) * 128], Q, Q[:, 0, i0:i0 + IG], start=True, stop=False)
                        b.mm(p_, p_[:, 0:IG], Kh, Kh[:, 1, jt * 128:(jt + 1) * 128], Q, Q[:, 1, i0:i0 + IG], start=False, stop=True)
                        m0 = i0 - jt * 128 + 1920
                        b.tt("dve", s_, s_[:, 0:IG], p_, p_[:, 0:IG], Th, Th[:, m0:m0 + IG], ALU.mult)
                        b.mm(po, po[:, 0:IG], V, V[:, jt, :], s_, s_[:, 0:IG], start=(jt == 0), stop=(jt == nj - 1))
                    b.cp("act", o, o[:, 0:IG], po, po[:, 0:IG])
                    if lat:
                        for d, wr in ((0, wfr), (1, wbr)):
                            b.mm(pc[d], pc[d][:, 0:IG], S0, S0[:, d, 0, :], Q, Q[:, 0, i0:i0 + IG], start=True, stop=False)
                            b.mm(pc[d], pc[d][:, 0:IG], S0, S0[:, d, 1, :], Q, Q[:, 1, i0:i0 + IG], start=False, stop=True)
                            b.tt("dve", tq, tq[:, 0:IG], pc[d], pc[d][:, 0:IG], wr, wr[:, i0:i0 + IG], ALU.mult)
                            b.tt("dve", o, o[:, 0:IG], o, o[:, 0:IG], tq, tq[:, 0:IG], ALU.add)
                    b.act(osq, osq[:, 0:IG], o, o[:, 0:IG], AF.Square)
                    b.mm(pn, pn[:, 0:IG], ones, ones[:], osq, osq[:, 0:IG])
                    b.act(rs, rs[:, 0:IG], pn, pn[:, 0:IG], AF.Ln, scale=1.0 / 128.0, bias=epsc[:, 0:1], extra=[epsc])
                    b.act(rs, rs[:, 0:IG], rs, rs[:, 0:IG], AF.Exp, scale=-0.5)
                    b.load(gch, gch[:, 0:IG], pT, pT.t[8 + h, :, off + i0:off + i0 + IG])
                    b.act(gch, gch[:, 0:IG], gch, gch[:, 0:IG], AF.Silu)
                    b.tt("dve", o, o[:, 0:IG], o, o[:, 0:IG], rs, rs[:, 0:IG], ALU.mult)
                    b.stt(ya, ya[:, 0:IG], o, o[:, 0:IG], retg[:, i, h:h + 1], gch, gch[:, 0:IG], ALU.mult, ALU.mult, extra=[retg])
                    b.load(yTd, yTd.t[h, :, off + i0:off + i0 + IG], ya, ya[:, 0:IG])
                if not lat:
                    b.load(kt, kt[:], ktok, ktok.t[off:off + L, :].rearrange("(j p) c -> p j c", p=128))
                    for d in range(2):
                        for jt in range(2):
                            b.ts("dve", Vs, Vs[:, d, jt, :], V, V[:, jt, :], wst[:, jt, d:d + 1], ALU.mult, extra=[wst])
                    for d in range(2):
                        for s in range(2):
                            for jt in range(2):
                                b.mm(pss, pss[:, 0:128], kt, kt[:, jt, s * 128:(s + 1) * 128], Vs, Vs[:, d, jt, :], start=(jt == 0), stop=(jt == 1))
                            b.cp("act", sts, sts[:], pss, pss[:, 0:128])
                            b.load(nsr, nsr.t[si, i, d, h, s * 32:(s + 1) * 32, :], sts, sts[h * 32:(h + 1) * 32, :])
    with b.phase() as ph:
        def _tail():
            scw = b.sb(ph, [128, 2, 4, 3], F32, "scw")
            b.load(scw, scw[:], dr["scwT"], dr["scwT"].t)
            bg = b.sb(ph, [128, LS], F32, "bg")
            cg = b.sb(ph, [128, LS], F32, "cg")
            hb_ = b.sb(ph, [128, LS], F32, "hb")
            up = b.sb(ph, [128, LS + 2], F32, "up")
            acc = b.sb(ph, [128, LS], F32, "acc")
            yb = b.sb(ph, [128, LS], BF16, "yb")
            for (off, L, lat) in SEQS:
                for cb in range(4):
                    b.load(bg, bg[:, 0:L], pT, pT.t[12 + cb, :, off:off + L])
                    b.load(cg, cg[:, 0:L], pT, pT.t[16 + cb, :, off:off + L])
                    b.load(hb_, hb_[:, 0:L], pT, pT.t[20 + cb, :, off:off + L])
                    b.memset("pool", up, up[:, 0:1], 0.0)
                    b.memset("pool", up, up[:, L + 1:L + 2], 0.0)
                    b.tt("pool", up, up[:, 1:L + 1], cg, cg[:, 0:L], hb_, hb_[:, 0:L], ALU.mult)
                    b.ts("dve", acc, acc[:, 0:L], up, up[:, 0:L], scw[:, i, cb, 0:1], ALU.mult, extra=[scw])
                    b.stt(acc, acc[:, 0:L], up, up[:, 1:L + 1], scw[:, i, cb, 1:2], acc, acc[:, 0:L], ALU.mult, ALU.add, extra=[scw])
                    b.stt(acc, acc[:, 0:L], up, up[:, 2:L + 2], scw[:, i, cb, 2:3], acc, acc[:, 0:L], ALU.mult, ALU.add, extra=[scw])
                    b.tt("dve", yb, yb[:, 0:L], acc, acc[:, 0:L], bg, bg[:, 0:L], ALU.mult)
                    b.load(yTd, yTd.t[4 + cb, :, off:off + L], yb, yb[:, 0:L])
        la = b.record(_tail)
        lb = b.record(lambda: peer_cast(b, nc, dr, l_, scr[0], scr[1], ph))
        b.emit_interleaved([la, lb])


def odd_mixer(b, nc, dr, i, pT, gbtok, qkvT, ktok32, vtok32, oTd, yTd, nsg, ident, ones, epsc, onec, outs_written, l_=0, scr=None):
    with b.phase() as ph:
        gcw = b.sb(ph, [128, 2, 12, 3], F32, "gcw")
        b.load(gcw, gcw[:], dr["gcwT"], dr["gcwT"].t)
        xp = b.sb(ph, [128, LS + 2], F32, "xp")
        acc = b.sb(ph, [128, LS], F32, "acc")
        sq = b.sb(ph, [128, 512], F32, "sq")
        rs = b.sb(ph, [128, 512], F32, "rs")
        pn = b.ps(ph, [128, 512], F32, "pn")
        ptr = b.ps(ph, [128, 512], F32, "ptr")
        tk = b.sb(ph, [128, 512], F32, "tk")
        for (off, L, lat) in SEQS:
            for ch in range(12):
                b.memset("pool", xp, xp[:, 0:1], 0.0)
                b.memset("pool", xp, xp[:, L + 1:L + 2], 0.0)
                b.load(xp, xp[:, 1:L + 1], pT, pT.t[ch, :, off:off + L])
                b.ts("dve", acc, acc[:, 0:L], xp, xp[:, 0:L], gcw[:, i, ch, 0:1], ALU.mult, extra=[gcw])
                b.stt(acc, acc[:, 0:L], xp, xp[:, 1:L + 1], gcw[:, i, ch, 1:2], acc, acc[:, 0:L], ALU.mult, ALU.add, extra=[gcw])
                b.stt(acc, acc[:, 0:L], xp, xp[:, 2:L + 2], gcw[:, i, ch, 2:3], acc, acc[:, 0:L], ALU.mult, ALU.add, extra=[gcw])
                b.act(acc, acc[:, 0:L], acc, acc[:, 0:L], AF.Silu)
                W = min(512, L)
                for pc_ in range(L // W):
                    sl = slice(pc_ * W, (pc_ + 1) * W)
                    if ch < 8:
                        b.act(sq, sq[:, 0:W], acc, acc[:, sl], AF.Square)
                        b.mm(pn, pn[:, 0:W], ones, ones[:], sq, sq[:, 0:W])
                        b.act(rs, rs[:, 0:W], pn, pn[:, 0:W], AF.Ln, bias=epsc[:, 0:1], extra=[epsc])
                        b.act(rs, rs[:, 0:W], rs, rs[:, 0:W], AF.Exp, scale=-0.5)
                        if ch < 4:
                            b.stt(acc, acc[:, sl], acc, acc[:, sl], 128.0 ** -0.5, rs, rs[:, 0:W], ALU.mult, ALU.mult)
                        else:
                            b.tt("dve", acc, acc[:, sl], acc, acc[:, sl], rs, rs[:, 0:W], ALU.mult)
                    if ch >= 4:
                        dst = ktok32 if ch < 8 else vtok32
                        hh = ch % 4
                        nt_ = W // 128
                        for t_ in range(nt_):
                            b.tr(ptr, ptr[:, t_ * 128:(t_ + 1) * 128], acc, acc[:, pc_ * W + t_ * 128: pc_ * W + (t_ + 1) * 128], ident)
                        b.cp("act", tk, tk[:, 0:W], ptr, ptr[:, 0:W])
                        tok0 = off + pc_ * W
                        b.load(dst, dst.t[tok0:tok0 + W, hh * 128:(hh + 1) * 128].rearrange("(t p) e -> p t e", p=128),
                               tk, tk[:, 0:W].rearrange("p (t e) -> p t e", e=128))
                b.load(qkvT, qkvT.t[ch, :, off:off + L], acc, acc[:, 0:L])
    with b.phase() as ph:
        Mf = b.sb(ph, [128, 128], F32, "Mf")
        Mb = b.sb(ph, [128, 128], F32, "Mb")
        Sf = b.sb(ph, [128, 128], F32, "Sf")
        Sb = b.sb(ph, [128, 128], F32, "Sb")
        for t_, k_ in ((Mf, "Mf"), (Mb, "Mb"), (Sf, "Sf"), (Sb, "Sb")):
            b.load(t_, t_[:], dr[k_], dr[k_].t)

        def t4(nm):
            return b.sb(ph, [128, 4, 128], F32, nm)
        qTt, kTt, ktk, vtk = t4("qTt"), t4("kTt"), t4("ktk"), t4("vtk")
        gb = b.sb(ph, [128, 16], F32, "gb")
        gc = b.sb(ph, [128, 4], F32, "gc")
        gl = b.sb(ph, [128, 4], F32, "gl")
        egl = b.sb(ph, [128, 4], F32, "egl")
        eg = b.sb(ph, [128, 4], F32, "eg")
        egd = b.sb(ph, [128, 4], F32, "egd")
        bge = b.sb(ph, [128, 4], F32, "bge")
        gB = [b.sb(ph, [128, 128], F32, "gB") for _ in range(2)]
        nd, dec, expbc, dincl, dstr = t4("nd"), t4("dec"), t4("expbc"), t4("dincl"), t4("dstr")
        A, AT, attn, attnT = t4("A"), t4("AT"), t4("attn"), t4("attnT")
        Pa, PTa, Pb, PTb, TT, Tm = t4("Pa"), t4("PTa"), t4("Pb"), t4("PTb"), t4("TT"), t4("Tm")
        Elo = b.sb(ph, [128, 7, 128], F32, "Elo")
        Eup = b.sb(ph, [128, 7, 128], F32, "Eup")
        b.load(Elo, Elo[:], dr["Elo"], dr["Elo"].t)
        b.load(Eup, Eup[:], dr["Eup"], dr["Eup"].t)
        kbg, vb, kd, u, wT, qgT, vnew, S, ost = t4("kbg"), t4("vb"), t4("kd"), t4("u"), t4("wT"), t4("qgT"), t4("vnew"), t4("S"), t4("ost")
        pS = b.ps(ph, [128, 512], F32, "pS")
        P_ = [b.ps(ph, [128, 512], F32, "pg") for _ in range(7)]

        def hs(t_, h):
            return t_[:, h, :]

        def ph_(p, h):
            return p[:, h * 128:(h + 1) * 128]

        def p3(p):
            return p[:].rearrange("p (h f) -> p h f", h=4)

        for si, (off, L, lat) in enumerate(SEQS):
            nt_ = L // 128
            for d in range(2):
                Mtri = Mf if d == 0 else Mb
                incl = Mb if d == 0 else Mf
                strict = Sb if d == 0 else Sf
                if lat:
                    b.load(S, S[:], dr["sgdn"], dr["sgdn"].t[i, d].rearrange("h k v -> k h v"))
                else:
                    b.memset("pool", S, S[:], 0.0)
                order = range(nt_) if d == 0 else range(nt_ - 1, -1, -1)
                for ti in order:
                    tok0 = off + ti * 128
                    b.load(qTt, qTt[:], qkvT, qkvT.t[0:4, :, tok0:tok0 + 128].rearrange("h p t -> p h t"))
                    b.load(kTt, kTt[:], qkvT, qkvT.t[4:8, :, tok0:tok0 + 128].rearrange("h p t -> p h t"))
                    b.load(ktk, ktk[:], ktok32, ktok32.t[tok0:tok0 + 128, :].rearrange("p (h e) -> p h e", h=4))
                    b.load(vtk, vtk[:], vtok32, vtok32.t[tok0:tok0 + 128, :].rearrange("p (h e) -> p h e", h=4))
                    b.load(gb, gb[:], gbtok, gbtok.t[tok0:tok0 + 128, :])
                    gcol = gb[:, d * 4:d * 4 + 4]
                    bcol = gb[:, 8 + d * 4:8 + d * 4 + 4]
                    b.mm(pS, pS[:, 0:4], Mtri, Mtri[:], gb, gcol)
                    b.mm(pS, pS[:, 4:8], ones, ones[:], gb, gcol)
                    b.cp("dve", gc, gc[:], pS, pS[:, 0:4])
                    b.cp("dve", gl, gl[:], pS, pS[:, 4:8])
                    b.act(egl, egl[:], gl, gl[:], AF.Exp)
                    b.act(eg, eg[:], gc, gc[:], AF.Exp)
                    b.tt("dve", egd, egd[:], gl, gl[:], gc, gc[:], ALU.subtract)
                    b.act(egd, egd[:], egd, egd[:], AF.Exp)
                    b.tt("dve", bge, bge[:], eg, eg[:], gb, bcol, ALU.mult)
                    for h in range(4):
                        g_ = gB[h % 2]
                        b.ts("pool", g_, g_[:], ones, ones[:], gb[:, d * 4 + h:d * 4 + h + 1], ALU.mult, extra=[gb])
                        b.mm(P_[0], ph_(P_[0], h), g_, g_[:], Mtri, Mtri[:])
                    for h in range(4):
                        b.ts("dve", nd, hs(nd, h), P_[0], ph_(P_[0], h), gc[:, h:h + 1], ALU.subtract, 0.0, ALU.max, extra=[gc])
                    b.act(dec, dec[:], nd, nd[:], AF.Exp, scale=-1.0)
                    b.act(expbc, expbc[:], P_[0], p3(P_[0]), AF.Exp)
                    b.tt("pool", dincl, dincl[:], dec, dec[:], incl, incl[:].unsqueeze(1).to_broadcast([128, 4, 128]), ALU.mult)
                    b.tt("pool", dstr, dstr[:], dec, dec[:], strict, strict[:].unsqueeze(1).to_broadcast([128, 4, 128]), ALU.mult)
                    for h in range(4):
                        b.mm(P_[1], ph_(P_[1], h), kTt, hs(kTt, h), kTt, hs(kTt, h))
                        b.mm(P_[2], ph_(P_[2], h), qTt, hs(qTt, h), kTt, hs(kTt, h))
                    for h in range(4):
                        b.stt(A, hs(A, h), P_[1], ph_(P_[1], h), gb[:, 8 + d * 4 + h:8 + d * 4 + h + 1], dstr, hs(dstr, h), ALU.mult, ALU.mult, extra=[gb])
                    b.tt("dve", attn, attn[:], P_[2], p3(P_[2]), dincl, dincl[:], ALU.mult)
                    for h in range(4):
                        b.tr(P_[3], ph_(P_[3], h), A, hs(A, h), ident)
                        b.tr(P_[4], ph_(P_[4], h), attn, hs(attn, h), ident)
                    b.cp("act", AT, AT[:], P_[3], p3(P_[3]))
                    b.cp("act", attnT, attnT[:], P_[4], p3(P_[4]))
                    b.cp("pool", Tm, Tm[:], ident, ident[:].unsqueeze(1).to_broadcast([128, 4, 128]))
                    b.cp("pool", TT, TT[:], ident, ident[:].unsqueeze(1).to_broadcast([128, 4, 128]))
                    EA_t = Elo if d == 0 else Eup
                    EAT_t = Eup if d == 0 else Elo
                    for lv in range(7):
                        b.tt("pool", Pa, Pa[:], A, A[:], EA_t, EA_t[:, lv, :].unsqueeze(1).to_broadcast([128, 4, 128]), ALU.mult)
                        b.tt("pool", PTa, PTa[:], AT, AT[:], EAT_t, EAT_t[:, lv, :].unsqueeze(1).to_broadcast([128, 4, 128]), ALU.mult)
                        for h in range(4):
                            b.mm(P_[1], ph_(P_[1], h), PTa, hs(PTa, h), Tm, hs(Tm, h))
                            b.mm(P_[2], ph_(P_[2], h), Pa, hs(Pa, h), TT, hs(TT, h))
                        b.cp("act", Pb, Pb[:], P_[1], p3(P_[1]))
                        b.cp("dve", PTb, PTb[:], P_[2], p3(P_[2]))
                        for h in range(4):
                            b.mm(P_[0], ph_(P_[0], h), TT, hs(TT, h), Pb, hs(Pb, h))
                            b.mm(P_[3], ph_(P_[3], h), Tm, hs(Tm, h), PTb, hs(PTb, h))
                        b.tt("dve", Tm, Tm[:], Tm, Tm[:], P_[0], p3(P_[0]), ALU.subtract)
                        b.tt("dve", TT, TT[:], TT, TT[:], P_[3], p3(P_[3]), ALU.subtract)
                    b.tt("pool", kbg, kbg[:], ktk, ktk[:], bge, bge[:].unsqueeze(2).to_broadcast([128, 4, 128]), ALU.mult)
                    b.tt("pool", vb, vb[:], vtk, vtk[:], gb, bcol.unsqueeze(2).to_broadcast([128, 4, 128]), ALU.mult)
                    b.tt("pool", kd, kd[:], ktk, ktk[:], egd, egd[:].unsqueeze(2).to_broadcast([128, 4, 128]), ALU.mult)
                    b.tt("dve", qgT, qgT[:], qTt, qTt[:], expbc, expbc[:], ALU.mult)
                    for h in range(4):
                        b.mm(P_[3], ph_(P_[3], h), TT, hs(TT, h), vb, hs(vb, h))
                        b.mm(P_[4], ph_(P_[4], h), kbg, hs(kbg, h), TT, hs(TT, h))
                    b.cp("act", u, u[:], P_[3], p3(P_[3]))
                    b.cp("act", wT, wT[:], P_[4], p3(P_[4]))
                    for h in range(4):
                        b.mm(P_[5], ph_(P_[5], h), wT, hs(wT, h), S, hs(S, h))
                    b.tt("dve", vnew, vnew[:], u, u[:], P_[5], p3(P_[5]), ALU.subtract)
                    for h in range(4):
                        b.mm(P_[6], ph_(P_[6], h), S, hs(S, h), qgT, hs(qgT, h), start=True, stop=False)
                        b.mm(P_[6], ph_(P_[6], h), vnew, hs(vnew, h), attnT, hs(attnT, h), start=False, stop=True)
                    for h in range(4):
                        b.mm(P_[5], ph_(P_[5], h), kd, hs(kd, h), vnew, hs(vnew, h))
                    b.cp("act", ost, ost[:], P_[6], p3(P_[6]))
                    b.load(oTd, oTd.t[d, :, :, tok0:tok0 + 128].rearrange("h p t -> p h t"), ost, ost[:])
                    for h in range(4):
                        b.stt(S, hs(S, h), S, hs(S, h), egl[:, h:h + 1], P_[5], ph_(P_[5], h), ALU.mult, ALU.add, extra=[egl])
                if not lat:
                    b.load(nsg, nsg.t[si, i, d].rearrange("h k v -> k h v"), S, S[:])
    with b.phase() as ph:
        def _tail():
            gng = b.sb(ph, [128, 2], F32, "gng")
            cfw = b.sb(ph, [128, 2, 4, 31], F32, "cfw")
            cfb = b.sb(ph, [128, 2, 4], F32, "cfb")
            lng = b.sb(ph, [128, 2, 4], F32, "lng")
            lnb = b.sb(ph, [128, 2, 4], F32, "lnb")
            for t_, k_ in ((gng, "gdngT"), (cfw, "cfwT"), (cfb, "cfbT"), (lng, "lngT"), (lnb, "lnbT")):
                b.load(t_, t_[:], dr[k_], dr[k_].t)
            of = b.sb(ph, [128, 512], F32, "of")
            ob = b.sb(ph, [128, 512], F32, "ob")
            zz = b.sb(ph, [128, 512], F32, "zz")
            sq = b.sb(ph, [128, 512], F32, "sq")
            rs = b.sb(ph, [128, 512], F32, "rs")
            yc = b.sb(ph, [128, 512], BF16, "yc")
            pn = b.ps(ph, [128, 512], F32, "pn")
            pv = b.ps(ph, [128, 512], F32, "pv")
            for tg in range(6):
                cs_ = slice(tg * 512, (tg + 1) * 512)
                for h in range(4):
                    b.load(of, of[:], oTd, oTd.t[0, h, :, cs_])
                    b.load(ob, ob[:], oTd, oTd.t[1, h, :, cs_])
                    b.load(zz, zz[:], pT, pT.t[12 + h, :, cs_])
                    b.tt("dve", of, of[:], of, of[:], ob, ob[:], ALU.add)
                    b.act(sq, sq[:], of, of[:], AF.Square)
                    b.mm(pn, pn[:], ones, ones[:], sq, sq[:])
                    b.act(rs, rs[:], pn, pn[:], AF.Ln, scale=1.0 / 128.0, bias=epsc[:, 0:1], extra=[epsc])
                    b.act(rs, rs[:], rs, rs[:], AF.Exp, scale=-0.5)
                    b.act(zz, zz[:], zz, zz[:], AF.Silu)
                    b.tt("dve", of, of[:], of, of[:], rs, rs[:], ALU.mult)
                    b.stt(yc, yc[:], of, of[:], gng[:, i:i + 1], zz, zz[:], ALU.mult, ALU.mult, extra=[gng])
                    b.load(yTd, yTd.t[h, :, cs_], yc, yc[:])
            ca = b.sb(ph, [128, LS], F32, "ca")
            cg = b.sb(ph, [128, LS], F32, "cg")
            hp = b.sb(ph, [128, LS + 30], F32, "hp")
            cv = b.sb(ph, [128, 4, LS], F32, "cv")
            mean = b.sb(ph, [128, 512], F32, "mean")
            xc = b.sb(ph, [128, 4, 512], F32, "xc")
            sq4 = b.sb(ph, [128, 4, 512], F32, "sq4")
            for (off, L, lat) in SEQS:
                for cb in range(4):
                    b.load(ca, ca[:, 0:L], pT, pT.t[16 + cb, :, off:off + L])
                    b.load(cg, cg[:, 0:L], pT, pT.t[20 + cb, :, off:off + L])
                    b.memset("pool", hp, hp[:, 0:15], 0.0)
                    b.memset("pool", hp, hp[:, L + 15:L + 30], 0.0)
                    b.act(cg, cg[:, 0:L], cg, cg[:, 0:L], AF.Sigmoid)
                    b.tt("pool", hp, hp[:, 15:L + 15], ca, ca[:, 0:L], cg, cg[:, 0:L], ALU.mult)
                    b.ts("dve", cv, cv[:, cb, 0:L], hp, hp[:, 0:L], cfw[:, i, cb, 0:1], ALU.mult, cfb[:, i, cb:cb + 1], ALU.add, extra=[cfw, cfb])
                    for k in range(1, 31):
                        b.stt(cv, cv[:, cb, 0:L], hp, hp[:, k:k + L], cfw[:, i, cb, k:k + 1], cv, cv[:, cb, 0:L], ALU.mult, ALU.add, extra=[cfw])
                W = min(512, L)
                for pc_ in range(L // W):
                    sl = slice(pc_ * W, (pc_ + 1) * W)
                    for cb in range(4):
                        b.mm(pn, pn[:, 0:W], ones, ones[:], cv, cv[:, cb, sl], start=(cb == 0), stop=(cb == 3))
                    b.act(mean, mean[:, 0:W], pn, pn[:, 0:W], AF.Copy, scale=1.0 / 512.0)
                    for cb in range(4):
                        b.tt("dve", xc, xc[:, cb, 0:W], cv, cv[:, cb, sl], mean, mean[:, 0:W], ALU.subtract)
                    b.act(sq4, sq4[:, :, 0:W], xc, xc[:, :, 0:W], AF.Square)
                    for cb in range(4):
                        b.mm(pv, pv[:, 0:W], ones, ones[:], sq4, sq4[:, cb, 0:W], start=(cb == 0), stop=(cb == 3))
                    b.act(rs, rs[:, 0:W], pv, pv[:, 0:W], AF.Ln, scale=1.0 / 512.0, bias=epsc[:, 0:1], extra=[epsc])
                    b.act(rs, rs[:, 0:W], rs, rs[:, 0:W], AF.Exp, scale=-0.5)
                    for cb in range(4):
                        b.tt("dve", xc, xc[:, cb, 0:W], xc, xc[:, cb, 0:W], rs, rs[:, 0:W], ALU.mult)
                        b.act(yc, yc[:, 0:W], xc, xc[:, cb, 0:W], AF.Silu, scale=lng[:, i, cb:cb + 1], bias=lnb[:, i, cb:cb + 1], extra=[lng, lnb])
                        b.load(yTd, yTd.t[4 + cb, :, off + pc_ * W:off + (pc_ + 1) * W], yc, yc[:, 0:W])
        la = b.record(_tail)
        lb = b.record(lambda: peer_cast(b, nc, dr, l_, scr[0], scr[1], ph))
        b.emit_interleaved([la, lb])


def peer_phase(b, nc, dr, l, i, even, xT, yTd, modT, A2, ident, ones, epsc, norm_mod, xview, do_peer=True, scr=None):
    uTb, vbd, ekg, h2bd = scr
    with b.phase() as ph:
        wout = b.sb(ph, [128, 8, 1024], BF16, "wout")
        wq = b.sb(ph, [128, 8, 2048], BF16, "wq")
        keys = b.sb(ph, [128, 16, 128], BF16, "keys")
        iota16 = b.sb(ph, [128, 16], F32, "iota16")
        wsrc = dr["ev_w_out" if even else "od_w_out"]
        cstg = [b.sb(ph, [128, 1024], F32, "cstg") for _ in range(2)]
        ck = 0
        for kc in range(8):
            b.load_cast(cstg, wout, wout[:, kc, :], wsrc, wsrc.t[i, kc * 128:(kc + 1) * 128, :], 1024, ck)
            ck += 1
            for c0 in (0, 1024):
                b.load_cast(cstg, wq, wq[:, kc, c0:c0 + 1024], dr["peer_wq"], dr["peer_wq"].t[l, kc * 128:(kc + 1) * 128, c0:c0 + 1024], 1024, ck)
                ck += 1
        for c8 in range(2):
            st = cstg[ck % 2]
            ck += 1
            b.load(st, st[:].rearrange("p (c n) -> p c n", c=8), dr["keysT"], dr["keysT"].t[l, c8 * 8:(c8 + 1) * 8].rearrange("c d n -> d c n"))
            b.cp("pool", keys, keys[:, c8 * 8:(c8 + 1) * 8, :], st, st[:].rearrange("p (c n) -> p c n", c=8))
        b.load(iota16, iota16[:], dr["iota16"], dr["iota16"].t)
        thr16 = b.sb(ph, [128, 16], F32, "thr16")
        b.ts("dve", thr16, thr16[:], iota16, iota16[:], 16.0, ALU.mult)
        xn = b.sb(ph, [128, 8, 512], F32, "xn")
        h2f = b.sb(ph, [128, 8, 512], F32, "h2f")
        h2b = b.sb(ph, [128, 8, 512], BF16, "h2b")
        yg, xg, sq = h2b, h2f, h2f
        tmp = b.sb(ph, [128, 512], F32, "tmp")
        rstd = b.sb(ph, [128, 512], F32, "rstd")
        psn = b.ps(ph, [128, 512], F32, "psn")
        pmm = [b.ps(ph, [128, 512], F32, "pmm") for _ in range(2)]
        psc = [b.ps(ph, [128, 512], F32, "psc") for _ in range(4)]
        qT = b.sb(ph, [128, 16, 512], BF16, "qT")
        h2t = b.sb(ph, [128, 1024], F32, "h2t")
        def mkset():
            return (b.sb(ph, [128, 16, 128], F32, "ssb"), b.sb(ph, [128, 256], F32, "wk"), b.sb(ph, [128, 16, 16], F32, "mx"),
                    b.sb(ph, [128, 16, 16], U32, "mi"), b.sb(ph, [128, 16, 16], F32, "mif"), b.sb(ph, [128, 8, 256], F32, "cs"),
                    b.sb(ph, [128, 8, 16], F32, "tsv"), b.sb(ph, [128, 8, 16], U32, "sel"), b.sb(ph, [128, 8, 16], F32, "self"),
                    b.sb(ph, [128, 8, 16], F32, "aq"), b.sb(ph, [128, 8, 16], F32, "bq"), b.sb(ph, [128, 128, 16], F32, "oh"),
                    b.sb(ph, [128, 128], F32, "i1sel"), b.sb(ph, [128, 128], F32, "i2sel"), b.sb(ph, [128, 8, 16], F32, "gt"),
                    b.sb(ph, [128, 8], F32, "zs"))
        bufsets = [mkset() + (psc[0:2],), mkset() + (psc[2:4],)]
        for tg in range(6):
            v = 0 if tg < 2 else 1
            b.load(yg, yg[:], yTd, xview(yTd, tg * 512, 512))
            b.load(xg, xg[:], xT, xview(xT, tg * 512, 512))
            for dc in range(8):
                p_ = pmm[dc % 2]
                for kc in range(8):
                    b.mm(p_, p_[:], wout, wout[:, kc, dc * 128:(dc + 1) * 128], yg, yg[:, kc, :], start=(kc == 0), stop=(kc == 7))
                b.stt(xn, xn[:, dc, :], p_, p_[:], modT[:, l, 16 + dc, v:v + 1], xg, xg[:, dc, :], ALU.mult, ALU.add, extra=[modT])
            if not do_peer:
                b.load(xT, xview(xT, tg * 512, 512), xn, xn[:])
                continue
            norm_mod(ph, xn, 512, lambda c: A2[:, l, c, v:v + 1], lambda c: modT[:, l, 24 + c, v:v + 1], hb=h2b, hf=h2f, tmp=tmp, sq=sq, ps_=psn, rstd=rstd)
            for hp_ in range(16):
                p_ = pmm[hp_ % 2]
                for kc in range(8):
                    b.mm(p_, p_[:], wq, wq[:, kc, hp_ * 128:(hp_ + 1) * 128], h2b, h2b[:, kc, :], start=(kc == 0), stop=(kc == 7))
                b.cp("act", qT, qT[:, hp_, :], p_, p_[:])
            def tile_body(tt_, B):
                (ssb, wk, mx, mi, mif, cs_, tsv, sel, self_, aq, bq, oh, i1sel, i2sel, gt, zs, PSC) = B
                mxv = mx[:].rearrange("p (h t) k -> p h t k", t=2)
                mifv = mif[:].rearrange("p (h t) k -> p h t k", t=2)
                ts_ = slice(tt_ * 128, (tt_ + 1) * 128)
                for h8 in range(2):
                    for hh in range(8):
                        hp_ = h8 * 8 + hh
                        pq = PSC[hh // 4]
                        b.mm(pq, pq[:, (hh % 4) * 128:(hh % 4 + 1) * 128], qT, qT[:, hp_, ts_], keys, keys[:, hp_, :])
                    for q2 in range(2):
                        b.cp("act", ssb, ssb[:, h8 * 8 + q2 * 4:h8 * 8 + (q2 + 1) * 4, :], PSC[q2], PSC[q2][:].rearrange("p (a n) -> p a n", a=4))
                for hp_ in range(16):
                    b.I("dve", lambda hp_=hp_: nc.vector.max(out=mx[:, hp_, 0:8], in_=ssb[:, hp_, :]), r=[ssb], w=[mx])
                    b.I("dve", lambda hp_=hp_: nc.vector.max_index(out=mi[:, hp_, 0:8], in_max=mx[:, hp_, 0:8], in_values=ssb[:, hp_, :]), r=[ssb, mx], w=[mi])
                    b.I("dve", lambda hp_=hp_: nc.vector.match_replace(out=wk[:, 0:128], in_to_replace=mx[:, hp_, 0:8], in_values=ssb[:, hp_, :], imm_value=-1e30), r=[ssb, mx], w=[wk])
                    b.I("dve", lambda hp_=hp_: nc.vector.max(out=mx[:, hp_, 8:16], in_=wk[:, 0:128]), r=[wk], w=[mx])
                    b.I("dve", lambda hp_=hp_: nc.vector.max_index(out=mi[:, hp_, 8:16], in_max=mx[:, hp_, 8:16], in_values=wk[:, 0:128]), r=[wk, mx], w=[mi])
                b.cp("dve", mif, mif[:], mi, mi[:])
                csv = cs_[:].rearrange("p h (a c) -> p h a c", a=16)
                b.tt("dve", cs_, csv, mx, mxv[:, :, 0, :].unsqueeze(3).to_broadcast([128, 8, 16, 16]),
                     mx, mxv[:, :, 1, :].unsqueeze(2).to_broadcast([128, 8, 16, 16]), ALU.add)
                for h in range(8):
                    b.I("dve", lambda h=h: nc.vector.max(out=tsv[:, h, 0:8], in_=cs_[:, h, :]), r=[cs_], w=[tsv])
                    b.I("dve", lambda h=h: nc.vector.max_index(out=sel[:, h, 0:8], in_max=tsv[:, h, 0:8], in_values=cs_[:, h, :]), r=[cs_, tsv], w=[sel])
                    b.I("dve", lambda h=h: nc.vector.match_replace(out=wk[:], in_to_replace=tsv[:, h, 0:8], in_values=cs_[:, h, :], imm_value=-1e30), r=[cs_, tsv], w=[wk])
                    b.I("dve", lambda h=h: nc.vector.max(out=tsv[:, h, 8:16], in_=wk[:]), r=[wk], w=[tsv])
                    b.I("dve", lambda h=h: nc.vector.max_index(out=sel[:, h, 8:16], in_max=tsv[:, h, 8:16], in_values=wk[:]), r=[wk, tsv], w=[sel])
                b.cp("dve", self_, self_[:], sel, sel[:])
                ohv0 = oh[:].rearrange("p (h k) a -> p h k a", h=8)
                b.tt("dve", oh, ohv0, self_, self_[:].unsqueeze(3).to_broadcast([128, 8, 16, 16]),
                     thr16, thr16[:].unsqueeze(1).unsqueeze(1).to_broadcast([128, 8, 16, 16]), ALU.is_ge)
                b.I("dve", lambda: nc.vector.tensor_reduce(out=aq[:].rearrange("p h k -> p (h k)"), in_=oh[:], axis=AX.X, op=ALU.add), r=[oh], w=[aq])
                b.ts("dve", aq, aq[:], aq, aq[:], -1.0, ALU.add)
                b.stt(bq, bq[:], aq, aq[:], -16.0, self_, self_[:], ALU.mult, ALU.add)
                for (qq, half, dst) in ((aq, 0, i1sel), (bq, 1, i2sel)):
                    ohv = oh[:].rearrange("p (h k) a -> p h k a", h=8)
                    b.tt("dve", oh, ohv, qq, qq[:].unsqueeze(3).to_broadcast([128, 8, 16, 16]),
                         iota16, iota16[:].unsqueeze(1).unsqueeze(1).to_broadcast([128, 8, 16, 16]), ALU.is_equal)
                    b.tt("dve", oh, ohv, oh, ohv, mif, mifv[:, :, half, :].unsqueeze(2).to_broadcast([128, 8, 16, 16]), ALU.mult)
                    b.I("dve", lambda dst=dst: nc.vector.tensor_reduce(out=dst[:], in_=oh[:], axis=AX.X, op=ALU.add), r=[oh], w=[dst])
                b.tt("dve", gt, gt[:], tsv, tsv[:], tsv, tsv[:, :, 0:1].to_broadcast([128, 8, 16]), ALU.subtract)
                b.act(gt, gt[:], gt, gt[:], AF.Exp)
                b.I("dve", lambda: nc.vector.tensor_reduce(out=zs[:], in_=gt[:], axis=AX.X, op=ALU.add), r=[gt], w=[zs])
                b.I("dve", lambda: nc.vector.reciprocal(out=zs[:], in_=zs[:]), r=[zs], w=[zs])
                b.tt("dve", gt, gt[:], gt, gt[:], zs, zs[:].unsqueeze(2).to_broadcast([128, 8, 16]), ALU.mult)
                tok0 = tg * 512 + tt_ * 128
                b.load(ekg, ekg.t[0, tok0:tok0 + 128, :], i1sel, i1sel[:])
                b.load(ekg, ekg.t[1, tok0:tok0 + 128, :], i2sel, i2sel[:])
                b.load(ekg, ekg.t[2, tok0:tok0 + 128, :], gt, gt[:].rearrange("p h k -> p (h k)"))

            for t2 in range(2):
                la = b.record(lambda: tile_body(2 * t2, bufsets[0]))
                lb = b.record(lambda: tile_body(2 * t2 + 1, bufsets[1]))
                b.emit_interleaved([la, lb])
            b.load(h2bd, xview(h2bd, tg * 512, 512), h2b, h2b[:])
            b.load(xT, xview(xT, tg * 512, 512), xn, xn[:])
    if do_peer:
        peer_dense(b, nc, dr, l, xT, modT, ident, uTb, vbd, ekg, h2bd, xview)


def peer_cast(b, nc, dr, l, uTb, vbd, ph):
    if True:
        st = [b.sb(ph, [128, 4, 1024], F32, "cst") for _ in range(3)]
        sb_ = [b.sb(ph, [128, 4, 1024], BF16, "csb") for _ in range(3)]
        k = 0
        engs = ("act", "pool", "dve")
        for (src, dst) in ((dr["peer_u"], uTb), (dr["peer_v"], vbd)):
            for g in range(32):
                f_ = st[k % 3]
                o_ = sb_[k % 3]
                b.load(f_, f_[:], src, src.t[l, g * 4:(g + 1) * 4].rearrange("c p n -> p c n"))
                b.cp(engs[k % 3], o_, o_[:], f_, f_[:])
                b.load(dst, dst.t[g * 4:(g + 1) * 4].rearrange("c p n -> p c n"), o_, o_[:])
                k += 1


def peer_dense(b, nc, dr, l, xT, modT, ident, uTb, vbd, ekg, h2bd, xview):
    TG = 384
    with b.phase() as ph:
        iota = b.sb(ph, [128, 128], F32, "iota128")
        b.load(iota, iota[:], dr["iota128"], dr["iota128"].t)
        GT = b.sb(ph, [128, 128, TG], BF16, "GT")
        h2b = b.sb(ph, [128, 8, TG], BF16, "h2b")
        xn = b.sb(ph, [128, 8, TG], F32, "xn")
        ek = b.sb(ph, [128, 3, 128], F32, "ek")
        ET = b.sb(ph, [128, 3, 128], F32, "ET")
        ohA = [b.sb(ph, [128, 16, 128], BF16, "ohA") for _ in range(2)]
        ohB = [b.sb(ph, [128, 16, 128], BF16, "ohB") for _ in range(2)]
        ohT = [b.sb(ph, [128, 16, 128], BF16, "ohT") for _ in range(2)]
        NW = 4
        Uc = [b.sb(ph, [128, 1024], BF16, "Uc") for _ in range(NW)]
        Vc = [b.sb(ph, [128, 1024], BF16, "Vc") for _ in range(NW)]
        gel = [b.sb(ph, [128, TG], BF16, "gel") for _ in range(4)]
        AT = [b.sb(ph, [128, TG], BF16, "AT") for _ in range(4)]
        fo = b.sb(ph, [128, 1024], F32, "fo")
        pss = [b.ps(ph, [128, 512], F32, "pss") for _ in range(2)]
        pg = pss
        po = [b.ps(ph, [128, 512], F32, "po") for _ in range(2 * (TG // 128))]
        for grp in range(NT // TG):
            g0 = grp * TG
            v = 0 if g0 < NPR else 1
            b.load(h2b, h2b[:], h2bd, xview(h2bd, g0, TG))
            b.load(xn, xn[:], xT, xview(xT, g0, TG))
            kq = 0
            for tl in range(TG // 128):
                tok0 = g0 + tl * 128
                b.load(ek, ek[:], ekg, ekg.t[:, tok0:tok0 + 128, :].rearrange("c t s -> t c s"))
                for c3 in range(3):
                    b.tr(pg[0], pg[0][:, c3 * 128:(c3 + 1) * 128], ek, ek[:, c3, :], ident)
                b.cp("act", ET, ET[:], pg[0], pg[0][:, 0:384].rearrange("p (c t) -> p c t", c=3))
                for sub in range(8):
                    t0 = sub * 16
                    A_ = ohA[sub % 2]
                    B_ = ohB[sub % 2]
                    T_ = ohT[sub % 2]
                    io3 = iota[:].unsqueeze(1).to_broadcast([128, 16, 128])
                    b.tt("dve", B_, B_[:], iota, io3, ET, ET[:, 0, t0:t0 + 16].unsqueeze(2).to_broadcast([128, 16, 128]), ALU.is_equal)
                    b.tt("dve", T_, T_[:], iota, io3, ET, ET[:, 1, t0:t0 + 16].unsqueeze(2).to_broadcast([128, 16, 128]), ALU.is_equal)
                    b.tt("pool", A_, A_[:], T_, T_[:], ET, ET[:, 2, t0:t0 + 16].unsqueeze(2).to_broadcast([128, 16, 128]), ALU.mult)
                    for q4 in range(4):
                        p_ = pg[kq % 2]
                        kq += 1
                        pv = p_[:].rearrange("p (i t) -> p i t", t=4)
                        for tq in range(4):
                            tt_ = q4 * 4 + tq
                            b.mm(p_, pv[:, :, tq], A_, A_[:, tt_, :], B_, B_[:, tt_, :])
                        tg0 = tl * 128 + t0 + q4 * 4
                        b.cp("act", GT, GT[:, :, tg0:tg0 + 4], p_, pv)
            def s_part(i1):
                u_ = Uc[i1 % NW]
                v_ = Vc[i1 % NW]
                b.load(u_, u_[:], uTb, uTb.t[i1])
                b.load(v_, v_[:], vbd, vbd.t[i1])
                ps_ = pss[i1 % 2]
                for kc in range(8):
                    b.mm(ps_, ps_[:, 0:TG], u_, u_[:, kc * 128:(kc + 1) * 128], h2b, h2b[:, kc, :], start=(kc == 0), stop=(kc == 7))
                g_ = gel[i1 % 4]
                a_ = AT[i1 % 4]
                b.act(g_, g_[:], ps_, ps_[:, 0:TG], AF.Gelu_apprx_tanh)
                b.tt("pool" if i1 % 2 == 0 else "dve", a_, a_[:], g_, g_[:], GT, GT[:, i1, :], ALU.mult)

            def v_part(i1):
                v_ = Vc[i1 % NW]
                a_ = AT[i1 % 4]
                for tl in range(TG // 128):
                    for half in range(2):
                        p_ = po[tl * 2 + half]
                        b.mm(p_, p_[:], a_, a_[:, tl * 128:(tl + 1) * 128], v_, v_[:, half * 512:(half + 1) * 512], start=(i1 == 0), stop=(i1 == 127))

            s_part(0)
            for i1 in range(128):
                if i1 + 1 < 128:
                    s_part(i1 + 1)
                v_part(i1)
            for tl in range(TG // 128):
                ts_ = slice(tl * 128, (tl + 1) * 128)
                v = 0 if (g0 + tl * 128) < NPR else 1
                for half in range(2):
                    b.cp("act", fo, fo[:, half * 512:(half + 1) * 512], po[tl * 2 + half], po[tl * 2 + half][:])
                for half in range(2):
                    p_ = pg[half]
                    for c4 in range(4):
                        c = half * 4 + c4
                        b.tr(p_, p_[:, c4 * 128:(c4 + 1) * 128], fo, fo[:, c * 128:(c + 1) * 128], ident)
                    for c4 in range(4):
                        c = half * 4 + c4
                        b.stt(xn, xn[:, c, ts_], p_, p_[:, c4 * 128:(c4 + 1) * 128], modT[:, l, 40 + c, v:v + 1], xn, xn[:, c, ts_], ALU.mult, ALU.add, extra=[modT])
            b.load(xT, xview(xT, g0, TG), xn, xn[:])


_CACHE = {}


def _perm_qk():
    idx = np.zeros(256, np.int64)
    for s in range(2):
        for h in range(4):
            for j in range(32):
                idx[s * 128 + h * 32 + j] = h * 64 + s * 32 + j
    return idx


def kernel(x_prompt, x_sample, state_ret, state_gdn, c, c_ctx, ada_w, ada_b, norm_mix_g, norm_ffn_g,
           final_norm_g, ev_w_in, ev_w_out, ret_gamma_logit, ret_norm_g, sc_conv_w, od_w_in, od_w_out,
           gdn_conv_w, gdn_a_log, gdn_dt_bias, gdn_norm_g, cf_dw_w, cf_dw_b, cf_ln_g, cf_ln_b,
           peer_wq, peer_keys, peer_u, peer_v):
    f = lambda a: np.ascontiguousarray(np.asarray(a, dtype=np.float32))
    x_prompt, x_sample = f(x_prompt), f(x_sample)
    shared = {}
    shared["ada_w"] = f(ada_w)
    shared["ada_bT"] = f(np.asarray(ada_b).reshape(4, 48, 128).transpose(2, 0, 1))
    shared["gmixT"] = f(np.asarray(norm_mix_g).reshape(4, 8, 128).transpose(2, 0, 1))
    shared["gffnT"] = f(np.asarray(norm_ffn_g).reshape(4, 8, 128).transpose(2, 0, 1))
    shared["gfinT"] = f(np.asarray(final_norm_g).reshape(8, 128).T)
    pi = _perm_qk()
    ew = np.array(ev_w_in, dtype=np.float32)
    ew[:, :, 0:256] = np.asarray(ev_w_in)[:, :, 0:256][:, :, pi]
    ew[:, :, 256:512] = np.asarray(ev_w_in)[:, :, 256:512][:, :, pi]
    shared["ev_w_in"] = f(ew)
    shared["ev_w_out"] = f(ev_w_out)
    shared["lgam"] = f(np.asarray(ret_gamma_logit).reshape(2, 8))
    shared["retgT"] = f(np.asarray(ret_norm_g).transpose(2, 0, 1))
    shared["scwT"] = f(np.asarray(sc_conv_w).reshape(2, 3, 4, 128).transpose(3, 0, 2, 1))
    shared["od_w_in"] = f(od_w_in)
    shared["od_w_out"] = f(od_w_out)
    shared["gcwT"] = f(np.asarray(gdn_conv_w).reshape(2, 3, 12, 128).transpose(3, 0, 2, 1))
    shared["alog"] = f(np.asarray(gdn_a_log).reshape(2, 8))
    shared["dtb"] = f(np.asarray(gdn_dt_bias).reshape(2, 8))
    shared["gdngT"] = f(np.asarray(gdn_norm_g).T)
    shared["cfwT"] = f(np.asarray(cf_dw_w).reshape(2, 31, 4, 128).transpose(3, 0, 2, 1))
    shared["cfbT"] = f(np.asarray(cf_dw_b).reshape(2, 4, 128).transpose(2, 0, 1))
    shared["lngT"] = f(np.asarray(cf_ln_g).reshape(2, 4, 128).transpose(2, 0, 1))
    shared["lnbT"] = f(np.asarray(cf_ln_b).reshape(2, 4, 128).transpose(2, 0, 1))
    shared["peer_wq"] = f(peer_wq)
    shared["keysT"] = f(np.asarray(peer_keys).reshape(4, 16, 128, 128).transpose(0, 1, 3, 2))
    shared["peer_u"] = f(np.asarray(peer_u).reshape(4, 128, 128, 8, 128).transpose(0, 1, 4, 3, 2)).reshape(4, 128, 128, 1024)
    shared["peer_v"] = f(peer_v).reshape(4, 128, 128, 1024)
    for k, v_ in host_consts().items():
        shared["c_" + k] = f(v_)
    in_maps = []
    cc = np.asarray(c, dtype=np.float32)
    cx = np.asarray(c_ctx, dtype=np.float32)
    for r in range(NCORES):
        m = dict(shared)
        xs = np.concatenate([x_prompt[4 * r:4 * r + 4].reshape(NPR, D), x_sample[r]], axis=0)
        m["xT0"] = f(xs.T.reshape(8, 128, NT))
        cv = np.stack([cx, cc[r]], axis=0)
        m["cT"] = f(cv.reshape(2, 8, 128).transpose(2, 1, 0))
        m["sret"] = f(np.asarray(state_ret)[r])
        m["sgdn"] = f(np.asarray(state_gdn)[r])
        in_maps.append(m)
    if "nc" not in _CACHE:
        _CACHE["nc"] = build()
    res = run_bass_kernel_spmd(_CACHE["nc"], in_maps, core_ids=list(range(NCORES)))
    y_prompt = np.zeros((32, 256, D), np.float32)
    y_sample = np.zeros((8, 2048, D), np.float32)
    nret = np.zeros((32, 2, 2, 4, 64, 128), np.float32)
    ngdn = np.zeros((32, 2, 2, 4, 128, 128), np.float32)
    if STOP:
        _CACHE["pT"] = [np.asarray(res.results[r]["pT"]) for r in range(NCORES)]
        _CACHE["yTd"] = [np.asarray(res.results[r]["yTd"]) for r in range(NCORES)]
        _CACHE["dmod"] = [np.asarray(res.results[r]["dmod"]) for r in range(NCORES)]
        _CACHE["dbg"] = [np.asarray(res.results[r]["dbg"]).reshape(D, NT).T for r in range(NCORES)]
    for r in range(NCORES):
        o = res.results[r]
        yt = np.asarray(o["yT"]).reshape(D, NT).T
        y_prompt[4 * r:4 * r + 4] = yt[:NPR].reshape(4, 256, D)
        y_sample[r] = yt[NPR:]
        nret[4 * r:4 * r + 4] = np.asarray(o["nsr"])
        ngdn[4 * r:4 * r + 4] = np.asarray(o["nsg"])
    return (y_prompt, y_sample, nret, ngdn)
```
